# Optimizing a Trainium2 kernel written in Bass

```python
import jax, jax.numpy as jnp
from jax import lax
import numpy as np

D_MODEL = 1024
BATCH = 8
SEQ = 4096
DEPTH = 1
DEC_BATCH = 2
DEC_SEQ = 8192
PAST_LEN = 128

N_META = 16
N_FOURIER_GROUPS = 4
FOURIER_GROUP_DIM = D_MODEL // 8
FOURIER_DIM = N_FOURIER_GROUPS * FOURIER_GROUP_DIM
N_HEADS = 8
QK_NOPE_DIM = 128
QK_ROPE_DIM = 64
QK_HEAD_DIM = QK_NOPE_DIM + QK_ROPE_DIM
V_HEAD_DIM = 128
Q_LORA_RANK = D_MODEL // 2
KV_LORA_RANK = D_MODEL // 4
ATTN_DIM = N_HEADS * V_HEAD_DIM
D_FF = ((-(-8 * D_MODEL // 3) + 255) // 256) * 256
ROPE_THETA = 10000.0
NORM_EPS = 1e-6
Q_BLOCK = 128
ATTN_SCALE = QK_HEAD_DIM ** -0.5
IN_SPLITS = (
    FOURIER_DIM,
    FOURIER_DIM + Q_LORA_RANK,
    FOURIER_DIM + Q_LORA_RANK + KV_LORA_RANK,
    FOURIER_DIM + Q_LORA_RANK + KV_LORA_RANK + QK_ROPE_DIM,
    FOURIER_DIM + Q_LORA_RANK + KV_LORA_RANK + QK_ROPE_DIM + D_MODEL,
)
IN_PROJ_DIM = IN_SPLITS[-1] + D_MODEL

kernel_name = "fnet_mla_gated_hybrid_encoder"


def _rmsnorm(x, g):
    xf = x.astype(jnp.float32)
    y = xf * lax.rsqrt(jnp.mean(xf * xf, axis=-1, keepdims=True) + NORM_EPS)
    return (y * g.astype(jnp.float32)).astype(x.dtype)


def _rope_tables(length):
    inv = 1.0 / (ROPE_THETA ** (jnp.arange(0, QK_ROPE_DIM, 2, dtype=jnp.float32) / QK_ROPE_DIM))
    ang = jnp.arange(length, dtype=jnp.float32)[:, None] * inv[None, :]
    return jnp.cos(ang), jnp.sin(ang)


def _apply_rope(x, cos, sin):
    xf = x.astype(jnp.float32)
    x1, x2 = jnp.split(xf, 2, axis=-1)
    c = cos[None, :, None, :]
    s = sin[None, :, None, :]
    return jnp.concatenate([x1 * c - x2 * s, x2 * c + x1 * s], axis=-1).astype(x.dtype)


def _attend(q, k, v):
    s = jnp.einsum("bthd,blhd->bhtl", q, k).astype(jnp.float32) * ATTN_SCALE
    p = jax.nn.softmax(s, axis=-1)
    return jnp.einsum("bhtl,blhd->bthd", p.astype(v.dtype), v)


def _fourier_mixer(u):
    b, l, _ = u.shape
    ug = u.astype(jnp.float32).reshape(b, l, N_FOURIER_GROUPS, FOURIER_GROUP_DIM)
    yf = jnp.fft.fft2(ug, axes=(1, 3), norm="ortho").real
    return yf.reshape(b, l, FOURIER_DIM).astype(u.dtype)


def _mla_mixer(c_q, c_kv, k_r, cos, sin, q_norm_g, kv_norm_g, w_uq, w_ukv):
    b, l, _ = c_q.shape
    q = (_rmsnorm(c_q, q_norm_g) @ w_uq).reshape(b, l, N_HEADS, QK_HEAD_DIM)
    q_nope, q_rope = jnp.split(q, [QK_NOPE_DIM], axis=-1)
    kv = (_rmsnorm(c_kv, kv_norm_g) @ w_ukv).reshape(b, l, N_HEADS, QK_NOPE_DIM + V_HEAD_DIM)
    k_nope, v = jnp.split(kv, [QK_NOPE_DIM], axis=-1)
    k_rope = _apply_rope(k_r[:, :, None, :], cos, sin)
    q = jnp.concatenate([q_nope, _apply_rope(q_rope, cos, sin)], axis=-1)
    k = jnp.concatenate([k_nope, jnp.broadcast_to(k_rope, (b, l, N_HEADS, QK_ROPE_DIM))], axis=-1)
    meta_out = _attend(q[:, :N_META], k, v)
    s_real = l - N_META
    nb = s_real // Q_BLOCK
    qb = q[:, N_META:].reshape(b, nb, Q_BLOCK, N_HEADS, QK_HEAD_DIM).transpose(1, 0, 2, 3, 4)
    ob = lax.map(lambda qq: _attend(qq, k, v), qb)
    real_out = ob.transpose(1, 0, 2, 3, 4).reshape(b, s_real, ATTN_DIM)
    return jnp.concatenate([meta_out.reshape(b, N_META, ATTN_DIM), real_out], axis=1)


def _layer(x, cos, sin, norm1_g, w_in, q_norm_g, kv_norm_g, w_uq, w_ukv, w_fourier_out,
           w_attn_out, w_o, norm2_g, w_ffn_gate, w_ffn_up, w_ffn_down):
    h = _rmsnorm(x, norm1_g)
    z = h @ w_in
    u_f, c_q, c_kv, k_r, g_a, g_b = jnp.split(z, list(IN_SPLITS), axis=-1)
    y_a = _fourier_mixer(u_f) @ w_fourier_out
    y_b = _mla_mixer(c_q, c_kv, k_r, cos, sin, q_norm_g, kv_norm_g, w_uq, w_ukv) @ w_attn_out
    merged = jax.nn.sigmoid(g_a) * y_a + jax.nn.sigmoid(g_b) * y_b
    x = x + merged @ w_o
    h2 = _rmsnorm(x, norm2_g)
    return x + (jax.nn.silu(h2 @ w_ffn_gate) * (h2 @ w_ffn_up)) @ w_ffn_down


def _trunk(x, meta_tokens, norm1_g, w_in, q_norm_g, kv_norm_g, w_uq, w_ukv, w_fourier_out,
           w_attn_out, w_o, norm2_g, w_ffn_gate, w_ffn_up, w_ffn_down, final_norm_g):
    b = x.shape[0]
    meta = jnp.broadcast_to(meta_tokens.astype(x.dtype)[None], (b, N_META, D_MODEL))
    h = jnp.concatenate([meta, x], axis=1)
    cos, sin = _rope_tables(h.shape[1])
    for i in range(DEPTH):
        h = _layer(h, cos, sin, norm1_g[i], w_in[i], q_norm_g[i], kv_norm_g[i], w_uq[i], w_ukv[i],
                   w_fourier_out[i], w_attn_out[i], w_o[i], norm2_g[i], w_ffn_gate[i],
                   w_ffn_up[i], w_ffn_down[i])
    h = _rmsnorm(h, final_norm_g)
    return h[:, N_META:]


def setup_inputs(seed: int = 0) -> dict:
    key = jax.random.key(seed)
    ks = jax.random.split(key, 20)

    def w(k, shape, fan_in):
        return jax.random.normal(k, shape, jnp.float32) * (fan_in ** -0.5)

    def g(k, shape):
        return 1.0 + 0.02 * jax.random.normal(k, shape, jnp.float32)

    return {
        "x_prompt": jax.random.normal(ks[0], (BATCH, SEQ, D_MODEL), jnp.float32),
        "x_sample": jax.random.normal(ks[1], (DEC_BATCH, DEC_SEQ, D_MODEL), jnp.float32),
        "meta_tokens": jax.random.normal(ks[2], (N_META, D_MODEL), jnp.float32),
        "norm1_g": g(ks[3], (DEPTH, D_MODEL)),
        "w_in": w(ks[4], (DEPTH, D_MODEL, IN_PROJ_DIM), D_MODEL),
        "q_norm_g": g(ks[5], (DEPTH, Q_LORA_RANK)),
        "kv_norm_g": g(ks[6], (DEPTH, KV_LORA_RANK)),
        "w_uq": w(ks[7], (DEPTH, Q_LORA_RANK, N_HEADS * QK_HEAD_DIM), Q_LORA_RANK),
        "w_ukv": w(ks[8], (DEPTH, KV_LORA_RANK, N_HEADS * (QK_NOPE_DIM + V_HEAD_DIM)), KV_LORA_RANK),
        "w_fourier_out": w(ks[9], (DEPTH, FOURIER_DIM, D_MODEL), FOURIER_DIM),
        "w_attn_out": w(ks[10], (DEPTH, ATTN_DIM, D_MODEL), ATTN_DIM),
        "w_o": w(ks[11], (DEPTH, D_MODEL, D_MODEL), D_MODEL),
        "norm2_g": g(ks[12], (DEPTH, D_MODEL)),
        "w_ffn_gate": w(ks[13], (DEPTH, D_MODEL, D_FF), D_MODEL),
        "w_ffn_up": w(ks[14], (DEPTH, D_MODEL, D_FF), D_MODEL),
        "w_ffn_down": w(ks[15], (DEPTH, D_FF, D_MODEL), D_FF),
        "final_norm_g": g(ks[16], (D_MODEL,)),
    }


def reference(x_prompt, x_sample, meta_tokens, norm1_g, w_in, q_norm_g, kv_norm_g, w_uq, w_ukv,
              w_fourier_out, w_attn_out, w_o, norm2_g, w_ffn_gate, w_ffn_up, w_ffn_down,
              final_norm_g):
    y_prompt = _trunk(x_prompt, meta_tokens, norm1_g, w_in, q_norm_g, kv_norm_g, w_uq, w_ukv,
                      w_fourier_out, w_attn_out, w_o, norm2_g, w_ffn_gate, w_ffn_up, w_ffn_down,
                      final_norm_g)
    y_sample = _trunk(x_sample, meta_tokens, norm1_g, w_in, q_norm_g, kv_norm_g, w_uq, w_ukv,
                      w_fourier_out, w_attn_out, w_o, norm2_g, w_ffn_gate, w_ffn_up, w_ffn_down,
                      final_norm_g)
    return (y_prompt, y_sample)
```

```python
import contextlib
import numpy as np
import ml_dtypes
import concourse.bass as bass
import concourse.mybir as mybir
from concourse.bass_utils import run_bass_kernel_spmd

F32 = mybir.dt.float32
BF16 = mybir.dt.bfloat16
AF = mybir.ActivationFunctionType
ALU = mybir.AluOpType

D = 1024
NM = 16
DFF = 2816
EPS = 1e-6
SCALE = 192 ** -0.5
LA, LB = 4112, 8208
DEBUG = False
STOP = None
KLIM = None
SMALL_TABS = False
POOL_EW = "dve"
POOL_AT = "dve"


class Buf:
    __slots__ = ("name", "writer", "readers", "excl")

    def __init__(self, name="", excl=False):
        self.name = name
        self.writer = None
        self.readers = []
        self.excl = excl


class Op:
    __slots__ = ("eng", "fn", "deps", "need_inc", "val", "is_dma", "dsem", "dval")

    def __init__(self, eng, fn, is_dma=False):
        self.eng = eng
        self.fn = fn
        self.deps = []
        self.need_inc = False
        self.val = None
        self.is_dma = is_dma
        self.dsem = None
        self.dval = None


ENGS = ("pe", "act", "dve", "pool", "sp")


class Prog:
    def __init__(self, nc, n_dma_sems=48):
        self.nc = nc
        self.ops = {e: [] for e in ENGS}
        self.n_dma_sems = n_dma_sems
        self.dma_last = [None] * n_dma_sems
        self.dma_uses = [0] * n_dma_sems
        self.dma_n = {"sp": 0, "pool": 0, "act": 0}
        self.dma_rng = {"sp": (0, 24), "act": (24, 16), "pool": (40, n_dma_sems - 40)}
        self.all_bufs = []
        self.final_dmas = []

    def buf(self, name="", excl=False):
        b = Buf(name, excl)
        self.all_bufs.append(b)
        return b

    def _add_dep(self, op, prod):
        if prod is None or prod is op:
            return
        if (not prod.is_dma) and prod.eng == "pe" and op.eng == "pe" and not op.is_dma:
            return
        if not prod.is_dma:
            prod.need_inc = True
        op.deps.append(prod)

    @staticmethod
    def _flat(x):
        out = []
        for b in x:
            if isinstance(b, (list, tuple)):
                out.extend(Prog._flat(b))
            else:
                out.append(b)
        return out

    def op(self, eng, fn, reads=(), writes=(), dma=False):
        o = Op(eng, fn, is_dma=dma)
        reads = self._flat(reads)
        writes = self._flat(writes)
        xr = [b for b in reads if b.excl and b not in writes]
        if xr:
            reads = [b for b in reads if not b.excl]
            writes = list(writes) + xr
        for b in reads:
            self._add_dep(o, b.writer)
        for b in writes:
            self._add_dep(o, b.writer)
            for r in b.readers:
                self._add_dep(o, r)
        if dma:
            base, cnt = self.dma_rng[eng]
            k = base + self.dma_n[eng] % cnt
            self.dma_n[eng] += 1
            self._add_dep(o, self.dma_last[k])
            self.dma_last[k] = o
            self.dma_uses[k] += 1
            o.dsem = k
            o.dval = 16 * self.dma_uses[k]
        for b in reads:
            b.readers.append(o)
        for b in writes:
            b.writer = o
            b.readers = []
        self.ops[eng].append(o)
        return o

    def dma(self, eng, out_ap, in_ap, reads=(), writes=(), final=False):
        o = self.op(eng, lambda e: e.dma_start(out=out_ap, in_=in_ap), reads, writes, dma=True)
        if final:
            self.final_dmas.append(o)
        return o

    def barrier(self):
        lasts = []
        for e in ENGS:
            for o in reversed(self.ops[e]):
                if not o.is_dma and o.fn is not None:
                    lasts.append(o)
                    break
        for k in range(self.n_dma_sems):
            if self.dma_last[k] is not None:
                lasts.append(self.dma_last[k])
        for e in ENGS:
            o = Op(e, None)
            for p in lasts:
                if (not p.is_dma) and p.eng == e and e == "pe":
                    continue
                if not p.is_dma:
                    p.need_inc = True
                o.deps.append(p)
            self.ops[e].append(o)
        for b in self.all_bufs:
            b.writer = None
            b.readers = []

    def emit(self):
        nc = self.nc
        fin = Op("sp", None)
        for o in self.final_dmas:
            fin.deps.append(o)
        for k in range(self.n_dma_sems):
            if self.dma_last[k] is not None:
                fin.deps.append(self.dma_last[k])
        self.ops["sp"].append(fin)
        for e in ENGS:
            c = 0
            for o in self.ops[e]:
                if o.is_dma:
                    continue
                if o.need_inc:
                    c += 1
                o.val = c
        with contextlib.ExitStack() as st:
            esem = {e: st.enter_context(nc.semaphore("s_" + e)) for e in ENGS}
            dsem = [st.enter_context(nc.semaphore("d%d" % k)) for k in range(self.n_dma_sems)]
            block = st.enter_context(nc.Block())
            ops = self.ops

            def run(e, eng):
                waited = {}
                for o in ops[e]:
                    for p in o.deps:
                        if p.is_dma:
                            key, sem, val = ("d", p.dsem), dsem[p.dsem], p.dval
                        else:
                            key, sem, val = ("e", p.eng), esem[p.eng], p.val
                        if waited.get(key, 0) >= val:
                            continue
                        waited[key] = val
                        eng.wait_ge(sem, val)
                    if o.fn is None:
                        continue
                    ins = o.fn(eng)
                    if o.is_dma:
                        ins.then_inc(dsem[o.dsem], 16)
                    elif o.need_inc:
                        ins.then_inc(esem[e], 1)

            @block.tensor
            def _(eng):
                run("pe", eng)

            @block.scalar
            def _(eng):
                run("act", eng)

            @block.vector
            def _(eng):
                run("dve", eng)

            @block.gpsimd
            def _(eng):
                run("pool", eng)

            @block.sync
            def _(eng):
                run("sp", eng)


class Rot:
    def __init__(self, P, aps, excl=False):
        self.items = [(ap, P.buf("", excl)) for ap in aps]
        self.i = 0

    def next(self):
        it = self.items[self.i % len(self.items)]
        self.i += 1
        return it


def job_geom(L):
    half = L // 2
    Fp = np.arange(1, half)
    Mp = L - Fp
    n_full = len(Fp) // 256
    rem = len(Fp) - 256 * n_full
    assert rem == 7
    korder = []
    sblocks = []
    for s in range(n_full):
        korder += [Fp[256 * s:256 * s + 256], Mp[256 * s:256 * s + 256]]
        sblocks += [Fp[256 * s:256 * s + 128], Fp[256 * s + 128:256 * s + 256]]
    korder += [Fp[-rem:], np.array([0, half]), Mp[-rem:]]
    sblocks += [np.concatenate([Fp[-rem:], np.array([0, half])])]
    korder = np.concatenate(korder)
    assert len(korder) == L and len(set(korder.tolist())) == L
    return dict(L=L, n_full=n_full, korder=korder, sblocks=sblocks, nb=2 * n_full + 1)


GEOM = {"A": job_geom(LA), "B": job_geom(LB)}
NGROUPS = {"A": 1, "B": 1}
QN = {"A": 4096, "B": 2048}
GBASE = {"A": 0, "B": 2}


def build_program():
    nc = bass.Bass("TRN2", target_bir_lowering=False)
    P = Prog(nc)

    def din(name, shape, dt=F32):
        return nc.dram_tensor(name, shape, dt, kind="ExternalInput").ap()

    def dscr(name, shape, dt=BF16):
        kind = "ExternalOutput" if DEBUG else "Internal"
        return nc.dram_tensor(name, shape, dt, kind=kind).ap()

    xk = {"A": din("xkA", [LA, D]), "B": din("xkB", [LB, D])}
    xq = {"A": din("xqA", [4096, D]), "B": din("xqB", [2048, D])}
    ropek = {"A": din("ropekA", [2, 64, LA]), "B": din("ropekB", [2, 64, LB])}
    ropeq = {"A": din("ropeqA", [2, 64, 4096]), "B": din("ropeqB", [2, 64, 2048])}
    if SMALL_TABS:
        tab = {"A": din("tabA", [1, 1, 128, 1024], BF16), "B": din("tabB", [1, 1, 128, 1024], BF16)}
    else:
        tab = {"A": din("tabA", [4, GEOM["A"]["nb"], 128, 1024], BF16),
               "B": din("tabB", [2, GEOM["B"]["nb"], 128, 1024], BF16)}
    tabs_small = {"A": din("tabAs", [GEOM["A"]["nb"], 128, 16], BF16),
                  "B": din("tabBs", [GEOM["B"]["nb"], 128, 4], BF16)}
    c128_d = din("c128", [128, 384], BF16)
    g1_d = din("g1", [128, 8])
    gq_d = din("gq", [128, 4])
    gkv_d = din("gkv", [128, 2])
    g2_d = din("g2", [128, 8])
    gfin_d = din("gfin", [128, D])
    w_in_d = din("w_in", [D, 3392])
    w_uq_d = din("w_uq", [512, 1536])
    w_ukv_d = din("w_ukv", [256, 2048])
    w_fo_d = din("w_fo", [512, D])
    w_ao_d = din("w_ao", [D, D])
    w_o_d = din("w_o", [D, D])
    w_g_d = din("w_g", [D, DFF])
    w_u_d = din("w_u", [D, DFF])
    w_d_d = din("w_d", [DFF, D])
    yA = nc.dram_tensor("yA", [4096, D], F32, kind="ExternalOutput").ap()
    yB = nc.dram_tensor("yB", [2048, D], F32, kind="ExternalOutput").ap()

    s_wk = dscr("s_wk", [D, 896])
    s_wq = dscr("s_wq", [D, 512])
    s_wgt = dscr("s_wgt", [D, 2048])
    s_wuq = dscr("s_wuq", [512, 2048])
    s_wukv = dscr("s_wukv", [256, 2048])
    s_wfo = dscr("s_wfo", [512, D])
    s_wao = dscr("s_wao", [D, D])
    s_wo = dscr("s_wo", [D, D])
    s_wg = dscr("s_wg", [D, DFF])
    s_wu = dscr("s_wu", [D, DFF])
    s_wd = dscr("s_wd", [DFF, D])
    FTs = dscr("FTs", [3, 128, 4, 2048])
    ATs = dscr("ATs", [3, 128, 8, 2048])
    x1s = dscr("x1s", [6144, D], F32)
    scr_bufs = {}

    def sb(name):
        if name not in scr_bufs:
            scr_bufs[name] = P.buf(name)
        return scr_bufs[name]

    def MM(out, lhsT, rhs, start, stop, reads, writes):
        P.op("pe", lambda e: e.matmul(out, lhsT=lhsT, rhs=rhs, start=start, stop=stop), reads, writes)

    def TR(out, in_, ident, reads, writes):
        P.op("pe", lambda e: e.transpose(out=out, in_=in_, identity=ident), reads, writes)

    def ACT(out, in_, func, reads, writes, scale=1.0, bias=0.0, accum_out=None):
        if accum_out is None:
            P.op("act", lambda e: e.activation(out=out, in_=in_, func=func, scale=scale, bias=bias), reads, writes)
        else:
            P.op("act", lambda e: e.activation(out=out, in_=in_, func=func, scale=scale, bias=bias,
                                               accum_out=accum_out), reads, writes)

    def ACTS(out, in_, func, scale_ap, reads, writes):
        P.op("act", lambda e: e.activation(out=out, in_=in_, func=func, scale=scale_ap), reads, writes)

    def COPY(eng, out, in_, reads, writes):
        if eng == "act":
            P.op("act", lambda e: e.copy(out=out, in_=in_), reads, writes)
        else:
            P.op(eng, lambda e: e.tensor_copy(out=out, in_=in_), reads, writes)

    def TT(eng, out, in0, in1, op, reads, writes):
        P.op(eng, lambda e: e.tensor_tensor(out=out, in0=in0, in1=in1, op=op), reads, writes)

    def RECIP(out, in_, reads, writes):
        P.op("dve", lambda e: e.reciprocal(out=out, in_=in_), reads, writes)

    def TSMUL(out, in0, sc, reads, writes):
        P.op("dve", lambda e: e.tensor_scalar_mul(out=out, in0=in0, scalar1=sc), reads, writes)

    evac_ctr = [0]

    def EVAC(out, in_, reads, writes, pref=None):
        evac_ctr[0] += 1
        COPY(pref or ("act" if evac_ctr[0] % 2 else "dve"), out, in_, reads, writes)

    def sbuf(st, name, shape, dt):
        return st.enter_context(nc.sbuf_tensor(name, shape, dt)).ap()

    def psum(st, name, shape, dt=F32):
        return st.enter_context(nc.psum_tensor(name, shape, dt)).ap()

    uid = [0]

    def nm(s):
        uid[0] += 1
        return "%s_%d" % (s, uid[0])

    def load_w(st, scr, R, C, name, nsplit=1):
        t = sbuf(st, nm(name), [128, R // 128, C], BF16)
        src = scr.rearrange("(c p) n -> p c n", p=128)
        nch = R // 128
        bl = []
        for c0 in range(nch):
            b = P.buf(name)
            P.dma("sp", t[:, c0, :], src[:, c0, :], writes=[b])
            bl.append(b)
        return t, bl

    with contextlib.ExitStack() as gst:
        ident = sbuf(gst, "ident", [128, 128], BF16)
        identf = sbuf(gst, "identf", [128, 128], F32)
        ones_bf = sbuf(gst, "ones_bf", [128, 128], BF16)
        ones32 = sbuf(gst, "ones32", [128, 128], F32)
        c128 = sbuf(gst, "c128s", [128, 384], BF16)
        cb = P.buf("const")
        P.op("pool", lambda e: e.memset(identf, 0.0), writes=[cb])
        P.op("pool", lambda e: e.affine_select(out=identf, in_=identf, pattern=[[-1, 128]],
                                               compare_op=ALU.not_equal, fill=1.0, base=0,
                                               channel_multiplier=1), reads=[cb], writes=[cb])
        P.op("dve", lambda e: e.tensor_copy(out=ident, in_=identf), reads=[cb], writes=[cb])
        P.op("pool", lambda e: e.memset(ones32, 1.0), writes=[cb])
        P.op("dve", lambda e: e.tensor_copy(out=ones_bf, in_=ones32), reads=[cb], writes=[cb])
        P.dma("sp", c128, c128_d, writes=[cb])

        class NT:
            def __init__(self, st, nx, ncol=512, npt=2):
                self.xs = Rot(P, [sbuf(st, nm("x"), [128, D], F32) for _ in range(nx)])
                self.stt = Rot(P, [sbuf(st, nm("st"), [128, 2], F32) for _ in range(12)])
                self.xn = Rot(P, [sbuf(st, nm("xn"), [128, D], BF16) for _ in range(4)])
                self.junk = sbuf(st, nm("junk"), [128, D], BF16)
                self.junkb = P.buf()
                self.pT = Rot(P, [psum(st, nm("pT"), [128, 8, 128], BF16) for _ in range(npt)], True)
                self.hTa = [sbuf(st, nm("hT"), [128, 8, ncol], BF16) for _ in range(2)]
                self.hTbs = [[P.buf() for _ in range(4)] for _ in range(2)]
                self.hi = 0

            def prep(self, tiles):
                stg = []
                for ti, (src, r) in enumerate(tiles):
                    x, xb = self.xs.next()
                    P.dma("sp", x[0:r, :], src, writes=[xb])
                    s, sbf = self.stt.next()
                    ACT(self.junk[0:r, :], x[0:r, :], AF.Square, [xb], [self.junkb, sbf], accum_out=s[0:r, 0:1])
                    ACT(s[0:r, 1:2], s[0:r, 0:1], AF.Sqrt, [sbf], [sbf], scale=1.0 / D, bias=EPS)
                    stg.append([x, xb, r, s, sbf, None, None])
                for e_ in stg:
                    x, xb, r, s, sbf = e_[0:5]
                    xn, xnb = self.xn.next()
                    RECIP(s[0:r, 1:2], s[0:r, 1:2], [sbf], [sbf])
                    TSMUL(xn[0:r, :], x[0:r, :], s[0:r, 1:2], [xb, sbf], [xnb])
                    e_[5], e_[6] = xn, xnb
                return stg

            def finish(self, stg):
                hT, hTb = self.hTa[self.hi % 2], self.hTbs[self.hi % 2]
                self.hi += 1
                col = 0
                xl = []
                for ti, (x, xb, r, s, sbf, xn, xnb) in enumerate(stg):
                    pT, pTb = self.pT.next()
                    for c in range(8):
                        TR(pT[:, c, 0:r], xn[0:r, c * 128:(c + 1) * 128], ident[0:r, 0:r], [xnb, cb], [pTb])
                    EVAC(hT[:, :, col:col + r], pT[:, :, 0:r], [pTb], [hTb[ti]])
                    xl.append((x, xb, r))
                    col += r
                return hT, hTb, col, xl

            def run(self, tiles):
                return self.finish(self.prep(tiles))

        gains = {}
        for name, gd, n in (("g1", g1_d, 8), ("gq", gq_d, 4), ("gkv", gkv_d, 2), ("g2", g2_d, 8)):
            t = sbuf(gst, name + "_sb", [128, n], F32)
            b = P.buf(name)
            P.dma("sp", t, gd, writes=[b])
            gains[name] = (t, b)
        wctr = [0]

        def conv_chunk(stage, obuf, engs, stq, dst, src, rc, pieces, gain, d_lo, d_hi):
            lo = min(p[1] for p in pieces)
            hi = max(p[1] + p[2] for p in pieces)
            stg, stb = stage.next()
            P.dma("sp", stg[:, 0:hi - lo], src[rc * 128:(rc + 1) * 128, lo:hi], writes=[stb])
            ob, obb = obuf.next()
            for (d0, s0, w) in pieces:
                wctr[0] += 1
                eng = engs[wctr[0] % len(engs)]
                o_ap = ob[:, d0 - d_lo:d0 - d_lo + w]
                i_ap = stg[:, s0 - lo:s0 - lo + w]
                if gain is None:
                    COPY(eng, o_ap, i_ap, [stb], [obb])
                else:
                    gt, gb = gains[gain]
                    if eng == "act":
                        ACTS(o_ap, i_ap, AF.Copy, gt[:, rc:rc + 1], [stb, gb], [obb])
                    else:
                        TSMUL(o_ap, i_ap, gt[:, rc:rc + 1], [stb, gb], [obb])
            P.dma(stq, dst[rc * 128:(rc + 1) * 128, d_lo:d_hi], ob[:, 0:d_hi - d_lo], reads=[obb])

        def conv_tasks(dst, Cd, src, R, pieces, gain, maxw=None):
            tasks = []
            if maxw is None:
                for rc in range(R // 128):
                    tasks.append((dst, src, rc, pieces, gain, 0, Cd))
            else:
                assert len(pieces) == 1
                d0, s0, w = pieces[0]
                nsp = (w + maxw - 1) // maxw
                step = (w + nsp - 1) // nsp
                for rc in range(R // 128):
                    for o in range(0, w, step):
                        ww = min(step, w - o)
                        tasks.append((dst, src, rc, [(d0 + o, s0 + o, ww)], gain, d0 + o, d0 + o + ww))
            return tasks

        pcs = []
        for h in range(8):
            pcs += [(256 * h, 192 * h, 192), (256 * h + 192, 192 * h + 160, 32), (256 * h + 224, 192 * h + 128, 32)]
        fg_tasks = (conv_tasks(s_wk, 896, w_in_d, D,
                               [(0, 0, 512), (512, 1024, 256), (768, 1280, 64), (832, 1312, 32), (864, 1280, 32)], "g1")
                    + conv_tasks(s_wq, 512, w_in_d, D, [(0, 512, 512)], "g1")
                    + conv_tasks(s_wuq, 2048, w_uq_d, 512, pcs, "gq")
                    + conv_tasks(s_wukv, 2048, w_ukv_d, 256, [(0, 0, 2048)], "gkv"))
        BGW = 704
        bg_tasks = (conv_tasks(s_wgt, 2048, w_in_d, D, [(0, 1344, 2048)], "g1", BGW)
                    + conv_tasks(s_wfo, D, w_fo_d, 512, [(0, 0, D)], None, BGW)
                    + conv_tasks(s_wao, D, w_ao_d, D, [(0, 0, D)], None, BGW)
                    + conv_tasks(s_wo, D, w_o_d, D, [(0, 0, D)], None, BGW)
                    + conv_tasks(s_wg, DFF, w_g_d, D, [(0, 0, DFF)], "g2", BGW)
                    + conv_tasks(s_wu, DFF, w_u_d, D, [(0, 0, DFF)], "g2", BGW)
                    + conv_tasks(s_wd, D, w_d_d, DFF, [(0, 0, D)], None, BGW))
        with contextlib.ExitStack() as st:
            stage = Rot(P, [sbuf(st, nm("wst"), [128, 2048], F32) for _ in range(3)])
            obuf = Rot(P, [sbuf(st, nm("wob"), [128, 2048], BF16) for _ in range(3)])
            for tsk in fg_tasks:
                conv_chunk(stage, obuf, ("act", "dve"), "act", *tsk)
        P.barrier()
        if STOP == "W":
            P.emit()
            return nc

        for job in ("A", "B"):
            G = GEOM[job]
            L, n_full, nb = G["L"], G["n_full"], G["nb"]
            nst = n_full + 1
            with contextlib.ExitStack() as jst:
                ckvT = sbuf(jst, nm("ckvT"), [128, 2, L], BF16)
                kropeT = sbuf(jst, nm("kropeT"), [128, L], BF16)
                krzb = P.buf()
                P.op("dve", lambda e, t=kropeT: e.memset(t, 0.0), writes=[krzb])
                ckvb = [P.buf() for _ in range(nst)]
                krb = [P.buf() for _ in range(nst)]
                with contextlib.ExitStack() as abst:
                    AB = sbuf(abst, nm("AB"), [128, nb, 1024], BF16)
                    ABb = [[P.buf(), P.buf()] for _ in range(nb)]
                    if DEBUG:
                        P.op("pool", lambda e, AB=AB, nb=nb: e.memset(AB[:, nb - 1, :], 0.0), writes=[ABb[nb - 1]])
                    with contextlib.ExitStack() as st:
                        nt = NT(st, 4, 512, 3)
                        wk, wkb = load_w(st, s_wk, D, 896, "wk")
                        mm = Rot(P, [psum(st, nm("mm"), [128, 512]) for _ in range(3)], True)
                        abp = Rot(P, [psum(st, nm("abp"), [128, 512]) for _ in range(2)], True)
                        uTa = [sbuf(st, nm("uT"), [128, 4, 512], BF16) for _ in range(2)]
                        uTbs = [[P.buf() for _ in range(4)] for _ in range(2)]
                        uti = [0]
                        uTt2 = sbuf(st, nm("uTt"), [128, 128], BF16)
                        uTt = uTt2.rearrange("p (g n) -> p g n", g=4)
                        uTtb = [P.buf() for _ in range(4)]
                        P.op("dve", lambda e, t=uTt2: e.memset(t, 0.0), writes=[uTtb])
                        sqr = Rot(P, [sbuf(st, nm("sq"), [128, 2, 512], BF16) for _ in range(2)])
                        crr = Rot(P, [sbuf(st, nm("cr"), [128, 2, 512], F32) for _ in range(1)])
                        Rr = Rot(P, [sbuf(st, nm("R"), [128, 512], F32) for _ in range(1)])
                        rcr = Rot(P, [sbuf(st, nm("rc"), [64, 2, 512], F32) for _ in range(2)])
                        t12 = Rot(P, [sbuf(st, nm("t12"), [64, 2, 512], F32) for _ in range(1)])
                        def ktiles(s_):
                            r0_ = 512 * s_
                            if s_ == n_full:
                                return [(xk[job][r0_:r0_ + 16, :], 16)]
                            return [(xk[job][r0_ + 128 * t:r0_ + 128 * (t + 1), :], 128) for t in range(4)]

                        slist = [s_ for s_ in range(nst) if KLIM is None or s_ in KLIM]
                        pre = nt.prep(ktiles(slist[0]))
                        for si, s in enumerate(slist):
                            tail = s == n_full
                            r0 = 512 * s
                            hT, hTb, n, _ = nt.finish(pre)
                            if si + 1 < len(slist):
                                pre = nt.prep(ktiles(slist[si + 1]))
                            if tail:
                                u, ub = uTt, uTtb
                            else:
                                u, ub = uTa[uti[0] % 2], uTbs[uti[0] % 2]
                                uti[0] += 1
                            for g in range(4):
                                ps, psb = mm.next()
                                for c in range(8):
                                    MM(ps[:, 0:n], wk[:, c, g * 128:(g + 1) * 128], hT[:, c, 0:n], c == 0, c == 7,
                                       [hTb, wkb[c]], [psb])
                                EVAC(u[:, g, 0:n], ps[:, 0:n], [psb], [ub[g]])
                            sq, sqb = sqr.next()
                            cr, crb = crr.next()
                            for c2 in range(2):
                                ps, psb = mm.next()
                                for c in range(8):
                                    MM(ps[:, 0:n], wk[:, c, 512 + c2 * 128:512 + (c2 + 1) * 128], hT[:, c, 0:n],
                                       c == 0, c == 7, [hTb, wkb[c]], [psb])
                                ACT(sq[:, c2, 0:n], ps[:, 0:n], AF.Square, [psb], [sqb])
                                COPY("dve", cr[:, c2, 0:n], ps[:, 0:n], [psb], [crb])
                            rc, rcb = rcr.next()
                            P.dma("sp", rc[:, :, 0:n], ropek[job][:, :, r0:r0 + n].rearrange("a p n -> p a n"),
                                  writes=[rcb])
                            tt, ttb = t12.next()
                            for j in range(2):
                                ps, psb = mm.next()
                                for c in range(8):
                                    MM(ps[0:64, 0:n], wk[:, c, 768 + 64 * j:832 + 64 * j], hT[:, c, 0:n],
                                       c == 0, c == 7, [hTb, wkb[c]], [psb])
                                TT("dve", tt[:, j, 0:n], ps[0:64, 0:n], rc[:, j, 0:n], ALU.mult, [psb, rcb], [ttb])
                            TT(POOL_EW, kropeT[0:64, r0:r0 + n], tt[:, 0, 0:n], tt[:, 1, 0:n], ALU.add, [ttb, krzb], [krb[s]])
                            if tail:
                                blks = [(2 * n_full, 0, 9, 9)]
                            else:
                                blks = [(2 * s, 0, 256, 128), (2 * s + 1, 128, 384, 128)]
                            for (blk, f0, m0, m) in blks:
                                for hf, (rhs_f, rhs_m) in enumerate(((c128[:, 0:128], c128[:, 0:128]),
                                                                     (c128[:, 128:256], c128[:, 256:384]))):
                                    ab, abb = abp.next()
                                    for g in range(4):
                                        o_ap = ab[0:m, g * 128:(g + 1) * 128]
                                        MM(o_ap, u[:, g, f0:f0 + m], rhs_f, True, False, [ub[g], cb], [abb])
                                        MM(o_ap, u[:, g, m0:m0 + m], rhs_m, False, True, [ub[g], cb], [abb])
                                    EVAC(AB[0:m, blk, hf * 512:(hf + 1) * 512], ab[0:m, :], [abb], [ABb[blk][hf]])
                            ps, psb = mm.next()
                            for c2 in range(2):
                                MM(ps[:, 0:n], ones_bf, sq[:, c2, 0:n], c2 == 0, c2 == 1, [sqb, cb], [psb])
                            R, Rb = Rr.next()
                            ACT(R[:, 0:n], ps[:, 0:n], AF.Sqrt, [psb], [Rb], scale=1.0 / 256, bias=EPS)
                            RECIP(R[:, 0:n], R[:, 0:n], [Rb], [Rb])
                            for c2 in range(2):
                                TT("dve", ckvT[:, c2, r0:r0 + n], cr[:, c2, 0:n], R[:, 0:n], ALU.mult,
                                   [crb, Rb], [ckvb[s]])
                    P.barrier()
                    if STOP == "K" + job:
                        dA = nc.dram_tensor("dbg_AB", [128, nb, 1024], BF16, kind="ExternalOutput").ap()
                        dC = nc.dram_tensor("dbg_ckv", [128, 2, L], BF16, kind="ExternalOutput").ap()
                        dK = nc.dram_tensor("dbg_kr", [128, L], BF16, kind="ExternalOutput").ap()
                        P.dma("sp", dA, AB, final=True)
                        P.dma("sp", dC, ckvT, final=True)
                        P.dma("sp", dK, kropeT, final=True)
                        P.emit()
                        return nc
                    nfc = 4 if job == "A" else 2
                    Ns = 8 if job == "A" else 2
                    with contextlib.ExitStack() as st:
                        if job == "A":
                            FT0 = sbuf(st, nm("FT0"), [128, 4, 2048], BF16)
                            FT1 = sbuf(st, nm("FT1"), [128, 4, 2048], BF16)
                            FT0b, FT1b = P.buf(), P.buf()
                            fdst = lambda g, j: (FT0[:, g, j * 512:(j + 1) * 512], FT0b)
                            mdst = lambda g, j: (FT1[:, g, j * 512:(j + 1) * 512], FT1b)
                            sdst = lambda g: (FT1[:, g, 2048 - Ns:2048], FT1b)
                        else:
                            FT0 = sbuf(st, nm("FT0"), [128, 4, 2048], BF16)
                            FT0b = P.buf()
                            FT1b = P.buf()
                            fdst = lambda g, j: (FT0[:, g, j * 512:(j + 1) * 512], FT0b)
                            mdst = lambda g, j: (FT0[:, g, 1024 + j * 512:1024 + (j + 1) * 512], FT1b)
                            sdst = lambda g: (FT0[:, g, 2048 - Ns:2048], FT1b)
                        tabs = Rot(P, [sbuf(st, nm("tab"), [128, 4, 1024], BF16) for _ in range(3)])
                        tsm = sbuf(st, nm("tsm"), [128, nb, 2 * Ns], BF16)
                        tsmb = P.buf()
                        P.dma("sp", tsm, tabs_small[job].rearrange("b p n -> p b n"), writes=[tsmb])
                        p1R = Rot(P, [sbuf(st, nm("p1sb"), [128, 512], F32) for _ in range(2)])
                        fps = psum(st, nm("fps"), [128, 8, 512])
                        fpb = [P.buf("", True) for _ in range(8)]
                        for j in range(nfc):
                            for s4 in range(0, 2 * n_full + 1, 4):
                                tail = s4 == 2 * n_full
                                T, Tb = tabs.next()
                                if tail:
                                    P.dma("sp", T[:, 0, :], tab[job][j, 2 * n_full, :, :], writes=[Tb])
                                    blks = [(2 * n_full, 0, 9)]
                                else:
                                    P.dma("sp", T, tab[job][j, s4:s4 + 4, :, :].rearrange("b p n -> p b n"),
                                          writes=[Tb])
                                    blks = [(s4 + bi, bi, 128) for bi in range(4)]
                                for (blk, bi, m) in blks:
                                    for g in range(4):
                                        MM(fps[:, g, :], AB[0:m, blk, g * 128:(g + 1) * 128], T[0:m, bi, 0:512],
                                           blk == 0, blk == nb - 1, [ABb[blk][0], Tb], [fpb[g]])
                                        MM(fps[:, 4 + g, :], AB[0:m, blk, 512 + g * 128:512 + (g + 1) * 128],
                                           T[0:m, bi, 512:1024], blk == 0, blk == nb - 1, [ABb[blk][1], Tb],
                                           [fpb[4 + g]])
                            for g in range(4):
                                p1, p1b = p1R.next()
                                COPY("act", p1, fps[:, g, :], [fpb[g]], [p1b])
                                d_ap, d_b = fdst(g, j)
                                TT("dve", d_ap, fps[:, 4 + g, :], p1, ALU.add, [fpb[4 + g], p1b], [d_b])
                                d_ap, d_b = mdst(g, j)
                                TT("dve", d_ap, p1, fps[:, 4 + g, :], ALU.subtract, [fpb[4 + g], p1b], [d_b])
                        for g in range(4):
                            reg = fps[:, 0, g * Ns:(g + 1) * Ns]
                            for blk in range(nb):
                                m = 9 if blk == nb - 1 else 128
                                MM(reg, AB[0:m, blk, g * 128:(g + 1) * 128], tsm[0:m, blk, 0:Ns],
                                   blk == 0, False, [ABb[blk][0], tsmb], [fpb[0]])
                                MM(reg, AB[0:m, blk, 512 + g * 128:512 + (g + 1) * 128], tsm[0:m, blk, Ns:2 * Ns],
                                   False, blk == nb - 1, [ABb[blk][1], tsmb], [fpb[0]])
                        for g in range(4):
                            d_ap, d_b = sdst(g)
                            COPY("act", d_ap, fps[:, 0, g * Ns:(g + 1) * Ns], [fpb[0]], [d_b])
                        if job == "A":
                            P.dma("sp", FTs[0], FT0, reads=[FT0b])
                            P.dma("sp", FTs[1], FT1, reads=[FT1b])
                        else:
                            P.dma("sp", FTs[2], FT0, reads=[FT0b, FT1b])
                    P.barrier()
                    if STOP == "D" + job:
                        P.emit()
                        return nc
                for gi in range(NGROUPS[job]):
                    gg = GBASE[job] + gi
                    QQ = QN[job]
                    nqc = QQ // 512
                    with contextlib.ExitStack() as gst2:
                        cqT = sbuf(gst2, nm("cqT"), [128, 4, QQ], BF16)
                        cqb = [P.buf() for _ in range(nqc)]
                        with contextlib.ExitStack() as st:
                            nt = NT(st, 5, 512, 4)
                            wq, wqb = load_w(st, s_wq, D, 512, "wq")
                            mm = Rot(P, [psum(st, nm("mm"), [128, 512]) for _ in range(4)], True)
                            sqr = Rot(P, [sbuf(st, nm("sq"), [128, 4, 512], BF16) for _ in range(2)])
                            crr = Rot(P, [sbuf(st, nm("cr"), [128, 4, 512], F32) for _ in range(2)])
                            Rr = Rot(P, [sbuf(st, nm("R"), [128, 512], F32) for _ in range(2)])
                            def qtiles(qc_):
                                q0_ = gi * 2048 + qc_ * 512
                                return [(xq[job][q0_ + 128 * t:q0_ + 128 * (t + 1), :], 128) for t in range(4)]

                            pre = nt.prep(qtiles(0))
                            for qc in range(nqc):
                                hT, hTb, n, _ = nt.finish(pre)
                                if qc + 1 < nqc:
                                    pre = nt.prep(qtiles(qc + 1))
                                sq, sqb = sqr.next()
                                cr, crb = crr.next()
                                for c4 in range(4):
                                    ps, psb = mm.next()
                                    for c in range(8):
                                        MM(ps, wq[:, c, c4 * 128:(c4 + 1) * 128], hT[:, c, :], c == 0, c == 7,
                                           [hTb, wqb[c]], [psb])
                                    ACT(sq[:, c4, :], ps, AF.Square, [psb], [sqb])
                                    COPY("dve", cr[:, c4, :], ps, [psb], [crb])
                                ps, psb = mm.next()
                                for c4 in range(4):
                                    MM(ps, ones_bf, sq[:, c4, :], c4 == 0, c4 == 3, [sqb, cb], [psb])
                                R, Rb = Rr.next()
                                ACT(R, ps, AF.Sqrt, [psb], [Rb], scale=1.0 / 512, bias=EPS)
                                RECIP(R, R, [Rb], [Rb])
                                for c4 in range(4):
                                    TT("dve", cqT[:, c4, qc * 512:(qc + 1) * 512], cr[:, c4, :], R, ALU.mult,
                                       [crb, Rb], [cqb[qc]])
                        P.barrier()
                        if STOP == "Q" + job:
                            dQ = nc.dram_tensor("dbg_cq", [128, 4, QQ], BF16, kind="ExternalOutput").ap()
                            P.dma("sp", dQ, cqT, final=True)
                            P.emit()
                            return nc
                        with contextlib.ExitStack() as st:
                            wuq, wuqb = load_w(st, s_wuq, 512, 2048, "wuq")
                            wukv, wukvb = load_w(st, s_wukv, 256, 2048, "wukv")
                            aoR = Rot(P, [sbuf(st, nm("ao"), [128, 512], BF16) for _ in range(4)])
                            nkv = 2 if job == "A" else 1
                            nkt = 4 * n_full + 1
                            KhR = Rot(P, [sbuf(st, nm("Kh"), [128, L], BF16) for _ in range(nkv)])
                            VhR = Rot(P, [sbuf(st, nm("Vh"), [128, nkt, 128], BF16) for _ in range(nkv)])
                            rq = sbuf(st, nm("rq"), [64, 2, QQ], F32)
                            rqb = P.buf()
                            P.dma("sp", rq, ropeq[job][:, :, 0:QQ].rearrange("a p n -> p a n"),
                                  writes=[rqb])
                            SR = Rot(P, [psum(st, nm("S"), [128, 2, 512]) for _ in range(2)], True)
                            poR = Rot(P, [psum(st, nm("po"), [128, 512]) for _ in range(2)], True)
                            mm = Rot(P, [psum(st, nm("mm"), [128, 512]) for _ in range(2)], True)
                            PTR = Rot(P, [sbuf(st, nm("PT"), [128, 2, 512], BF16) for _ in range(4)])
                            qnR = Rot(P, [sbuf(st, nm("qn"), [128, 512], BF16) for _ in range(2)])
                            qrR = Rot(P, [sbuf(st, nm("qr"), [128, 512], BF16) for _ in range(2)])
                            for (qr_, qrb_) in qrR.items:
                                P.op("dve", lambda e, t=qr_: e.memset(t, 0.0), writes=[qrb_])
                            t12 = Rot(P, [sbuf(st, nm("t12"), [64, 2, 512], F32) for _ in range(2)])
                            accR = Rot(P, [sbuf(st, nm("acc"), [128, 2, 512], F32) for _ in range(2)])
                            recR = Rot(P, [sbuf(st, nm("rec"), [128, 512], F32) for _ in range(2)])

                            def kvproj(h):
                                Kh, Khb = KhR.next()
                                Vh, Vhb = VhR.next()
                                for s in range(nst):
                                    n = 16 if s == n_full else 512
                                    r0 = 512 * s
                                    ps, psb = mm.next()
                                    for c in range(2):
                                        MM(ps[:, 0:n], wukv[:, c, h * 256:h * 256 + 128], ckvT[:, c, r0:r0 + n],
                                           c == 0, c == 1, [ckvb[s], wukvb[c]], [psb])
                                    EVAC(Kh[:, r0:r0 + n], ps[:, 0:n], [psb], [Khb], "act")
                                    ps, psb = mm.next()
                                    if s == n_full:
                                        for c in range(2):
                                            MM(ps[0:16, 0:128], ckvT[:, c, r0:r0 + 16],
                                               wukv[:, c, h * 256 + 128:h * 256 + 256], c == 0, c == 1,
                                               [ckvb[s], wukvb[c]], [psb])
                                        EVAC(Vh[0:16, 4 * n_full, :], ps[0:16, 0:128], [psb], [Vhb], "act")
                                    else:
                                        for t in range(4):
                                            for c in range(2):
                                                MM(ps[:, t * 128:(t + 1) * 128],
                                                   ckvT[:, c, r0 + t * 128:r0 + (t + 1) * 128],
                                                   wukv[:, c, h * 256 + 128:h * 256 + 256], c == 0, c == 1,
                                                   [ckvb[s], wukvb[c]], [psb])
                                        EVAC(Vh[:, 4 * s:4 * s + 4, :], ps.rearrange("p (t d) -> p t d", t=4),
                                             [psb], [Vhb])
                                return Kh, Khb, Vh, Vhb

                            def qproj(h, qc):
                                q0 = qc * 512
                                ps, psb = mm.next()
                                for c in range(4):
                                    MM(ps, wuq[:, c, h * 256:h * 256 + 128], cqT[:, c, q0:q0 + 512],
                                       c == 0, c == 3, [cqb[qc], wuqb[c]], [psb])
                                qn, qnb = qnR.next()
                                COPY("act", qn, ps, [psb], [qnb])
                                tt, ttb = t12.next()
                                for j in range(2):
                                    ps, psb = mm.next()
                                    for c in range(4):
                                        MM(ps[0:64, :], wuq[:, c, h * 256 + 128 + 64 * j:h * 256 + 192 + 64 * j],
                                           cqT[:, c, q0:q0 + 512], c == 0, c == 3, [cqb[qc], wuqb[c]], [psb])
                                    TT("dve", tt[:, j, :], ps[0:64, :], rq[:, j, q0:q0 + 512], ALU.mult,
                                       [psb, rqb], [ttb])
                                qr, qrb = qrR.next()
                                TT(POOL_EW, qr[0:64, :], tt[:, 0, :], tt[:, 1, :], ALU.add, [ttb], [qrb])
                                return qn, qnb, qr, qrb

                            nfull_kb = 4 * n_full
                            npairs = nfull_kb // 2 + 1

                            def emit_S(i, Kh, Khb, qn, qnb, qr, qrb):
                                S, Sb = SR.next()
                                if i == npairs - 1:
                                    k0 = nfull_kb * 128
                                    MM(S[0:16, 0, :], Kh[:, k0:k0 + 16], qn, True, False, [Khb, qnb], [Sb])
                                    MM(S[0:16, 0, :], kropeT[:, k0:k0 + 16], qr, False, True, [krb[n_full], qrb], [Sb])
                                else:
                                    for j in range(2):
                                        k0 = (2 * i + j) * 128
                                        MM(S[:, j, :], Kh[:, k0:k0 + 128], qn, True, False, [Khb, qnb], [Sb])
                                        MM(S[:, j, :], kropeT[:, k0:k0 + 128], qr, False, True, [krb[k0 // 512], qrb], [Sb])
                                return S, Sb

                            def emit_exp(i, S, Sb, acc, accb):
                                PT, PTb = PTR.next()
                                if i == npairs - 1:
                                    ACT(PT[0:16, 0, :], S[0:16, 0, :], AF.Exp, [Sb], [PTb], scale=SCALE)
                                    TT("dve", acc[0:16, 0, :], acc[0:16, 0, :], PT[0:16, 0, :], ALU.add, [accb, PTb], [accb])
                                else:
                                    ACT(PT, S, AF.Exp, [Sb], [PTb], scale=SCALE)
                                    if i == 0:
                                        COPY("dve", acc, PT, [PTb], [accb])
                                    else:
                                        TT("dve", acc, acc, PT, ALU.add, [accb, PTb], [accb])
                                return PT, PTb

                            def emit_pv(i, PT, PTb, Vh, Vhb, po, pob):
                                kp = 2 * i
                                if i == npairs - 1:
                                    MM(po, Vh[0:16, kp, :], PT[0:16, 0, :], False, True, [Vhb, PTb], [pob])
                                else:
                                    for j in range(2):
                                        MM(po, Vh[:, kp + j, :], PT[:, j, :], (kp + j) == 0, False, [Vhb, PTb], [pob])

                            def finalize(h, qc, po, pob, acc, accb):
                                q0 = qc * 512
                                ps, psb = mm.next()
                                MM(ps, ones32, acc[:, 0, :], True, False, [accb, cb], [psb])
                                MM(ps, ones32, acc[:, 1, :], False, True, [accb, cb], [psb])
                                rec, recb = recR.next()
                                RECIP(rec, ps, [psb], [recb])
                                ao, aob = aoR.next()
                                TT("dve", ao, po, rec, ALU.mult, [pob, recb], [aob])
                                ggq = GBASE[job] + qc // 4
                                P.dma("sp", ATs[ggq, :, h, (qc % 4) * 512:(qc % 4 + 1) * 512], ao, reads=[aob])

                            if job == "A" and bg_tasks:
                                bstage = Rot(P, [sbuf(st, nm("bst"), [128, BGW], F32) for _ in range(4)])
                                bobuf = Rot(P, [sbuf(st, nm("bob"), [128, BGW], BF16) for _ in range(4)])

                            its = [(h, qc) for h in range(8) for qc in range(nqc)]
                            kv = {0: kvproj(0)}
                            q_next = qproj(0, 0)
                            pending = None
                            for idx, (h, qc) in enumerate(its):
                                qn, qnb, qr, qrb = q_next
                                Kh, Khb, Vh, Vhb = kv[h]
                                acc, accb = accR.next()
                                po, pob = poR.next()
                                nxt = its[idx + 1] if idx + 1 < len(its) else None
                                S_next = emit_S(0, Kh, Khb, qn, qnb, qr, qrb)
                                for i in range(npairs):
                                    S, Sb = S_next
                                    if i + 1 < npairs:
                                        S_next = emit_S(i + 1, Kh, Khb, qn, qnb, qr, qrb)
                                    if i == 1 and pending is not None:
                                        finalize(*pending)
                                        pending = None
                                    if i == 3 and nxt is not None:
                                        if nxt[0] != h and nkv == 2:
                                            kv[nxt[0]] = kvproj(nxt[0])
                                        q_next = qproj(nxt[0], nxt[1])
                                    PTcur = emit_exp(i, S, Sb, acc, accb)
                                    if i >= 1:
                                        emit_pv(i - 1, PTprev[0], PTprev[1], Vh, Vhb, po, pob)
                                    PTprev = PTcur
                                    if job == "A" and bg_tasks and i in (5, 9, 13):
                                        conv_chunk(bstage, bobuf, ("act",), "sp", *bg_tasks.pop(0))
                                emit_pv(npairs - 1, PTprev[0], PTprev[1], Vh, Vhb, po, pob)
                                pending = (h, qc, po, pob, acc, accb)
                                if nxt is not None and nxt[0] != h and nkv == 1:
                                    kv[nxt[0]] = kvproj(nxt[0])
                            finalize(*pending)
                        P.barrier()
                        if STOP == "At" + job:
                            P.emit()
                            return nc

        if bg_tasks:
            with contextlib.ExitStack() as st:
                stage = Rot(P, [sbuf(st, nm("wst"), [128, BGW], F32) for _ in range(3)])
                obuf = Rot(P, [sbuf(st, nm("wob"), [128, BGW], BF16) for _ in range(3)])
                while bg_tasks:
                    conv_chunk(stage, obuf, ("act", "dve"), "act", *bg_tasks.pop(0))
            P.barrier()
        with contextlib.ExitStack() as st:
            nt = NT(st, 8)
            wgt, wgtb = load_w(st, s_wgt, D, 2048, "wgt", 2)
            wfo, wfob = load_w(st, s_wfo, 512, D, "wfo")
            wao, waob = load_w(st, s_wao, D, D, "wao")
            wo, wob = load_w(st, s_wo, D, D, "wo")
            mm = Rot(P, [psum(st, nm("mm"), [128, 512]) for _ in range(6)], True)
            FTc = Rot(P, [sbuf(st, nm("FTc"), [128, 4, 512], BF16) for _ in range(2)])
            ATc = Rot(P, [sbuf(st, nm("ATc"), [128, 8, 512], BF16) for _ in range(2)])
            sgR = Rot(P, [sbuf(st, nm("sg"), [128, 2, 512], BF16) for _ in range(2)])
            tAB = Rot(P, [sbuf(st, nm("tAB"), [128, 2, 512], F32) for _ in range(2)])
            mTR = Rot(P, [sbuf(st, nm("mT"), [128, 8, 512], BF16) for _ in range(2)])
            x1R = Rot(P, [sbuf(st, nm("x1"), [128, D], F32) for _ in range(5)])
            pend = []
            def t1tiles(ci_):
                gg_, qc_ = ci_ // 4, ci_ % 4
                job_ = "A" if gg_ < 2 else "B"
                gi_ = gg_ if gg_ < 2 else 0
                q0_ = gi_ * 2048 + qc_ * 512
                return [(xq[job_][q0_ + 128 * t:q0_ + 128 * (t + 1), :], 128) for t in range(4)]

            pre = nt.prep(t1tiles(0))
            for gg in range(3):
                for qc in range(4):
                    hT, hTb, n, xl = nt.finish(pre)
                    if gg * 4 + qc + 1 < 12:
                        pre = nt.prep(t1tiles(gg * 4 + qc + 1))
                    for (d_, s_, b_) in pend:
                        P.dma("sp", d_, s_, reads=[b_])
                    pend = []
                    ft, ftb = FTc.next()
                    P.dma("sp", ft, FTs[gg, :, :, qc * 512:(qc + 1) * 512], writes=[ftb])
                    at, atb = ATc.next()
                    P.dma("sp", at, ATs[gg, :, :, qc * 512:(qc + 1) * 512], writes=[atb])
                    mT, mTb = mTR.next()
                    for j in range(8):
                        sg, sgb = sgR.next()
                        for k in range(2):
                            ps, psb = mm.next()
                            for c in range(8):
                                MM(ps, wgt[:, c, k * 1024 + j * 128:k * 1024 + (j + 1) * 128], hT[:, c, :],
                                   c == 0, c == 7, [hTb, wgtb[c]], [psb])
                            ACT(sg[:, k, :], ps, AF.Sigmoid, [psb], [sgb])
                        tab_, tabb = tAB.next()
                        ps, psb = mm.next()
                        for c in range(4):
                            MM(ps, wfo[:, c, j * 128:(j + 1) * 128], ft[:, c, :], c == 0, c == 3, [ftb, wfob[c]], [psb])
                        TT("dve", tab_[:, 0, :], ps, sg[:, 0, :], ALU.mult, [psb, sgb], [tabb])
                        ps, psb = mm.next()
                        for c in range(8):
                            MM(ps, wao[:, c, j * 128:(j + 1) * 128], at[:, c, :], c == 0, c == 7, [atb, waob[c]], [psb])
                        TT("dve", tab_[:, 1, :], ps, sg[:, 1, :], ALU.mult, [psb, sgb], [tabb])
                        TT(POOL_EW, mT[:, j, :], tab_[:, 0, :], tab_[:, 1, :], ALU.add, [tabb], [mTb])
                    for t in range(4):
                        x1, x1b = x1R.next()
                        x, xb, _r = xl[t]
                        for hh in range(2):
                            ps, psb = mm.next()
                            for c in range(8):
                                MM(ps, mT[:, c, t * 128:(t + 1) * 128], wo[:, c, hh * 512:(hh + 1) * 512],
                                   c == 0, c == 7, [mTb, wob[c]], [psb])
                            TT("dve", x1[:, hh * 512:(hh + 1) * 512], ps, x[:, hh * 512:(hh + 1) * 512], ALU.add,
                               [psb, xb], [x1b])
                        row = gg * 2048 + qc * 512 + t * 128
                        pend.append((x1s[row:row + 128, :], x1, x1b))
            for (d_, s_, b_) in pend:
                P.dma("sp", d_, s_, reads=[b_])
        P.barrier()
        if STOP == "T1":
            P.emit()
            return nc

        TC = 256
        with contextlib.ExitStack() as st:
            nt = NT(st, 4, TC)
            wg, wgb = load_w(st, s_wg, D, DFF, "wg", 2)
            wu, wub = load_w(st, s_wu, D, DFF, "wu", 2)
            wd, wdb = load_w(st, s_wd, DFF, D, "wd", 2)
            gfin = sbuf(st, "gfin_sb", [128, D], F32)
            gfb = P.buf()
            P.dma("sp", gfin, gfin_d, writes=[gfb])
            mm = Rot(P, [psum(st, nm("mm"), [128, 512]) for _ in range(6)], True)
            aTR = Rot(P, [sbuf(st, nm("aT"), [128, 22, TC], BF16)])
            slR = Rot(P, [sbuf(st, nm("sl"), [128, TC], F32) for _ in range(2)])
            x2R = Rot(P, [sbuf(st, nm("x2"), [128, D], F32) for _ in range(4)])
            pend = []
            stR = Rot(P, [sbuf(st, nm("st2"), [128, 2], F32) for _ in range(2)])
            junk2 = sbuf(st, nm("junk2"), [128, D], BF16)
            junk2b = P.buf()

            def STT(out, in0, sc, in1, reads, writes):
                P.op("dve", lambda e: e.scalar_tensor_tensor(out=out, in0=in0, scalar=sc, in1=in1,
                                                             op0=ALU.mult, op1=ALU.mult), reads, writes)

            nt2 = TC // 128
            def t2tiles(ci_):
                return [(x1s[ci_ * TC + 128 * t:ci_ * TC + 128 * (t + 1), :], 128) for t in range(nt2)]

            nch2 = 6144 // TC
            pre = nt.prep(t2tiles(0))
            for ci in range(nch2):
                row0 = ci * TC
                hT, hTb, n, xl = nt.finish(pre)
                if ci + 1 < nch2:
                    pre = nt.prep(t2tiles(ci + 1))
                for (d_, s_, b_) in pend:
                    P.dma("sp", d_, s_, reads=[b_], final=True)
                pend = []
                aT, aTb = aTR.next()
                for j in range(22):
                    pg, pgb = mm.next()
                    for c in range(8):
                        MM(pg[:, 0:TC], wg[:, c, j * 128:(j + 1) * 128], hT[:, c, :], c == 0, c == 7, [hTb, wgb[c]], [pgb])
                    pu, pub = mm.next()
                    for c in range(8):
                        MM(pu[:, 0:TC], wu[:, c, j * 128:(j + 1) * 128], hT[:, c, :], c == 0, c == 7, [hTb, wub[c]], [pub])
                    sl, slb = slR.next()
                    ACT(sl, pg[:, 0:TC], AF.Silu, [pgb], [slb])
                    TT("dve", aT[:, j, :], sl, pu[:, 0:TC], ALU.mult, [slb, pub], [aTb])
                for t in range(nt2):
                    x2, x2b = x2R.next()
                    x, xb, _r = xl[t]
                    for hh in range(2):
                        ps, psb = mm.next()
                        for j in range(22):
                            MM(ps, aT[:, j, t * 128:(t + 1) * 128], wd[:, j, hh * 512:(hh + 1) * 512],
                               j == 0, j == 21, [aTb, wdb[j]], [psb])
                        TT("dve", x2[:, hh * 512:(hh + 1) * 512], ps, x[:, hh * 512:(hh + 1) * 512], ALU.add,
                           [psb, xb], [x2b])
                    s2, s2b = stR.next()
                    ACT(junk2, x2, AF.Square, [x2b], [junk2b, s2b], accum_out=s2[:, 0:1])
                    ACT(s2[:, 1:2], s2[:, 0:1], AF.Sqrt, [s2b], [s2b], scale=1.0 / D, bias=EPS)
                    RECIP(s2[:, 1:2], s2[:, 1:2], [s2b], [s2b])
                    STT(x2, x2, s2[:, 1:2], gfin, [x2b, s2b, gfb], [x2b])
                    r_ = row0 + t * 128
                    if r_ < 4096:
                        dst = yA[r_:r_ + 128, :]
                    else:
                        dst = yB[r_ - 4096:r_ - 4096 + 128, :]
                    pend.append((dst, x2, x2b))
            for (d_, s_, b_) in pend:
                P.dma("sp", d_, s_, reads=[b_], final=True)
        P.emit()
    return nc


def _rope_tab(pos):
    inv = (1.0 / (10000.0 ** (np.arange(0, 64, 2, dtype=np.float32) / np.float32(64)))).astype(np.float32)
    ang = pos.astype(np.float32)[:, None] * inv[None, :]
    c = np.cos(ang).astype(np.float32).T
    s = np.sin(ang).astype(np.float32).T
    cc = np.concatenate([c, c], 0)
    ss = np.concatenate([-s, s], 0)
    return np.ascontiguousarray(np.stack([cc, ss], 0))


def _dft_tab(G, qpos, chunk=512):
    L = G["L"]
    nq = len(qpos)
    nqc = max(1, nq // chunk)
    w = nq // nqc
    out = np.zeros((nqc, G["nb"], 128, 2 * w), dtype=ml_dtypes.bfloat16)
    sc = 1.0 / np.sqrt(L)
    q = qpos.astype(np.int64)
    for b, sp in enumerate(G["sblocks"]):
        m = len(sp)
        prod = (sp.astype(np.int64)[:, None] * q[None, :]) % L
        ang = prod.astype(np.float64) * (2.0 * np.pi / L)
        cs = (np.cos(ang) * sc).reshape(m, nqc, w)
        sn = (-np.sin(ang) * sc).reshape(m, nqc, w)
        out[:, b, 0:m, 0:w] = cs.transpose(1, 0, 2).astype(ml_dtypes.bfloat16)
        out[:, b, 0:m, w:2 * w] = sn.transpose(1, 0, 2).astype(ml_dtypes.bfloat16)
    return out


def _qorders():
    fA = np.arange(16, LA // 2)
    sA = np.array([LA // 2] + list(range(LA - 15, LA)))
    qA = np.concatenate([fA, sA[0:8], LA - fA, sA[8:16]])
    assert len(qA) == 4096 and len(set(qA.tolist())) == 4096 and qA.min() == 16 and qA.max() == LA - 1
    fB = np.arange(16, LB // 2)
    sB = np.array([LB // 2] + list(range(LB - 15, LB)))
    qB = []
    for qt in range(4):
        f = fB[1022 * qt:1022 * (qt + 1)]
        sgl = sB[4 * qt:4 * qt + 4]
        qB.append(np.concatenate([f, sgl[0:2], LB - f, sgl[2:4]]))
    allB = np.concatenate(qB)
    assert len(allB) == 8192 and len(set(allB.tolist())) == 8192 and allB.min() == 16 and allB.max() == LB - 1
    return qA, qB


_CACHE = {}


def _consts():
    if "c" in _CACHE:
        return _CACHE["c"]
    k = np.arange(128)
    ang = 2.0 * np.pi * ((k[:, None] * k[None, :]) % 128) / 128.0
    sc = 1.0 / np.sqrt(128.0)
    c128 = np.concatenate([np.cos(ang) * sc, np.sin(ang) * sc, -np.sin(ang) * sc], 1).astype(ml_dtypes.bfloat16)
    GA, GB = GEOM["A"], GEOM["B"]
    qA, qB = _qorders()
    tabA = _dft_tab(GA, qA[0:2048])
    tabAs = np.ascontiguousarray(_dft_tab(GA, qA[4088:4096], chunk=8)[0])
    tabB = [_dft_tab(GB, qB[qt][0:1024]) for qt in range(4)]
    tabBs = [np.ascontiguousarray(_dft_tab(GB, qB[qt][2046:2048], chunk=2)[0]) for qt in range(4)]
    ropekA = _rope_tab(GA["korder"])
    ropekB = _rope_tab(GB["korder"])
    ropeqA = _rope_tab(qA)
    ropeqB = [_rope_tab(qB[qt]) for qt in range(4)]
    _CACHE["c"] = dict(c128=c128, tabA=tabA, tabB=tabB, tabAs=tabAs, tabBs=tabBs, ropekA=ropekA, ropekB=ropekB,
                       ropeqA=ropeqA, ropeqB=ropeqB, qA=qA, qB=qB)
    return _CACHE["c"]


def kernel(x_prompt, x_sample, meta_tokens, norm1_g, w_in, q_norm_g, kv_norm_g, w_uq, w_ukv,
           w_fourier_out, w_attn_out, w_o, norm2_g, w_ffn_gate, w_ffn_up, w_ffn_down, final_norm_g):
    f = lambda a: np.ascontiguousarray(np.asarray(a, dtype=np.float32))
    x_prompt, x_sample, meta = f(x_prompt), f(x_sample), f(meta_tokens)
    C = _consts()
    gl = lambda g: np.ascontiguousarray(f(g).reshape(-1, 128).T)
    common = {
        "c128": C["c128"], "g1": gl(norm1_g[0]), "gq": gl(q_norm_g[0]), "gkv": gl(kv_norm_g[0]), "g2": gl(norm2_g[0]),
        "gfin": np.ascontiguousarray(np.broadcast_to(f(final_norm_g)[None, :], (128, D))),
        "w_in": f(w_in[0]), "w_uq": f(w_uq[0]), "w_ukv": f(w_ukv[0]), "w_fo": f(w_fourier_out[0]),
        "w_ao": f(w_attn_out[0]), "w_o": f(w_o[0]), "w_g": f(w_ffn_gate[0]), "w_u": f(w_ffn_up[0]),
        "w_d": f(w_ffn_down[0]), "tabA": C["tabA"], "tabAs": C["tabAs"], "ropekA": C["ropekA"], "ropeqA": C["ropeqA"],
        "ropekB": C["ropekB"],
    }
    koA, koB = GEOM["A"]["korder"], GEOM["B"]["korder"]
    qA, qB = C["qA"], C["qB"]
    xkB = []
    for s in range(2):
        full = np.concatenate([meta, x_sample[s]], 0)
        xkB.append(np.ascontiguousarray(full[koB]))
    in_maps = []
    for c in range(8):
        fullA = np.concatenate([meta, x_prompt[c]], 0)
        s, qt = c // 4, c % 4
        m = dict(common)
        m["xkA"] = np.ascontiguousarray(fullA[koA])
        m["xqA"] = np.ascontiguousarray(x_prompt[c][qA - 16])
        m["xkB"] = xkB[s]
        m["xqB"] = np.ascontiguousarray(x_sample[s][qB[qt] - 16])
        m["tabB"] = C["tabB"][qt]
        m["tabBs"] = C["tabBs"][qt]
        if SMALL_TABS:
            m["tabA"] = C["tabA"][0:1, 0:1]
            m["tabB"] = C["tabB"][qt][0:1, 0:1]
        m["ropeqB"] = C["ropeqB"][qt]
        in_maps.append(m)
    if "nc" not in _CACHE:
        _CACHE["nc"] = build_program()
    res = run_bass_kernel_spmd(_CACHE["nc"], in_maps, core_ids=list(range(8)))
    _CACHE["last"] = res
    y_prompt = np.empty((8, 4096, D), np.float32)
    y_sample = np.empty((2, 8192, D), np.float32)
    for c in range(8):
        s, qt = c // 4, c % 4
        y_prompt[c][qA - 16] = np.asarray(res.results[c]["yA"], dtype=np.float32)
        y_sample[s][qB[qt] - 16] = np.asarray(res.results[c]["yB"], dtype=np.float32)
    return (y_prompt, y_sample)
```

```python
import contextlib
import numpy as np
import ml_dtypes
import concourse.bass as bass
import concourse.mybir as mybir
from concourse.bass_utils import run_bass_kernel_spmd

F32 = mybir.dt.float32
BF16 = mybir.dt.bfloat16
AF = mybir.ActivationFunctionType
ALU = mybir.AluOpType

D = 1024
NM = 16
DFF = 2816
EPS = 1e-6
SCALE = 192 ** -0.5
LA, LB = 4112, 8208
DEBUG = False
STOP = None
KLIM = None
SMALL_TABS = False
POOL_EW = "dve"
POOL_AT = "dve"


class Buf:
    __slots__ = ("name", "writer", "readers", "excl")

    def __init__(self, name="", excl=False):
        self.name = name
        self.writer = None
        self.readers = []
        self.excl = excl


class Op:
    __slots__ = ("eng", "fn", "deps", "need_inc", "val", "is_dma", "dsem", "dval")

    def __init__(self, eng, fn, is_dma=False):
        self.eng = eng
        self.fn = fn
        self.deps = []
        self.need_inc = False
        self.val = None
        self.is_dma = is_dma
        self.dsem = None
        self.dval = None


ENGS = ("pe", "act", "dve", "pool", "sp")


class Prog:
    def __init__(self, nc, n_dma_sems=48):
        self.nc = nc
        self.ops = {e: [] for e in ENGS}
        self.n_dma_sems = n_dma_sems
        self.dma_last = [None] * n_dma_sems
        self.dma_uses = [0] * n_dma_sems
        self.dma_n = {"sp": 0, "pool": 0, "act": 0}
        self.dma_rng = {"sp": (0, 24), "act": (24, 16), "pool": (40, n_dma_sems - 40)}
        self.all_bufs = []
        self.final_dmas = []

    def buf(self, name="", excl=False):
        b = Buf(name, excl)
        self.all_bufs.append(b)
        return b

    def _add_dep(self, op, prod):
        if prod is None or prod is op:
            return
        if (not prod.is_dma) and prod.eng == "pe" and op.eng == "pe" and not op.is_dma:
            return
        if not prod.is_dma:
            prod.need_inc = True
        op.deps.append(prod)

    @staticmethod
    def _flat(x):
        out = []
        for b in x:
            if isinstance(b, (list, tuple)):
                out.extend(Prog._flat(b))
            else:
                out.append(b)
        return out

    def op(self, eng, fn, reads=(), writes=(), dma=False):
        o = Op(eng, fn, is_dma=dma)
        reads = self._flat(reads)
        writes = self._flat(writes)
        xr = [b for b in reads if b.excl and b not in writes]
        if xr:
            reads = [b for b in reads if not b.excl]
            writes = list(writes) + xr
        for b in reads:
            self._add_dep(o, b.writer)
        for b in writes:
            self._add_dep(o, b.writer)
            for r in b.readers:
                self._add_dep(o, r)
        if dma:
            base, cnt = self.dma_rng[eng]
            k = base + self.dma_n[eng] % cnt
            self.dma_n[eng] += 1
            self._add_dep(o, self.dma_last[k])
            self.dma_last[k] = o
            self.dma_uses[k] += 1
            o.dsem = k
            o.dval = 16 * self.dma_uses[k]
        for b in reads:
            b.readers.append(o)
        for b in writes:
            b.writer = o
            b.readers = []
        self.ops[eng].append(o)
        return o

    def dma(self, eng, out_ap, in_ap, reads=(), writes=(), final=False):
        o = self.op(eng, lambda e: e.dma_start(out=out_ap, in_=in_ap), reads, writes, dma=True)
        if final:
            self.final_dmas.append(o)
        return o

    def barrier(self):
        lasts = []
        for e in ENGS:
            for o in reversed(self.ops[e]):
                if not o.is_dma and o.fn is not None:
                    lasts.append(o)
                    break
        for k in range(self.n_dma_sems):
            if self.dma_last[k] is not None:
                lasts.append(self.dma_last[k])
        for e in ENGS:
            o = Op(e, None)
            for p in lasts:
                if (not p.is_dma) and p.eng == e and e == "pe":
                    continue
                if not p.is_dma:
                    p.need_inc = True
                o.deps.append(p)
            self.ops[e].append(o)
        for b in self.all_bufs:
            b.writer = None
            b.readers = []

    def emit(self):
        nc = self.nc
        fin = Op("sp", None)
        for o in self.final_dmas:
            fin.deps.append(o)
        for k in range(self.n_dma_sems):
            if self.dma_last[k] is not None:
                fin.deps.append(self.dma_last[k])
        self.ops["sp"].append(fin)
        for e in ENGS:
            c = 0
            for o in self.ops[e]:
                if o.is_dma:
                    continue
                if o.need_inc:
                    c += 1
                o.val = c
        with contextlib.ExitStack() as st:
            esem = {e: st.enter_context(nc.semaphore("s_" + e)) for e in ENGS}
            dsem = [st.enter_context(nc.semaphore("d%d" % k)) for k in range(self.n_dma_sems)]
            block = st.enter_context(nc.Block())
            ops = self.ops

            def run(e, eng):
                waited = {}
                for o in ops[e]:
                    for p in o.deps:
                        if p.is_dma:
                            key, sem, val = ("d", p.dsem), dsem[p.dsem], p.dval
                        else:
                            key, sem, val = ("e", p.eng), esem[p.eng], p.val
                        if waited.get(key, 0) >= val:
                            continue
                        waited[key] = val
                        eng.wait_ge(sem, val)
                    if o.fn is None:
                        continue
                    ins = o.fn(eng)
                    if o.is_dma:
                        ins.then_inc(dsem[o.dsem], 16)
                    elif o.need_inc:
                        ins.then_inc(esem[e], 1)

            @block.tensor
            def _(eng):
                run("pe", eng)

            @block.scalar
            def _(eng):
                run("act", eng)

            @block.vector
            def _(eng):
                run("dve", eng)

            @block.gpsimd
            def _(eng):
                run("pool", eng)

            @block.sync
            def _(eng):
                run("sp", eng)


class Rot:
    def __init__(self, P, aps, excl=False):
        self.items = [(ap, P.buf("", excl)) for ap in aps]
        self.i = 0

    def next(self):
        it = self.items[self.i % len(self.items)]
        self.i += 1
        return it


def job_geom(L):
    half = L // 2
    Fp = np.arange(1, half)
    Mp = L - Fp
    n_full = len(Fp) // 256
    rem = len(Fp) - 256 * n_full
    assert rem == 7
    korder = []
    sblocks = []
    for s in range(n_full):
        korder += [Fp[256 * s:256 * s + 256], Mp[256 * s:256 * s + 256]]
        sblocks += [Fp[256 * s:256 * s + 128], Fp[256 * s + 128:256 * s + 256]]
    korder += [Fp[-rem:], np.array([0, half]), Mp[-rem:]]
    sblocks += [np.concatenate([Fp[-rem:], np.array([0, half])])]
    korder = np.concatenate(korder)
    assert len(korder) == L and len(set(korder.tolist())) == L
    return dict(L=L, n_full=n_full, korder=korder, sblocks=sblocks, nb=2 * n_full + 1)


GEOM = {"A": job_geom(LA), "B": job_geom(LB)}
NGROUPS = {"A": 1, "B": 1}
QN = {"A": 4096, "B": 2048}
GBASE = {"A": 0, "B": 2}


def build_program():
    nc = bass.Bass("TRN2", target_bir_lowering=False)
    P = Prog(nc)

    def din(name, shape, dt=F32):
        return nc.dram_tensor(name, shape, dt, kind="ExternalInput").ap()

    def dscr(name, shape, dt=BF16):
        kind = "ExternalOutput" if DEBUG else "Internal"
        return nc.dram_tensor(name, shape, dt, kind=kind).ap()

    xk = {"A": din("xkA", [LA, D]), "B": din("xkB", [LB, D])}
    xq = {"A": din("xqA", [4096, D]), "B": din("xqB", [2048, D])}
    ropek = {"A": din("ropekA", [2, 64, LA]), "B": din("ropekB", [2, 64, LB])}
    ropeq = {"A": din("ropeqA", [2, 64, 4096]), "B": din("ropeqB", [2, 64, 2048])}
    if SMALL_TABS:
        tab = {"A": din("tabA", [1, 1, 128, 1024], BF16), "B": din("tabB", [1, 1, 128, 1024], BF16)}
    else:
        tab = {"A": din("tabA", [4, GEOM["A"]["nb"], 128, 1024], BF16),
               "B": din("tabB", [2, GEOM["B"]["nb"], 128, 1024], BF16)}
    tabs_small = {"A": din("tabAs", [GEOM["A"]["nb"], 128, 16], BF16),
                  "B": din("tabBs", [GEOM["B"]["nb"], 128, 4], BF16)}
    c128_d = din("c128", [128, 384], BF16)
    g1_d = din("g1", [128, 8])
    gq_d = din("gq", [128, 4])
    gkv_d = din("gkv", [128, 2])
    g2_d = din("g2", [128, 8])
    gfin_d = din("gfin", [128, D])
    w_in_d = din("w_in", [D, 3392])
    w_uq_d = din("w_uq", [512, 1536])
    w_ukv_d = din("w_ukv", [256, 2048])
    w_fo_d = din("w_fo", [512, D])
    w_ao_d = din("w_ao", [D, D])
    w_o_d = din("w_o", [D, D])
    w_g_d = din("w_g", [D, DFF])
    w_u_d = din("w_u", [D, DFF])
    w_d_d = din("w_d", [DFF, D])
    yA = nc.dram_tensor("yA", [4096, D], F32, kind="ExternalOutput").ap()
    yB = nc.dram_tensor("yB", [2048, D], F32, kind="ExternalOutput").ap()

    s_wk = dscr("s_wk", [D, 896])
    s_wq = dscr("s_wq", [D, 512])
    s_wgt = dscr("s_wgt", [D, 2048])
    s_wuq = dscr("s_wuq", [512, 2048])
    s_wukv = dscr("s_wukv", [256, 2048])
    s_wfo = dscr("s_wfo", [512, D])
    s_wao = dscr("s_wao", [D, D])
    s_wo = dscr("s_wo", [D, D])
    s_wg = dscr("s_wg", [D, DFF])
    s_wu = dscr("s_wu", [D, DFF])
    s_wd = dscr("s_wd", [DFF, D])
    FTs = dscr("FTs", [3, 128, 4, 2048])
    ATs = dscr("ATs", [3, 128, 8, 2048])
    x1s = dscr("x1s", [6144, D], F32)
    scr_bufs = {}

    def sb(name):
        if name not in scr_bufs:
            scr_bufs[name] = P.buf(name)
        return scr_bufs[name]

    def MM(out, lhsT, rhs, start, stop, reads, writes):
        P.op("pe", lambda e: e.matmul(out, lhsT=lhsT, rhs=rhs, start=start, stop=stop), reads, writes)

    def TR(out, in_, ident, reads, writes):
        P.op("pe", lambda e: e.transpose(out=out, in_=in_, identity=ident), reads, writes)

    def ACT(out, in_, func, reads, writes, scale=1.0, bias=0.0, accum_out=None):
        if accum_out is None:
            P.op("act", lambda e: e.activation(out=out, in_=in_, func=func, scale=scale, bias=bias), reads, writes)
        else:
            P.op("act", lambda e: e.activation(out=out, in_=in_, func=func, scale=scale, bias=bias,
                                               accum_out=accum_out), reads, writes)

    def ACTS(out, in_, func, scale_ap, reads, writes):
        P.op("act", lambda e: e.activation(out=out, in_=in_, func=func, scale=scale_ap), reads, writes)

    def COPY(eng, out, in_, reads, writes):
        if eng == "act":
            P.op("act", lambda e: e.copy(out=out, in_=in_), reads, writes)
        else:
            P.op(eng, lambda e: e.tensor_copy(out=out, in_=in_), reads, writes)

    def TT(eng, out, in0, in1, op, reads, writes):
        P.op(eng, lambda e: e.tensor_tensor(out=out, in0=in0, in1=in1, op=op), reads, writes)

    def RECIP(out, in_, reads, writes):
        P.op("dve", lambda e: e.reciprocal(out=out, in_=in_), reads, writes)

    def TSMUL(out, in0, sc, reads, writes):
        P.op("dve", lambda e: e.tensor_scalar_mul(out=out, in0=in0, scalar1=sc), reads, writes)

    evac_ctr = [0]

    def EVAC(out, in_, reads, writes, pref=None):
        evac_ctr[0] += 1
        COPY(pref or ("act" if evac_ctr[0] % 2 else "dve"), out, in_, reads, writes)

    def sbuf(st, name, shape, dt):
        return st.enter_context(nc.sbuf_tensor(name, shape, dt)).ap()

    def psum(st, name, shape, dt=F32):
        return st.enter_context(nc.psum_tensor(name, shape, dt)).ap()

    uid = [0]

    def nm(s):
        uid[0] += 1
        return "%s_%d" % (s, uid[0])

    def load_w(st, scr, R, C, name, nsplit=1):
        t = sbuf(st, nm(name), [128, R // 128, C], BF16)
        src = scr.rearrange("(c p) n -> p c n", p=128)
        nch = R // 128
        bl = []
        for c0 in range(nch):
            b = P.buf(name)
            P.dma("sp", t[:, c0, :], src[:, c0, :], writes=[b])
            bl.append(b)
        return t, bl

    with contextlib.ExitStack() as gst:
        ident = sbuf(gst, "ident", [128, 128], BF16)
        identf = sbuf(gst, "identf", [128, 128], F32)
        ones_bf = sbuf(gst, "ones_bf", [128, 128], BF16)
        ones32 = sbuf(gst, "ones32", [128, 128], F32)
        c128 = sbuf(gst, "c128s", [128, 384], BF16)
        cb = P.buf("const")
        P.op("pool", lambda e: e.memset(identf, 0.0), writes=[cb])
        P.op("pool", lambda e: e.affine_select(out=identf, in_=identf, pattern=[[-1, 128]],
                                               compare_op=ALU.not_equal, fill=1.0, base=0,
                                               channel_multiplier=1), reads=[cb], writes=[cb])
        P.op("dve", lambda e: e.tensor_copy(out=ident, in_=identf), reads=[cb], writes=[cb])
        P.op("pool", lambda e: e.memset(ones32, 1.0), writes=[cb])
        P.op("dve", lambda e: e.tensor_copy(out=ones_bf, in_=ones32), reads=[cb], writes=[cb])
        P.dma("sp", c128, c128_d, writes=[cb])

        class NT:
            def __init__(self, st, nx, ncol=512, npt=2):
                self.xs = Rot(P, [sbuf(st, nm("x"), [128, D], F32) for _ in range(nx)])
                self.stt = Rot(P, [sbuf(st, nm("st"), [128, 2], F32) for _ in range(12)])
                self.xn = Rot(P, [sbuf(st, nm("xn"), [128, D], BF16) for _ in range(4)])
                self.junk = sbuf(st, nm("junk"), [128, D], BF16)
                self.junkb = P.buf()
                self.pT = Rot(P, [psum(st, nm("pT"), [128, 8, 128], BF16) for _ in range(npt)], True)
                self.hTa = [sbuf(st, nm("hT"), [128, 8, ncol], BF16) for _ in range(2)]
                self.hTbs = [[P.buf() for _ in range(4)] for _ in range(2)]
                self.hi = 0

            def prep(self, tiles):
                stg = []
                for ti, (src, r) in enumerate(tiles):
                    x, xb = self.xs.next()
                    P.dma("sp", x[0:r, :], src, writes=[xb])
                    s, sbf = self.stt.next()
                    ACT(self.junk[0:r, :], x[0:r, :], AF.Square, [xb], [self.junkb, sbf], accum_out=s[0:r, 0:1])
                    ACT(s[0:r, 1:2], s[0:r, 0:1], AF.Sqrt, [sbf], [sbf], scale=1.0 / D, bias=EPS)
                    stg.append([x, xb, r, s, sbf, None, None])
                for e_ in stg:
                    x, xb, r, s, sbf = e_[0:5]
                    xn, xnb = self.xn.next()
                    RECIP(s[0:r, 1:2], s[0:r, 1:2], [sbf], [sbf])
                    TSMUL(xn[0:r, :], x[0:r, :], s[0:r, 1:2], [xb, sbf], [xnb])
                    e_[5], e_[6] = xn, xnb
                return stg

            def finish(self, stg):
                hT, hTb = self.hTa[self.hi % 2], self.hTbs[self.hi % 2]
                self.hi += 1
                col = 0
                xl = []
                for ti, (x, xb, r, s, sbf, xn, xnb) in enumerate(stg):
                    pT, pTb = self.pT.next()
                    for c in range(8):
                        TR(pT[:, c, 0:r], xn[0:r, c * 128:(c + 1) * 128], ident[0:r, 0:r], [xnb, cb], [pTb])
                    EVAC(hT[:, :, col:col + r], pT[:, :, 0:r], [pTb], [hTb[ti]])
                    xl.append((x, xb, r))
                    col += r
                return hT, hTb, col, xl

            def run(self, tiles):
                return self.finish(self.prep(tiles))

        gains = {}
        for name, gd, n in (("g1", g1_d, 8), ("gq", gq_d, 4), ("gkv", gkv_d, 2), ("g2", g2_d, 8)):
            t = sbuf(gst, name + "_sb", [128, n], F32)
            b = P.buf(name)
            P.dma("sp", t, gd, writes=[b])
            gains[name] = (t, b)
        wctr = [0]

        def conv_chunk(stage, obuf, engs, stq, dst, src, rc, pieces, gain, d_lo, d_hi):
            lo = min(p[1] for p in pieces)
            hi = max(p[1] + p[2] for p in pieces)
            stg, stb = stage.next()
            P.dma("sp", stg[:, 0:hi - lo], src[rc * 128:(rc + 1) * 128, lo:hi], writes=[stb])
            ob, obb = obuf.next()
            for (d0, s0, w) in pieces:
                wctr[0] += 1
                eng = engs[wctr[0] % len(engs)]
                o_ap = ob[:, d0 - d_lo:d0 - d_lo + w]
                i_ap = stg[:, s0 - lo:s0 - lo + w]
                if gain is None:
                    COPY(eng, o_ap, i_ap, [stb], [obb])
                else:
                    gt, gb = gains[gain]
                    if eng == "act":
                        ACTS(o_ap, i_ap, AF.Copy, gt[:, rc:rc + 1], [stb, gb], [obb])
                    else:
                        TSMUL(o_ap, i_ap, gt[:, rc:rc + 1], [stb, gb], [obb])
            P.dma(stq, dst[rc * 128:(rc + 1) * 128, d_lo:d_hi], ob[:, 0:d_hi - d_lo], reads=[obb])

        def conv_tasks(dst, Cd, src, R, pieces, gain, maxw=None):
            tasks = []
            if maxw is None:
                for rc in range(R // 128):
                    tasks.append((dst, src, rc, pieces, gain, 0, Cd))
            else:
                assert len(pieces) == 1
                d0, s0, w = pieces[0]
                nsp = (w + maxw - 1) // maxw
                step = (w + nsp - 1) // nsp
                for rc in range(R // 128):
                    for o in range(0, w, step):
                        ww = min(step, w - o)
                        tasks.append((dst, src, rc, [(d0 + o, s0 + o, ww)], gain, d0 + o, d0 + o + ww))
            return tasks

        pcs = []
        for h in range(8):
            pcs += [(256 * h, 192 * h, 192), (256 * h + 192, 192 * h + 160, 32), (256 * h + 224, 192 * h + 128, 32)]
        fg_tasks = (conv_tasks(s_wk, 896, w_in_d, D,
                               [(0, 0, 512), (512, 1024, 256), (768, 1280, 64), (832, 1312, 32), (864, 1280, 32)], "g1")
                    + conv_tasks(s_wq, 512, w_in_d, D, [(0, 512, 512)], "g1")
                    + conv_tasks(s_wuq, 2048, w_uq_d, 512, pcs, "gq")
                    + conv_tasks(s_wukv, 2048, w_ukv_d, 256, [(0, 0, 2048)], "gkv"))
        BGW = 704
        bg_tasks = (conv_tasks(s_wgt, 2048, w_in_d, D, [(0, 1344, 2048)], "g1", BGW)
                    + conv_tasks(s_wfo, D, w_fo_d, 512, [(0, 0, D)], None, BGW)
                    + conv_tasks(s_wao, D, w_ao_d, D, [(0, 0, D)], None, BGW)
                    + conv_tasks(s_wo, D, w_o_d, D, [(0, 0, D)], None, BGW)
                    + conv_tasks(s_wg, DFF, w_g_d, D, [(0, 0, DFF)], "g2", BGW)
                    + conv_tasks(s_wu, DFF, w_u_d, D, [(0, 0, DFF)], "g2", BGW)
                    + conv_tasks(s_wd, D, w_d_d, DFF, [(0, 0, D)], None, BGW))
        with contextlib.ExitStack() as st:
            stage = Rot(P, [sbuf(st, nm("wst"), [128, 2048], F32) for _ in range(3)])
            obuf = Rot(P, [sbuf(st, nm("wob"), [128, 2048], BF16) for _ in range(3)])
            for tsk in fg_tasks:
                conv_chunk(stage, obuf, ("act", "dve"), "act", *tsk)
        P.barrier()
        if STOP == "W":
            P.emit()
            return nc

        for job in ("A", "B"):
            G = GEOM[job]
            L, n_full, nb = G["L"], G["n_full"], G["nb"]
            nst = n_full + 1
            with contextlib.ExitStack() as jst:
                ckvT = sbuf(jst, nm("ckvT"), [128, 2, L], BF16)
                kropeT = sbuf(jst, nm("kropeT"), [128, L], BF16)
                krzb = P.buf()
                P.op("dve", lambda e, t=kropeT: e.memset(t, 0.0), writes=[krzb])
                ckvb = [P.buf() for _ in range(nst)]
                krb = [P.buf() for _ in range(nst)]
                with contextlib.ExitStack() as abst:
                    AB = sbuf(abst, nm("AB"), [128, nb, 1024], BF16)
                    ABb = [[P.buf(), P.buf()] for _ in range(nb)]
                    if DEBUG:
                        P.op("pool", lambda e, AB=AB, nb=nb: e.memset(AB[:, nb - 1, :], 0.0), writes=[ABb[nb - 1]])
                    with contextlib.ExitStack() as st:
                        nt = NT(st, 4)
                        wk, wkb = load_w(st, s_wk, D, 896, "wk")
                        mm = Rot(P, [psum(st, nm("mm"), [128, 512]) for _ in range(4)], True)
                        abp = Rot(P, [psum(st, nm("abp"), [128, 512]) for _ in range(2)], True)
                        uTa = [sbuf(st, nm("uT"), [128, 4, 512], BF16) for _ in range(2)]
                        uTbs = [[P.buf() for _ in range(4)] for _ in range(2)]
                        uti = [0]
                        uTt2 = sbuf(st, nm("uTt"), [128, 128], BF16)
                        uTt = uTt2.rearrange("p (g n) -> p g n", g=4)
                        uTtb = [P.buf() for _ in range(4)]
                        P.op("dve", lambda e, t=uTt2: e.memset(t, 0.0), writes=[uTtb])
                        sqr = Rot(P, [sbuf(st, nm("sq"), [128, 2, 512], BF16) for _ in range(2)])
                        crr = Rot(P, [sbuf(st, nm("cr"), [128, 2, 512], F32) for _ in range(1)])
                        Rr = Rot(P, [sbuf(st, nm("R"), [128, 512], F32) for _ in range(1)])
                        rcr = Rot(P, [sbuf(st, nm("rc"), [64, 2, 512], F32) for _ in range(2)])
                        t12 = Rot(P, [sbuf(st, nm("t12"), [64, 2, 512], F32) for _ in range(1)])
                        def ktiles(s_):
                            r0_ = 512 * s_
                            if s_ == n_full:
                                return [(xk[job][r0_:r0_ + 16, :], 16)]
                            return [(xk[job][r0_ + 128 * t:r0_ + 128 * (t + 1), :], 128) for t in range(4)]

                        slist = [s_ for s_ in range(nst) if KLIM is None or s_ in KLIM]
                        pre = nt.prep(ktiles(slist[0]))
                        for si, s in enumerate(slist):
                            tail = s == n_full
                            r0 = 512 * s
                            hT, hTb, n, _ = nt.finish(pre)
                            if si + 1 < len(slist):
                                pre = nt.prep(ktiles(slist[si + 1]))
                            if tail:
                                u, ub = uTt, uTtb
                            else:
                                u, ub = uTa[uti[0] % 2], uTbs[uti[0] % 2]
                                uti[0] += 1
                            for g in range(4):
                                ps, psb = mm.next()
                                for c in range(8):
                                    MM(ps[:, 0:n], wk[:, c, g * 128:(g + 1) * 128], hT[:, c, 0:n], c == 0, c == 7,
                                       [hTb, wkb[c]], [psb])
                                EVAC(u[:, g, 0:n], ps[:, 0:n], [psb], [ub[g]])
                            sq, sqb = sqr.next()
                            cr, crb = crr.next()
                            for c2 in range(2):
                                ps, psb = mm.next()
                                for c in range(8):
                                    MM(ps[:, 0:n], wk[:, c, 512 + c2 * 128:512 + (c2 + 1) * 128], hT[:, c, 0:n],
                                       c == 0, c == 7, [hTb, wkb[c]], [psb])
                                ACT(sq[:, c2, 0:n], ps[:, 0:n], AF.Square, [psb], [sqb])
                                COPY("dve", cr[:, c2, 0:n], ps[:, 0:n], [psb], [crb])
                            rc, rcb = rcr.next()
                            P.dma("sp", rc[:, :, 0:n], ropek[job][:, :, r0:r0 + n].rearrange("a p n -> p a n"),
                                  writes=[rcb])
                            tt, ttb = t12.next()
                            for j in range(2):
                                ps, psb = mm.next()
                                for c in range(8):
                                    MM(ps[0:64, 0:n], wk[:, c, 768 + 64 * j:832 + 64 * j], hT[:, c, 0:n],
                                       c == 0, c == 7, [hTb, wkb[c]], [psb])
                                TT("dve", tt[:, j, 0:n], ps[0:64, 0:n], rc[:, j, 0:n], ALU.mult, [psb, rcb], [ttb])
                            TT(POOL_EW, kropeT[0:64, r0:r0 + n], tt[:, 0, 0:n], tt[:, 1, 0:n], ALU.add, [ttb, krzb], [krb[s]])
                            if tail:
                                blks = [(2 * n_full, 0, 9, 9)]
                            else:
                                blks = [(2 * s, 0, 256, 128), (2 * s + 1, 128, 384, 128)]
                            for (blk, f0, m0, m) in blks:
                                for hf, (rhs_f, rhs_m) in enumerate(((c128[:, 0:128], c128[:, 0:128]),
                                                                     (c128[:, 128:256], c128[:, 256:384]))):
                                    ab, abb = abp.next()
                                    for g in range(4):
                                        o_ap = ab[0:m, g * 128:(g + 1) * 128]
                                        MM(o_ap, u[:, g, f0:f0 + m], rhs_f, True, False, [ub[g], cb], [abb])
                                        MM(o_ap, u[:, g, m0:m0 + m], rhs_m, False, True, [ub[g], cb], [abb])
                                    EVAC(AB[0:m, blk, hf * 512:(hf + 1) * 512], ab[0:m, :], [abb], [ABb[blk][hf]])
                            ps, psb = mm.next()
                            for c2 in range(2):
                                MM(ps[:, 0:n], ones_bf, sq[:, c2, 0:n], c2 == 0, c2 == 1, [sqb, cb], [psb])
                            R, Rb = Rr.next()
                            ACT(R[:, 0:n], ps[:, 0:n], AF.Sqrt, [psb], [Rb], scale=1.0 / 256, bias=EPS)
                            RECIP(R[:, 0:n], R[:, 0:n], [Rb], [Rb])
                            for c2 in range(2):
                                TT("dve", ckvT[:, c2, r0:r0 + n], cr[:, c2, 0:n], R[:, 0:n], ALU.mult,
                                   [crb, Rb], [ckvb[s]])
                    P.barrier()
                    if STOP == "K" + job:
                        dA = nc.dram_tensor("dbg_AB", [128, nb, 1024], BF16, kind="ExternalOutput").ap()
                        dC = nc.dram_tensor("dbg_ckv", [128, 2, L], BF16, kind="ExternalOutput").ap()
                        dK = nc.dram_tensor("dbg_kr", [128, L], BF16, kind="ExternalOutput").ap()
                        P.dma("sp", dA, AB, final=True)
                        P.dma("sp", dC, ckvT, final=True)
                        P.dma("sp", dK, kropeT, final=True)
                        P.emit()
                        return nc
                    nfc = 4 if job == "A" else 2
                    Ns = 8 if job == "A" else 2
                    with contextlib.ExitStack() as st:
                        if job == "A":
                            FT0 = sbuf(st, nm("FT0"), [128, 4, 2048], BF16)
                            FT1 = sbuf(st, nm("FT1"), [128, 4, 2048], BF16)
                            FT0b, FT1b = P.buf(), P.buf()
                            fdst = lambda g, j: (FT0[:, g, j * 512:(j + 1) * 512], FT0b)
                            mdst = lambda g, j: (FT1[:, g, j * 512:(j + 1) * 512], FT1b)
                            sdst = lambda g: (FT1[:, g, 2048 - Ns:2048], FT1b)
                        else:
                            FT0 = sbuf(st, nm("FT0"), [128, 4, 2048], BF16)
                            FT0b = P.buf()
                            FT1b = P.buf()
                            fdst = lambda g, j: (FT0[:, g, j * 512:(j + 1) * 512], FT0b)
                            mdst = lambda g, j: (FT0[:, g, 1024 + j * 512:1024 + (j + 1) * 512], FT1b)
                            sdst = lambda g: (FT0[:, g, 2048 - Ns:2048], FT1b)
                        tabs = Rot(P, [sbuf(st, nm("tab"), [128, 4, 1024], BF16) for _ in range(3)])
                        tsm = sbuf(st, nm("tsm"), [128, nb, 2 * Ns], BF16)
                        tsmb = P.buf()
                        P.dma("sp", tsm, tabs_small[job].rearrange("b p n -> p b n"), writes=[tsmb])
                        p1R = Rot(P, [sbuf(st, nm("p1sb"), [128, 512], F32) for _ in range(2)])
                        fps = psum(st, nm("fps"), [128, 8, 512])
                        fpb = [P.buf("", True) for _ in range(8)]
                        for j in range(nfc):
                            for s4 in range(0, 2 * n_full + 1, 4):
                                tail = s4 == 2 * n_full
                                T, Tb = tabs.next()
                                if tail:
                                    P.dma("sp", T[:, 0, :], tab[job][j, 2 * n_full, :, :], writes=[Tb])
                                    blks = [(2 * n_full, 0, 9)]
                                else:
                                    P.dma("sp", T, tab[job][j, s4:s4 + 4, :, :].rearrange("b p n -> p b n"),
                                          writes=[Tb])
                                    blks = [(s4 + bi, bi, 128) for bi in range(4)]
                                for (blk, bi, m) in blks:
                                    for g in range(4):
                                        MM(fps[:, g, :], AB[0:m, blk, g * 128:(g + 1) * 128], T[0:m, bi, 0:512],
                                           blk == 0, blk == nb - 1, [ABb[blk][0], Tb], [fpb[g]])
                                        MM(fps[:, 4 + g, :], AB[0:m, blk, 512 + g * 128:512 + (g + 1) * 128],
                                           T[0:m, bi, 512:1024], blk == 0, blk == nb - 1, [ABb[blk][1], Tb],
                                           [fpb[4 + g]])
                            for g in range(4):
                                p1, p1b = p1R.next()
                                COPY("act", p1, fps[:, g, :], [fpb[g]], [p1b])
                                d_ap, d_b = fdst(g, j)
                                TT("dve", d_ap, fps[:, 4 + g, :], p1, ALU.add, [fpb[4 + g], p1b], [d_b])
                                d_ap, d_b = mdst(g, j)
                                TT("dve", d_ap, p1, fps[:, 4 + g, :], ALU.subtract, [fpb[4 + g], p1b], [d_b])
                        for g in range(4):
                            reg = fps[:, 0, g * Ns:(g + 1) * Ns]
                            for blk in range(nb):
                                m = 9 if blk == nb - 1 else 128
                                MM(reg, AB[0:m, blk, g * 128:(g + 1) * 128], tsm[0:m, blk, 0:Ns],
                                   blk == 0, False, [ABb[blk][0], tsmb], [fpb[0]])
                                MM(reg, AB[0:m, blk, 512 + g * 128:512 + (g + 1) * 128], tsm[0:m, blk, Ns:2 * Ns],
                                   False, blk == nb - 1, [ABb[blk][1], tsmb], [fpb[0]])
                        for g in range(4):
                            d_ap, d_b = sdst(g)
                            COPY("act", d_ap, fps[:, 0, g * Ns:(g + 1) * Ns], [fpb[0]], [d_b])
                        if job == "A":
                            P.dma("sp", FTs[0], FT0, reads=[FT0b])
                            P.dma("sp", FTs[1], FT1, reads=[FT1b])
                        else:
                            P.dma("sp", FTs[2], FT0, reads=[FT0b, FT1b])
                    P.barrier()
                    if STOP == "D" + job:
                        P.emit()
                        return nc
                for gi in range(NGROUPS[job]):
                    gg = GBASE[job] + gi
                    QQ = QN[job]
                    nqc = QQ // 512
                    with contextlib.ExitStack() as gst2:
                        cqT = sbuf(gst2, nm("cqT"), [128, 4, QQ], BF16)
                        cqb = [P.buf() for _ in range(nqc)]
                        with contextlib.ExitStack() as st:
                            nt = NT(st, 5, 512, 4)
                            wq, wqb = load_w(st, s_wq, D, 512, "wq")
                            mm = Rot(P, [psum(st, nm("mm"), [128, 512]) for _ in range(4)], True)
                            sqr = Rot(P, [sbuf(st, nm("sq"), [128, 4, 512], BF16) for _ in range(2)])
                            crr = Rot(P, [sbuf(st, nm("cr"), [128, 4, 512], F32) for _ in range(2)])
                            Rr = Rot(P, [sbuf(st, nm("R"), [128, 512], F32) for _ in range(2)])
                            def qtiles(qc_):
                                q0_ = gi * 2048 + qc_ * 512
                                return [(xq[job][q0_ + 128 * t:q0_ + 128 * (t + 1), :], 128) for t in range(4)]

                            pre = nt.prep(qtiles(0))
                            for qc in range(nqc):
                                hT, hTb, n, _ = nt.finish(pre)
                                if qc + 1 < nqc:
                                    pre = nt.prep(qtiles(qc + 1))
                                sq, sqb = sqr.next()
                                cr, crb = crr.next()
                                for c4 in range(4):
                                    ps, psb = mm.next()
                                    for c in range(8):
                                        MM(ps, wq[:, c, c4 * 128:(c4 + 1) * 128], hT[:, c, :], c == 0, c == 7,
                                           [hTb, wqb[c]], [psb])
                                    ACT(sq[:, c4, :], ps, AF.Square, [psb], [sqb])
                                    COPY("dve", cr[:, c4, :], ps, [psb], [crb])
                                ps, psb = mm.next()
                                for c4 in range(4):
                                    MM(ps, ones_bf, sq[:, c4, :], c4 == 0, c4 == 3, [sqb, cb], [psb])
                                R, Rb = Rr.next()
                                ACT(R, ps, AF.Sqrt, [psb], [Rb], scale=1.0 / 512, bias=EPS)
                                RECIP(R, R, [Rb], [Rb])
                                for c4 in range(4):
                                    TT("dve", cqT[:, c4, qc * 512:(qc + 1) * 512], cr[:, c4, :], R, ALU.mult,
                                       [crb, Rb], [cqb[qc]])
                        P.barrier()
                        if STOP == "Q" + job:
                            dQ = nc.dram_tensor("dbg_cq", [128, 4, QQ], BF16, kind="ExternalOutput").ap()
                            P.dma("sp", dQ, cqT, final=True)
                            P.emit()
                            return nc
                        with contextlib.ExitStack() as st:
                            wuq, wuqb = load_w(st, s_wuq, 512, 2048, "wuq")
                            wukv, wukvb = load_w(st, s_wukv, 256, 2048, "wukv")
                            aoR = Rot(P, [sbuf(st, nm("ao"), [128, 512], BF16) for _ in range(4)])
                            nkv = 2 if job == "A" else 1
                            nkt = 4 * n_full + 1
                            KhR = Rot(P, [sbuf(st, nm("Kh"), [128, L], BF16) for _ in range(nkv)])
                            VhR = Rot(P, [sbuf(st, nm("Vh"), [128, nkt, 128], BF16) for _ in range(nkv)])
                            rq = sbuf(st, nm("rq"), [64, 2, QQ], F32)
                            rqb = P.buf()
                            P.dma("sp", rq, ropeq[job][:, :, 0:QQ].rearrange("a p n -> p a n"),
                                  writes=[rqb])
                            SR = Rot(P, [psum(st, nm("S"), [128, 2, 512]) for _ in range(2)], True)
                            poR = Rot(P, [psum(st, nm("po"), [128, 512]) for _ in range(2)], True)
                            mm = Rot(P, [psum(st, nm("mm"), [128, 512]) for _ in range(2)], True)
                            PTR = Rot(P, [sbuf(st, nm("PT"), [128, 2, 512], BF16) for _ in range(4)])
                            qnR = Rot(P, [sbuf(st, nm("qn"), [128, 512], BF16) for _ in range(2)])
                            qrR = Rot(P, [sbuf(st, nm("qr"), [128, 512], BF16) for _ in range(2)])
                            for (qr_, qrb_) in qrR.items:
                                P.op("dve", lambda e, t=qr_: e.memset(t, 0.0), writes=[qrb_])
                            t12 = Rot(P, [sbuf(st, nm("t12"), [64, 2, 512], F32) for _ in range(2)])
                            accR = Rot(P, [sbuf(st, nm("acc"), [128, 2, 512], F32) for _ in range(2)])
                            recR = Rot(P, [sbuf(st, nm("rec"), [128, 512], F32) for _ in range(2)])

                            def kvproj(h):
                                Kh, Khb = KhR.next()
                                Vh, Vhb = VhR.next()
                                for s in range(nst):
                                    n = 16 if s == n_full else 512
                                    r0 = 512 * s
                                    ps, psb = mm.next()
                                    for c in range(2):
                                        MM(ps[:, 0:n], wukv[:, c, h * 256:h * 256 + 128], ckvT[:, c, r0:r0 + n],
                                           c == 0, c == 1, [ckvb[s], wukvb[c]], [psb])
                                    EVAC(Kh[:, r0:r0 + n], ps[:, 0:n], [psb], [Khb], "act")
                                    ps, psb = mm.next()
                                    if s == n_full:
                                        for c in range(2):
                                            MM(ps[0:16, 0:128], ckvT[:, c, r0:r0 + 16],
                                               wukv[:, c, h * 256 + 128:h * 256 + 256], c == 0, c == 1,
                                               [ckvb[s], wukvb[c]], [psb])
                                        EVAC(Vh[0:16, 4 * n_full, :], ps[0:16, 0:128], [psb], [Vhb], "dve" if nkv == 1 else "act")
                                    else:
                                        for t in range(4):
                                            for c in range(2):
                                                MM(ps[:, t * 128:(t + 1) * 128],
                                                   ckvT[:, c, r0 + t * 128:r0 + (t + 1) * 128],
                                                   wukv[:, c, h * 256 + 128:h * 256 + 256], c == 0, c == 1,
                                                   [ckvb[s], wukvb[c]], [psb])
                                        EVAC(Vh[:, 4 * s:4 * s + 4, :], ps.rearrange("p (t d) -> p t d", t=4),
                                             [psb], [Vhb], "dve" if nkv == 1 else "act")
                                return Kh, Khb, Vh, Vhb

                            def qproj(h, qc):
                                q0 = qc * 512
                                ps, psb = mm.next()
                                for c in range(4):
                                    MM(ps, wuq[:, c, h * 256:h * 256 + 128], cqT[:, c, q0:q0 + 512],
                                       c == 0, c == 3, [cqb[qc], wuqb[c]], [psb])
                                qn, qnb = qnR.next()
                                COPY("act", qn, ps, [psb], [qnb])
                                tt, ttb = t12.next()
                                for j in range(2):
                                    ps, psb = mm.next()
                                    for c in range(4):
                                        MM(ps[0:64, :], wuq[:, c, h * 256 + 128 + 64 * j:h * 256 + 192 + 64 * j],
                                           cqT[:, c, q0:q0 + 512], c == 0, c == 3, [cqb[qc], wuqb[c]], [psb])
                                    TT("dve", tt[:, j, :], ps[0:64, :], rq[:, j, q0:q0 + 512], ALU.mult,
                                       [psb, rqb], [ttb])
                                qr, qrb = qrR.next()
                                TT(POOL_EW, qr[0:64, :], tt[:, 0, :], tt[:, 1, :], ALU.add, [ttb], [qrb])
                                return qn, qnb, qr, qrb

                            nfull_kb = 4 * n_full
                            npairs = nfull_kb // 2 + 1

                            def emit_S(i, Kh, Khb, qn, qnb, qr, qrb):
                                S, Sb = SR.next()
                                if i == npairs - 1:
                                    k0 = nfull_kb * 128
                                    MM(S[0:16, 0, :], Kh[:, k0:k0 + 16], qn, True, False, [Khb, qnb], [Sb])
                                    MM(S[0:16, 0, :], kropeT[:, k0:k0 + 16], qr, False, True, [krb[n_full], qrb], [Sb])
                                else:
                                    for j in range(2):
                                        k0 = (2 * i + j) * 128
                                        MM(S[:, j, :], Kh[:, k0:k0 + 128], qn, True, False, [Khb, qnb], [Sb])
                                        MM(S[:, j, :], kropeT[:, k0:k0 + 128], qr, False, True, [krb[k0 // 512], qrb], [Sb])
                                return S, Sb

                            def emit_exp(i, S, Sb, acc, accb):
                                PT, PTb = PTR.next()
                                if i == npairs - 1:
                                    ACT(PT[0:16, 0, :], S[0:16, 0, :], AF.Exp, [Sb], [PTb], scale=SCALE)
                                    TT("dve", acc[0:16, 0, :], acc[0:16, 0, :], PT[0:16, 0, :], ALU.add, [accb, PTb], [accb])
                                else:
                                    ACT(PT, S, AF.Exp, [Sb], [PTb], scale=SCALE)
                                    if i == 0:
                                        COPY("dve", acc, PT, [PTb], [accb])
                                    else:
                                        TT("dve", acc, acc, PT, ALU.add, [accb, PTb], [accb])
                                return PT, PTb

                            def emit_pv(i, PT, PTb, Vh, Vhb, po, pob):
                                kp = 2 * i
                                if i == npairs - 1:
                                    MM(po, Vh[0:16, kp, :], PT[0:16, 0, :], False, True, [Vhb, PTb], [pob])
                                else:
                                    for j in range(2):
                                        MM(po, Vh[:, kp + j, :], PT[:, j, :], (kp + j) == 0, False, [Vhb, PTb], [pob])

                            def finalize(h, qc, po, pob, acc, accb):
                                q0 = qc * 512
                                ps, psb = mm.next()
                                MM(ps, ones32, acc[:, 0, :], True, False, [accb, cb], [psb])
                                MM(ps, ones32, acc[:, 1, :], False, True, [accb, cb], [psb])
                                rec, recb = recR.next()
                                RECIP(rec, ps, [psb], [recb])
                                ao, aob = aoR.next()
                                TT("dve", ao, po, rec, ALU.mult, [pob, recb], [aob])
                                ggq = GBASE[job] + qc // 4
                                P.dma("sp", ATs[ggq, :, h, (qc % 4) * 512:(qc % 4 + 1) * 512], ao, reads=[aob])

                            if job == "A" and bg_tasks:
                                bstage = Rot(P, [sbuf(st, nm("bst"), [128, BGW], F32) for _ in range(4)])
                                bobuf = Rot(P, [sbuf(st, nm("bob"), [128, BGW], BF16) for _ in range(4)])

                            its = [(h, qc) for h in range(8) for qc in range(nqc)]
                            kv = {0: kvproj(0)}
                            q_next = qproj(0, 0)
                            pending = None
                            for idx, (h, qc) in enumerate(its):
                                qn, qnb, qr, qrb = q_next
                                Kh, Khb, Vh, Vhb = kv[h]
                                acc, accb = accR.next()
                                po, pob = poR.next()
                                nxt = its[idx + 1] if idx + 1 < len(its) else None
                                S_next = emit_S(0, Kh, Khb, qn, qnb, qr, qrb)
                                for i in range(npairs):
                                    S, Sb = S_next
                                    if i + 1 < npairs:
                                        S_next = emit_S(i + 1, Kh, Khb, qn, qnb, qr, qrb)
                                    if i == 1 and pending is not None:
                                        finalize(*pending)
                                        pending = None
                                    if i == 3 and nxt is not None:
                                        if nxt[0] != h and nkv == 2:
                                            kv[nxt[0]] = kvproj(nxt[0])
                                        q_next = qproj(nxt[0], nxt[1])
                                    PTcur = emit_exp(i, S, Sb, acc, accb)
                                    if i >= 1:
                                        emit_pv(i - 1, PTprev[0], PTprev[1], Vh, Vhb, po, pob)
                                    PTprev = PTcur
                                    if job == "A" and bg_tasks and i in (5, 9, 13):
                                        conv_chunk(bstage, bobuf, ("act",), "sp", *bg_tasks.pop(0))
                                emit_pv(npairs - 1, PTprev[0], PTprev[1], Vh, Vhb, po, pob)
                                pending = (h, qc, po, pob, acc, accb)
                                if nxt is not None and nxt[0] != h and nkv == 1:
                                    kv[nxt[0]] = kvproj(nxt[0])
                            finalize(*pending)
                        P.barrier()
                        if STOP == "At" + job:
                            P.emit()
                            return nc

        if bg_tasks:
            with contextlib.ExitStack() as st:
                stage = Rot(P, [sbuf(st, nm("wst"), [128, BGW], F32) for _ in range(3)])
                obuf = Rot(P, [sbuf(st, nm("wob"), [128, BGW], BF16) for _ in range(3)])
                while bg_tasks:
                    conv_chunk(stage, obuf, ("act", "dve"), "act", *bg_tasks.pop(0))
            P.barrier()
        with contextlib.ExitStack() as st:
            nt = NT(st, 8)
            wgt, wgtb = load_w(st, s_wgt, D, 2048, "wgt", 2)
            wfo, wfob = load_w(st, s_wfo, 512, D, "wfo")
            wao, waob = load_w(st, s_wao, D, D, "wao")
            wo, wob = load_w(st, s_wo, D, D, "wo")
            mm = Rot(P, [psum(st, nm("mm"), [128, 512]) for _ in range(6)], True)
            FTc = Rot(P, [sbuf(st, nm("FTc"), [128, 4, 512], BF16) for _ in range(2)])
            ATc = Rot(P, [sbuf(st, nm("ATc"), [128, 8, 512], BF16) for _ in range(2)])
            sgR = Rot(P, [sbuf(st, nm("sg"), [128, 2, 512], BF16) for _ in range(2)])
            tAB = Rot(P, [sbuf(st, nm("tAB"), [128, 2, 512], F32) for _ in range(2)])
            mTR = Rot(P, [sbuf(st, nm("mT"), [128, 8, 512], BF16) for _ in range(2)])
            x1R = Rot(P, [sbuf(st, nm("x1"), [128, D], F32) for _ in range(5)])
            pend = []
            def t1tiles(ci_):
                gg_, qc_ = ci_ // 4, ci_ % 4
                job_ = "A" if gg_ < 2 else "B"
                gi_ = gg_ if gg_ < 2 else 0
                q0_ = gi_ * 2048 + qc_ * 512
                return [(xq[job_][q0_ + 128 * t:q0_ + 128 * (t + 1), :], 128) for t in range(4)]

            pre = nt.prep(t1tiles(0))
            for gg in range(3):
                for qc in range(4):
                    hT, hTb, n, xl = nt.finish(pre)
                    if gg * 4 + qc + 1 < 12:
                        pre = nt.prep(t1tiles(gg * 4 + qc + 1))
                    for (d_, s_, b_) in pend:
                        P.dma("sp", d_, s_, reads=[b_])
                    pend = []
                    ft, ftb = FTc.next()
                    P.dma("sp", ft, FTs[gg, :, :, qc * 512:(qc + 1) * 512], writes=[ftb])
                    at, atb = ATc.next()
                    P.dma("sp", at, ATs[gg, :, :, qc * 512:(qc + 1) * 512], writes=[atb])
                    mT, mTb = mTR.next()
                    for j in range(8):
                        sg, sgb = sgR.next()
                        for k in range(2):
                            ps, psb = mm.next()
                            for c in range(8):
                                MM(ps, wgt[:, c, k * 1024 + j * 128:k * 1024 + (j + 1) * 128], hT[:, c, :],
                                   c == 0, c == 7, [hTb, wgtb[c]], [psb])
                            ACT(sg[:, k, :], ps, AF.Sigmoid, [psb], [sgb])
                        tab_, tabb = tAB.next()
                        ps, psb = mm.next()
                        for c in range(4):
                            MM(ps, wfo[:, c, j * 128:(j + 1) * 128], ft[:, c, :], c == 0, c == 3, [ftb, wfob[c]], [psb])
                        TT("dve", tab_[:, 0, :], ps, sg[:, 0, :], ALU.mult, [psb, sgb], [tabb])
                        ps, psb = mm.next()
                        for c in range(8):
                            MM(ps, wao[:, c, j * 128:(j + 1) * 128], at[:, c, :], c == 0, c == 7, [atb, waob[c]], [psb])
                        TT("dve", tab_[:, 1, :], ps, sg[:, 1, :], ALU.mult, [psb, sgb], [tabb])
                        TT(POOL_EW, mT[:, j, :], tab_[:, 0, :], tab_[:, 1, :], ALU.add, [tabb], [mTb])
                    for t in range(4):
                        x1, x1b = x1R.next()
                        x, xb, _r = xl[t]
                        for hh in range(2):
                            ps, psb = mm.next()
                            for c in range(8):
                                MM(ps, mT[:, c, t * 128:(t + 1) * 128], wo[:, c, hh * 512:(hh + 1) * 512],
                                   c == 0, c == 7, [mTb, wob[c]], [psb])
                            TT("dve", x1[:, hh * 512:(hh + 1) * 512], ps, x[:, hh * 512:(hh + 1) * 512], ALU.add,
                               [psb, xb], [x1b])
                        row = gg * 2048 + qc * 512 + t * 128
                        pend.append((x1s[row:row + 128, :], x1, x1b))
            for (d_, s_, b_) in pend:
                P.dma("sp", d_, s_, reads=[b_])
        P.barrier()
        if STOP == "T1":
            P.emit()
            return nc

        TC = 256
        with contextlib.ExitStack() as st:
            nt = NT(st, 4, TC)
            wg, wgb = load_w(st, s_wg, D, DFF, "wg", 2)
            wu, wub = load_w(st, s_wu, D, DFF, "wu", 2)
            wd, wdb = load_w(st, s_wd, DFF, D, "wd", 2)
            gfin = sbuf(st, "gfin_sb", [128, D], F32)
            gfb = P.buf()
            P.dma("sp", gfin, gfin_d, writes=[gfb])
            mm = Rot(P, [psum(st, nm("mm"), [128, 512]) for _ in range(6)], True)
            aTR = Rot(P, [sbuf(st, nm("aT"), [128, 22, TC], BF16)])
            slR = Rot(P, [sbuf(st, nm("sl"), [128, TC], F32) for _ in range(2)])
            x2R = Rot(P, [sbuf(st, nm("x2"), [128, D], F32) for _ in range(4)])
            pend = []
            stR = Rot(P, [sbuf(st, nm("st2"), [128, 2], F32) for _ in range(2)])
            junk2 = sbuf(st, nm("junk2"), [128, D], BF16)
            junk2b = P.buf()

            def STT(out, in0, sc, in1, reads, writes):
                P.op("dve", lambda e: e.scalar_tensor_tensor(out=out, in0=in0, scalar=sc, in1=in1,
                                                             op0=ALU.mult, op1=ALU.mult), reads, writes)

            nt2 = TC // 128
            def t2tiles(ci_):
                return [(x1s[ci_ * TC + 128 * t:ci_ * TC + 128 * (t + 1), :], 128) for t in range(nt2)]

            nch2 = 6144 // TC
            pre = nt.prep(t2tiles(0))
            for ci in range(nch2):
                row0 = ci * TC
                hT, hTb, n, xl = nt.finish(pre)
                if ci + 1 < nch2:
                    pre = nt.prep(t2tiles(ci + 1))
                for (d_, s_, b_) in pend:
                    P.dma("sp", d_, s_, reads=[b_], final=True)
                pend = []
                aT, aTb = aTR.next()
                for j in range(22):
                    pg, pgb = mm.next()
                    for c in range(8):
                        MM(pg[:, 0:TC], wg[:, c, j * 128:(j + 1) * 128], hT[:, c, :], c == 0, c == 7, [hTb, wgb[c]], [pgb])
                    pu, pub = mm.next()
                    for c in range(8):
                        MM(pu[:, 0:TC], wu[:, c, j * 128:(j + 1) * 128], hT[:, c, :], c == 0, c == 7, [hTb, wub[c]], [pub])
                    sl, slb = slR.next()
                    ACT(sl, pg[:, 0:TC], AF.Silu, [pgb], [slb])
                    TT("dve", aT[:, j, :], sl, pu[:, 0:TC], ALU.mult, [slb, pub], [aTb])
                for t in range(nt2):
                    x2, x2b = x2R.next()
                    x, xb, _r = xl[t]
                    for hh in range(2):
                        ps, psb = mm.next()
                        for j in range(22):
                            MM(ps, aT[:, j, t * 128:(t + 1) * 128], wd[:, j, hh * 512:(hh + 1) * 512],
                               j == 0, j == 21, [aTb, wdb[j]], [psb])
                        TT("dve", x2[:, hh * 512:(hh + 1) * 512], ps, x[:, hh * 512:(hh + 1) * 512], ALU.add,
                           [psb, xb], [x2b])
                    s2, s2b = stR.next()
                    ACT(junk2, x2, AF.Square, [x2b], [junk2b, s2b], accum_out=s2[:, 0:1])
                    ACT(s2[:, 1:2], s2[:, 0:1], AF.Sqrt, [s2b], [s2b], scale=1.0 / D, bias=EPS)
                    RECIP(s2[:, 1:2], s2[:, 1:2], [s2b], [s2b])
                    STT(x2, x2, s2[:, 1:2], gfin, [x2b, s2b, gfb], [x2b])
                    r_ = row0 + t * 128
                    if r_ < 4096:
                        dst = yA[r_:r_ + 128, :]
                    else:
                        dst = yB[r_ - 4096:r_ - 4096 + 128, :]
                    pend.append((dst, x2, x2b))
            for (d_, s_, b_) in pend:
                P.dma("sp", d_, s_, reads=[b_], final=True)
        P.emit()
    return nc


def _rope_tab(pos):
    inv = (1.0 / (10000.0 ** (np.arange(0, 64, 2, dtype=np.float32) / np.float32(64)))).astype(np.float32)
    ang = pos.astype(np.float32)[:, None] * inv[None, :]
    c = np.cos(ang).astype(np.float32).T
    s = np.sin(ang).astype(np.float32).T
    cc = np.concatenate([c, c], 0)
    ss = np.concatenate([-s, s], 0)
    return np.ascontiguousarray(np.stack([cc, ss], 0))


def _dft_tab(G, qpos, chunk=512):
    L = G["L"]
    nq = len(qpos)
    nqc = max(1, nq // chunk)
    w = nq // nqc
    out = np.zeros((nqc, G["nb"], 128, 2 * w), dtype=ml_dtypes.bfloat16)
    sc = 1.0 / np.sqrt(L)
    q = qpos.astype(np.int64)
    for b, sp in enumerate(G["sblocks"]):
        m = len(sp)
        prod = (sp.astype(np.int64)[:, None] * q[None, :]) % L
        ang = prod.astype(np.float64) * (2.0 * np.pi / L)
        cs = (np.cos(ang) * sc).reshape(m, nqc, w)
        sn = (-np.sin(ang) * sc).reshape(m, nqc, w)
        out[:, b, 0:m, 0:w] = cs.transpose(1, 0, 2).astype(ml_dtypes.bfloat16)
        out[:, b, 0:m, w:2 * w] = sn.transpose(1, 0, 2).astype(ml_dtypes.bfloat16)
    return out


def _qorders():
    fA = np.arange(16, LA // 2)
    sA = np.array([LA // 2] + list(range(LA - 15, LA)))
    qA = np.concatenate([fA, sA[0:8], LA - fA, sA[8:16]])
    assert len(qA) == 4096 and len(set(qA.tolist())) == 4096 and qA.min() == 16 and qA.max() == LA - 1
    fB = np.arange(16, LB // 2)
    sB = np.array([LB // 2] + list(range(LB - 15, LB)))
    qB = []
    for qt in range(4):
        f = fB[1022 * qt:1022 * (qt + 1)]
        sgl = sB[4 * qt:4 * qt + 4]
        qB.append(np.concatenate([f, sgl[0:2], LB - f, sgl[2:4]]))
    allB = np.concatenate(qB)
    assert len(allB) == 8192 and len(set(allB.tolist())) == 8192 and allB.min() == 16 and allB.max() == LB - 1
    return qA, qB


_CACHE = {}


def _consts():
    if "c" in _CACHE:
        return _CACHE["c"]
    k = np.arange(128)
    ang = 2.0 * np.pi * ((k[:, None] * k[None, :]) % 128) / 128.0
    sc = 1.0 / np.sqrt(128.0)
    c128 = np.concatenate([np.cos(ang) * sc, np.sin(ang) * sc, -np.sin(ang) * sc], 1).astype(ml_dtypes.bfloat16)
    GA, GB = GEOM["A"], GEOM["B"]
    qA, qB = _qorders()
    tabA = _dft_tab(GA, qA[0:2048])
    tabAs = np.ascontiguousarray(_dft_tab(GA, qA[4088:4096], chunk=8)[0])
    tabB = [_dft_tab(GB, qB[qt][0:1024]) for qt in range(4)]
    tabBs = [np.ascontiguousarray(_dft_tab(GB, qB[qt][2046:2048], chunk=2)[0]) for qt in range(4)]
    ropekA = _rope_tab(GA["korder"])
    ropekB = _rope_tab(GB["korder"])
    ropeqA = _rope_tab(qA)
    ropeqB = [_rope_tab(qB[qt]) for qt in range(4)]
    _CACHE["c"] = dict(c128=c128, tabA=tabA, tabB=tabB, tabAs=tabAs, tabBs=tabBs, ropekA=ropekA, ropekB=ropekB,
                       ropeqA=ropeqA, ropeqB=ropeqB, qA=qA, qB=qB)
    return _CACHE["c"]


def kernel(x_prompt, x_sample, meta_tokens, norm1_g, w_in, q_norm_g, kv_norm_g, w_uq, w_ukv,
           w_fourier_out, w_attn_out, w_o, norm2_g, w_ffn_gate, w_ffn_up, w_ffn_down, final_norm_g):
    f = lambda a: np.ascontiguousarray(np.asarray(a, dtype=np.float32))
    x_prompt, x_sample, meta = f(x_prompt), f(x_sample), f(meta_tokens)
    C = _consts()
    gl = lambda g: np.ascontiguousarray(f(g).reshape(-1, 128).T)
    common = {
        "c128": C["c128"], "g1": gl(norm1_g[0]), "gq": gl(q_norm_g[0]), "gkv": gl(kv_norm_g[0]), "g2": gl(norm2_g[0]),
        "gfin": np.ascontiguousarray(np.broadcast_to(f(final_norm_g)[None, :], (128, D))),
        "w_in": f(w_in[0]), "w_uq": f(w_uq[0]), "w_ukv": f(w_ukv[0]), "w_fo": f(w_fourier_out[0]),
        "w_ao": f(w_attn_out[0]), "w_o": f(w_o[0]), "w_g": f(w_ffn_gate[0]), "w_u": f(w_ffn_up[0]),
        "w_d": f(w_ffn_down[0]), "tabA": C["tabA"], "tabAs": C["tabAs"], "ropekA": C["ropekA"], "ropeqA": C["ropeqA"],
        "ropekB": C["ropekB"],
    }
    koA, koB = GEOM["A"]["korder"], GEOM["B"]["korder"]
    qA, qB = C["qA"], C["qB"]
    xkB = []
    for s in range(2):
        full = np.concatenate([meta, x_sample[s]], 0)
        xkB.append(np.ascontiguousarray(full[koB]))
    in_maps = []
    for c in range(8):
        fullA = np.concatenate([meta, x_prompt[c]], 0)
        s, qt = c // 4, c % 4
        m = dict(common)
        m["xkA"] = np.ascontiguousarray(fullA[koA])
        m["xqA"] = np.ascontiguousarray(x_prompt[c][qA - 16])
        m["xkB"] = xkB[s]
        m["xqB"] = np.ascontiguousarray(x_sample[s][qB[qt] - 16])
        m["tabB"] = C["tabB"][qt]
        m["tabBs"] = C["tabBs"][qt]
        if SMALL_TABS:
            m["tabA"] = C["tabA"][0:1, 0:1]
            m["tabB"] = C["tabB"][qt][0:1, 0:1]
        m["ropeqB"] = C["ropeqB"][qt]
        in_maps.append(m)
    if "nc" not in _CACHE:
        _CACHE["nc"] = build_program()
    res = run_bass_kernel_spmd(_CACHE["nc"], in_maps, core_ids=list(range(8)))
    _CACHE["last"] = res
    y_prompt = np.empty((8, 4096, D), np.float32)
    y_sample = np.empty((2, 8192, D), np.float32)
    for c in range(8):
        s, qt = c // 4, c % 4
        y_prompt[c][qA - 16] = np.asarray(res.results[c]["yA"], dtype=np.float32)
        y_sample[s][qB[qt] - 16] = np.asarray(res.results[c]["yB"], dtype=np.float32)
    return (y_prompt, y_sample)
```

```python
import contextlib
import numpy as np
import ml_dtypes
import concourse.bass as bass
import concourse.mybir as mybir
from concourse.bass_utils import run_bass_kernel_spmd

F32 = mybir.dt.float32
BF16 = mybir.dt.bfloat16
AF = mybir.ActivationFunctionType
ALU = mybir.AluOpType

D = 1024
NM = 16
DFF = 2816
EPS = 1e-6
SCALE = 192 ** -0.5
LA, LB = 4112, 8208
DEBUG = False
STOP = None
KLIM = None
SMALL_TABS = False
POOL_EW = "dve"
POOL_AT = "dve"


class Buf:
    __slots__ = ("name", "writer", "readers", "excl")

    def __init__(self, name="", excl=False):
        self.name = name
        self.writer = None
        self.readers = []
        self.excl = excl


class Op:
    __slots__ = ("eng", "fn", "deps", "need_inc", "val", "is_dma", "dsem", "dval")

    def __init__(self, eng, fn, is_dma=False):
        self.eng = eng
        self.fn = fn
        self.deps = []
        self.need_inc = False
        self.val = None
        self.is_dma = is_dma
        self.dsem = None
        self.dval = None


ENGS = ("pe", "act", "dve", "pool", "sp")


class Prog:
    def __init__(self, nc, n_dma_sems=48):
        self.nc = nc
        self.ops = {e: [] for e in ENGS}
        self.n_dma_sems = n_dma_sems
        self.dma_last = [None] * n_dma_sems
        self.dma_uses = [0] * n_dma_sems
        self.dma_n = {"sp": 0, "pool": 0, "act": 0}
        self.dma_rng = {"sp": (0, 24), "act": (24, 16), "pool": (40, n_dma_sems - 40)}
        self.all_bufs = []
        self.final_dmas = []

    def buf(self, name="", excl=False):
        b = Buf(name, excl)
        self.all_bufs.append(b)
        return b

    def _add_dep(self, op, prod):
        if prod is None or prod is op:
            return
        if (not prod.is_dma) and prod.eng == "pe" and op.eng == "pe" and not op.is_dma:
            return
        if not prod.is_dma:
            prod.need_inc = True
        op.deps.append(prod)

    @staticmethod
    def _flat(x):
        out = []
        for b in x:
            if isinstance(b, (list, tuple)):
                out.extend(Prog._flat(b))
            else:
                out.append(b)
        return out

    def op(self, eng, fn, reads=(), writes=(), dma=False):
        o = Op(eng, fn, is_dma=dma)
        reads = self._flat(reads)
        writes = self._flat(writes)
        xr = [b for b in reads if b.excl and b not in writes]
        if xr:
            reads = [b for b in reads if not b.excl]
            writes = list(writes) + xr
        for b in reads:
            self._add_dep(o, b.writer)
        for b in writes:
            self._add_dep(o, b.writer)
            for r in b.readers:
                self._add_dep(o, r)
        if dma:
            base, cnt = self.dma_rng[eng]
            k = base + self.dma_n[eng] % cnt
            self.dma_n[eng] += 1
            self._add_dep(o, self.dma_last[k])
            self.dma_last[k] = o
            self.dma_uses[k] += 1
            o.dsem = k
            o.dval = 16 * self.dma_uses[k]
        for b in reads:
            b.readers.append(o)
        for b in writes:
            b.writer = o
            b.readers = []
        self.ops[eng].append(o)
        return o

    def dma(self, eng, out_ap, in_ap, reads=(), writes=(), final=False):
        o = self.op(eng, lambda e: e.dma_start(out=out_ap, in_=in_ap), reads, writes, dma=True)
        if final:
            self.final_dmas.append(o)
        return o

    def barrier(self):
        lasts = []
        for e in ENGS:
            for o in reversed(self.ops[e]):
                if not o.is_dma and o.fn is not None:
                    lasts.append(o)
                    break
        for k in range(self.n_dma_sems):
            if self.dma_last[k] is not None:
                lasts.append(self.dma_last[k])
        for e in ENGS:
            o = Op(e, None)
            for p in lasts:
                if (not p.is_dma) and p.eng == e and e == "pe":
                    continue
                if not p.is_dma:
                    p.need_inc = True
                o.deps.append(p)
            self.ops[e].append(o)
        for b in self.all_bufs:
            b.writer = None
            b.readers = []

    def emit(self):
        nc = self.nc
        fin = Op("sp", None)
        for o in self.final_dmas:
            fin.deps.append(o)
        for k in range(self.n_dma_sems):
            if self.dma_last[k] is not None:
                fin.deps.append(self.dma_last[k])
        self.ops["sp"].append(fin)
        for e in ENGS:
            c = 0
            for o in self.ops[e]:
                if o.is_dma:
                    continue
                if o.need_inc:
                    c += 1
                o.val = c
        with contextlib.ExitStack() as st:
            esem = {e: st.enter_context(nc.semaphore("s_" + e)) for e in ENGS}
            dsem = [st.enter_context(nc.semaphore("d%d" % k)) for k in range(self.n_dma_sems)]
            block = st.enter_context(nc.Block())
            ops = self.ops

            def run(e, eng):
                waited = {}
                for o in ops[e]:
                    for p in o.deps:
                        if p.is_dma:
                            key, sem, val = ("d", p.dsem), dsem[p.dsem], p.dval
                        else:
                            key, sem, val = ("e", p.eng), esem[p.eng], p.val
                        if waited.get(key, 0) >= val:
                            continue
                        waited[key] = val
                        eng.wait_ge(sem, val)
                    if o.fn is None:
                        continue
                    ins = o.fn(eng)
                    if o.is_dma:
                        ins.then_inc(dsem[o.dsem], 16)
                    elif o.need_inc:
                        ins.then_inc(esem[e], 1)

            @block.tensor
            def _(eng):
                run("pe", eng)

            @block.scalar
            def _(eng):
                run("act", eng)

            @block.vector
            def _(eng):
                run("dve", eng)

            @block.gpsimd
            def _(eng):
                run("pool", eng)

            @block.sync
            def _(eng):
                run("sp", eng)


class Rot:
    def __init__(self, P, aps, excl=False):
        self.items = [(ap, P.buf("", excl)) for ap in aps]
        self.i = 0

    def next(self):
        it = self.items[self.i % len(self.items)]
        self.i += 1
        return it


def job_geom(L):
    half = L // 2
    Fp = np.arange(1, half)
    Mp = L - Fp
    n_full = len(Fp) // 256
    rem = len(Fp) - 256 * n_full
    assert rem == 7
    korder = []
    sblocks = []
    for s in range(n_full):
        korder += [Fp[256 * s:256 * s + 256], Mp[256 * s:256 * s + 256]]
        sblocks += [Fp[256 * s:256 * s + 128], Fp[256 * s + 128:256 * s + 256]]
    korder += [Fp[-rem:], np.array([0, half]), Mp[-rem:]]
    sblocks += [np.concatenate([Fp[-rem:], np.array([0, half])])]
    korder = np.concatenate(korder)
    assert len(korder) == L and len(set(korder.tolist())) == L
    return dict(L=L, n_full=n_full, korder=korder, sblocks=sblocks, nb=2 * n_full + 1)


GEOM = {"A": job_geom(LA), "B": job_geom(LB)}
NGROUPS = {"A": 1, "B": 1}
QN = {"A": 4096, "B": 2048}
GBASE = {"A": 0, "B": 2}


def build_program():
    nc = bass.Bass("TRN2", target_bir_lowering=False)
    P = Prog(nc)

    def din(name, shape, dt=F32):
        return nc.dram_tensor(name, shape, dt, kind="ExternalInput").ap()

    def dscr(name, shape, dt=BF16):
        kind = "ExternalOutput" if DEBUG else "Internal"
        return nc.dram_tensor(name, shape, dt, kind=kind).ap()

    xk = {"A": din("xkA", [LA, D]), "B": din("xkB", [LB, D])}
    xq = {"A": din("xqA", [4096, D]), "B": din("xqB", [2048, D])}
    ropek = {"A": din("ropekA", [2, 64, LA]), "B": din("ropekB", [2, 64, LB])}
    ropeq = {"A": din("ropeqA", [2, 64, 4096]), "B": din("ropeqB", [2, 64, 2048])}
    if SMALL_TABS:
        tab = {"A": din("tabA", [1, 1, 128, 1024], BF16), "B": din("tabB", [1, 1, 128, 1024], BF16)}
    else:
        tab = {"A": din("tabA", [4, GEOM["A"]["nb"], 128, 1024], BF16),
               "B": din("tabB", [2, GEOM["B"]["nb"], 128, 1024], BF16)}
    tabs_small = {"A": din("tabAs", [GEOM["A"]["nb"], 128, 16], BF16),
                  "B": din("tabBs", [GEOM["B"]["nb"], 128, 4], BF16)}
    c128_d = din("c128", [128, 384], BF16)
    g1_d = din("g1", [128, 8])
    gq_d = din("gq", [128, 4])
    gkv_d = din("gkv", [128, 2])
    g2_d = din("g2", [128, 8])
    gfin_d = din("gfin", [128, D])
    w_in_d = din("w_in", [D, 3392])
    w_uq_d = din("w_uq", [512, 1536])
    w_ukv_d = din("w_ukv", [256, 2048])
    w_fo_d = din("w_fo", [512, D])
    w_ao_d = din("w_ao", [D, D])
    w_o_d = din("w_o", [D, D])
    w_g_d = din("w_g", [D, DFF])
    w_u_d = din("w_u", [D, DFF])
    w_d_d = din("w_d", [DFF, D])
    yA = nc.dram_tensor("yA", [4096, D], F32, kind="ExternalOutput").ap()
    yB = nc.dram_tensor("yB", [2048, D], F32, kind="ExternalOutput").ap()

    s_wk = dscr("s_wk", [D, 896])
    s_wq = dscr("s_wq", [D, 512])
    s_wgt = dscr("s_wgt", [D, 2048])
    s_wuq = dscr("s_wuq", [512, 2048])
    s_wukv = dscr("s_wukv", [256, 2048])
    s_wfo = dscr("s_wfo", [512, D])
    s_wao = dscr("s_wao", [D, D])
    s_wo = dscr("s_wo", [D, D])
    s_wg = dscr("s_wg", [D, DFF])
    s_wu = dscr("s_wu", [D, DFF])
    s_wd = dscr("s_wd", [DFF, D])
    FTs = dscr("FTs", [3, 128, 4, 2048])
    ATs = dscr("ATs", [3, 128, 8, 2048])
    x1s = dscr("x1s", [6144, D], F32)
    scr_bufs = {}

    def sb(name):
        if name not in scr_bufs:
            scr_bufs[name] = P.buf(name)
        return scr_bufs[name]

    def MM(out, lhsT, rhs, start, stop, reads, writes):
        P.op("pe", lambda e: e.matmul(out, lhsT=lhsT, rhs=rhs, start=start, stop=stop), reads, writes)

    def TR(out, in_, ident, reads, writes):
        P.op("pe", lambda e: e.transpose(out=out, in_=in_, identity=ident), reads, writes)

    def ACT(out, in_, func, reads, writes, scale=1.0, bias=0.0, accum_out=None):
        if accum_out is None:
            P.op("act", lambda e: e.activation(out=out, in_=in_, func=func, scale=scale, bias=bias), reads, writes)
        else:
            P.op("act", lambda e: e.activation(out=out, in_=in_, func=func, scale=scale, bias=bias,
                                               accum_out=accum_out), reads, writes)

    def ACTS(out, in_, func, scale_ap, reads, writes):
        P.op("act", lambda e: e.activation(out=out, in_=in_, func=func, scale=scale_ap), reads, writes)

    def COPY(eng, out, in_, reads, writes):
        if eng == "act":
            P.op("act", lambda e: e.copy(out=out, in_=in_), reads, writes)
        else:
            P.op(eng, lambda e: e.tensor_copy(out=out, in_=in_), reads, writes)

    def TT(eng, out, in0, in1, op, reads, writes):
        P.op(eng, lambda e: e.tensor_tensor(out=out, in0=in0, in1=in1, op=op), reads, writes)

    def RECIP(out, in_, reads, writes):
        P.op("dve", lambda e: e.reciprocal(out=out, in_=in_), reads, writes)

    def TSMUL(out, in0, sc, reads, writes):
        P.op("dve", lambda e: e.tensor_scalar_mul(out=out, in0=in0, scalar1=sc), reads, writes)

    evac_ctr = [0]

    def EVAC(out, in_, reads, writes, pref=None):
        evac_ctr[0] += 1
        COPY(pref or ("act" if evac_ctr[0] % 2 else "dve"), out, in_, reads, writes)

    def sbuf(st, name, shape, dt):
        return st.enter_context(nc.sbuf_tensor(name, shape, dt)).ap()

    def psum(st, name, shape, dt=F32):
        return st.enter_context(nc.psum_tensor(name, shape, dt)).ap()

    uid = [0]

    def nm(s):
        uid[0] += 1
        return "%s_%d" % (s, uid[0])

    def load_w(st, scr, R, C, name, nsplit=1):
        t = sbuf(st, nm(name), [128, R // 128, C], BF16)
        src = scr.rearrange("(c p) n -> p c n", p=128)
        nch = R // 128
        bl = []
        for c0 in range(nch):
            b = P.buf(name)
            P.dma("sp", t[:, c0, :], src[:, c0, :], writes=[b])
            bl.append(b)
        return t, bl

    with contextlib.ExitStack() as gst:
        ident = sbuf(gst, "ident", [128, 128], BF16)
        identf = sbuf(gst, "identf", [128, 128], F32)
        ones_bf = sbuf(gst, "ones_bf", [128, 128], BF16)
        ones32 = sbuf(gst, "ones32", [128, 128], F32)
        c128 = sbuf(gst, "c128s", [128, 384], BF16)
        cb = P.buf("const")
        P.op("pool", lambda e: e.memset(identf, 0.0), writes=[cb])
        P.op("pool", lambda e: e.affine_select(out=identf, in_=identf, pattern=[[-1, 128]],
                                               compare_op=ALU.not_equal, fill=1.0, base=0,
                                               channel_multiplier=1), reads=[cb], writes=[cb])
        P.op("dve", lambda e: e.tensor_copy(out=ident, in_=identf), reads=[cb], writes=[cb])
        P.op("pool", lambda e: e.memset(ones32, 1.0), writes=[cb])
        P.op("dve", lambda e: e.tensor_copy(out=ones_bf, in_=ones32), reads=[cb], writes=[cb])
        P.dma("sp", c128, c128_d, writes=[cb])

        class NT:
            def __init__(self, st, nx, ncol=512, npt=2):
                self.xs = Rot(P, [sbuf(st, nm("x"), [128, D], F32) for _ in range(nx)])
                self.stt = Rot(P, [sbuf(st, nm("st"), [128, 2], F32) for _ in range(12)])
                self.xn = Rot(P, [sbuf(st, nm("xn"), [128, D], BF16) for _ in range(4)])
                self.junk = sbuf(st, nm("junk"), [128, D], BF16)
                self.junkb = P.buf()
                self.pT = Rot(P, [psum(st, nm("pT"), [128, 8, 128], BF16) for _ in range(npt)], True)
                self.hTa = [sbuf(st, nm("hT"), [128, 8, ncol], BF16) for _ in range(2)]
                self.hTbs = [[P.buf() for _ in range(4)] for _ in range(2)]
                self.hi = 0

            def prep(self, tiles):
                stg = []
                for ti, (src, r) in enumerate(tiles):
                    x, xb = self.xs.next()
                    P.dma("sp", x[0:r, :], src, writes=[xb])
                    s, sbf = self.stt.next()
                    ACT(self.junk[0:r, :], x[0:r, :], AF.Square, [xb], [self.junkb, sbf], accum_out=s[0:r, 0:1])
                    ACT(s[0:r, 1:2], s[0:r, 0:1], AF.Sqrt, [sbf], [sbf], scale=1.0 / D, bias=EPS)
                    stg.append([x, xb, r, s, sbf, None, None])
                for e_ in stg:
                    x, xb, r, s, sbf = e_[0:5]
                    xn, xnb = self.xn.next()
                    RECIP(s[0:r, 1:2], s[0:r, 1:2], [sbf], [sbf])
                    TSMUL(xn[0:r, :], x[0:r, :], s[0:r, 1:2], [xb, sbf], [xnb])
                    e_[5], e_[6] = xn, xnb
                return stg

            def finish(self, stg):
                hT, hTb = self.hTa[self.hi % 2], self.hTbs[self.hi % 2]
                self.hi += 1
                col = 0
                xl = []
                for ti, (x, xb, r, s, sbf, xn, xnb) in enumerate(stg):
                    pT, pTb = self.pT.next()
                    for c in range(8):
                        TR(pT[:, c, 0:r], xn[0:r, c * 128:(c + 1) * 128], ident[0:r, 0:r], [xnb, cb], [pTb])
                    EVAC(hT[:, :, col:col + r], pT[:, :, 0:r], [pTb], [hTb[ti]])
                    xl.append((x, xb, r))
                    col += r
                return hT, hTb, col, xl

            def run(self, tiles):
                return self.finish(self.prep(tiles))

        gains = {}
        for name, gd, n in (("g1", g1_d, 8), ("gq", gq_d, 4), ("gkv", gkv_d, 2), ("g2", g2_d, 8)):
            t = sbuf(gst, name + "_sb", [128, n], F32)
            b = P.buf(name)
            P.dma("sp", t, gd, writes=[b])
            gains[name] = (t, b)
        wctr = [0]

        def conv_chunk(stage, obuf, engs, stq, dst, src, rc, pieces, gain, d_lo, d_hi):
            lo = min(p[1] for p in pieces)
            hi = max(p[1] + p[2] for p in pieces)
            stg, stb = stage.next()
            P.dma("sp", stg[:, 0:hi - lo], src[rc * 128:(rc + 1) * 128, lo:hi], writes=[stb])
            ob, obb = obuf.next()
            for (d0, s0, w) in pieces:
                wctr[0] += 1
                eng = engs[wctr[0] % len(engs)]
                o_ap = ob[:, d0 - d_lo:d0 - d_lo + w]
                i_ap = stg[:, s0 - lo:s0 - lo + w]
                if gain is None:
                    COPY(eng, o_ap, i_ap, [stb], [obb])
                else:
                    gt, gb = gains[gain]
                    if eng == "act":
                        ACTS(o_ap, i_ap, AF.Copy, gt[:, rc:rc + 1], [stb, gb], [obb])
                    else:
                        TSMUL(o_ap, i_ap, gt[:, rc:rc + 1], [stb, gb], [obb])
            P.dma(stq, dst[rc * 128:(rc + 1) * 128, d_lo:d_hi], ob[:, 0:d_hi - d_lo], reads=[obb])

        def conv_tasks(dst, Cd, src, R, pieces, gain, maxw=None):
            tasks = []
            if maxw is None:
                for rc in range(R // 128):
                    tasks.append((dst, src, rc, pieces, gain, 0, Cd))
            else:
                assert len(pieces) == 1
                d0, s0, w = pieces[0]
                nsp = (w + maxw - 1) // maxw
                step = (w + nsp - 1) // nsp
                for rc in range(R // 128):
                    for o in range(0, w, step):
                        ww = min(step, w - o)
                        tasks.append((dst, src, rc, [(d0 + o, s0 + o, ww)], gain, d0 + o, d0 + o + ww))
            return tasks

        pcs = []
        for h in range(8):
            pcs += [(256 * h, 192 * h, 192), (256 * h + 192, 192 * h + 160, 32), (256 * h + 224, 192 * h + 128, 32)]
        fg_tasks = (conv_tasks(s_wk, 896, w_in_d, D,
                               [(0, 0, 512), (512, 1024, 256), (768, 1280, 64), (832, 1312, 32), (864, 1280, 32)], "g1")
                    + conv_tasks(s_wq, 512, w_in_d, D, [(0, 512, 512)], "g1")
                    + conv_tasks(s_wuq, 2048, w_uq_d, 512, pcs, "gq")
                    + conv_tasks(s_wukv, 2048, w_ukv_d, 256, [(0, 0, 2048)], "gkv"))
        BGW = 704
        bg_tasks = (conv_tasks(s_wgt, 2048, w_in_d, D, [(0, 1344, 2048)], "g1", BGW)
                    + conv_tasks(s_wfo, D, w_fo_d, 512, [(0, 0, D)], None, BGW)
                    + conv_tasks(s_wao, D, w_ao_d, D, [(0, 0, D)], None, BGW)
                    + conv_tasks(s_wo, D, w_o_d, D, [(0, 0, D)], None, BGW)
                    + conv_tasks(s_wg, DFF, w_g_d, D, [(0, 0, DFF)], "g2", BGW)
                    + conv_tasks(s_wu, DFF, w_u_d, D, [(0, 0, DFF)], "g2", BGW)
                    + conv_tasks(s_wd, D, w_d_d, DFF, [(0, 0, D)], None, BGW))
        with contextlib.ExitStack() as st:
            stage = Rot(P, [sbuf(st, nm("wst"), [128, 2048], F32) for _ in range(3)])
            obuf = Rot(P, [sbuf(st, nm("wob"), [128, 2048], BF16) for _ in range(3)])
            for tsk in fg_tasks:
                conv_chunk(stage, obuf, ("act", "dve"), "act", *tsk)
        P.barrier()
        if STOP == "W":
            P.emit()
            return nc

        for job in ("A", "B"):
            G = GEOM[job]
            L, n_full, nb = G["L"], G["n_full"], G["nb"]
            nst = n_full + 1
            with contextlib.ExitStack() as jst:
                ckvT = sbuf(jst, nm("ckvT"), [128, 2, L], BF16)
                kropeT = sbuf(jst, nm("kropeT"), [128, L], BF16)
                krzb = P.buf()
                P.op("dve", lambda e, t=kropeT: e.memset(t, 0.0), writes=[krzb])
                ckvb = [P.buf() for _ in range(nst)]
                krb = [P.buf() for _ in range(nst)]
                with contextlib.ExitStack() as abst:
                    AB = sbuf(abst, nm("AB"), [128, nb, 1024], BF16)
                    ABb = [[P.buf(), P.buf()] for _ in range(nb)]
                    if DEBUG:
                        P.op("pool", lambda e, AB=AB, nb=nb: e.memset(AB[:, nb - 1, :], 0.0), writes=[ABb[nb - 1]])
                    with contextlib.ExitStack() as st:
                        nt = NT(st, 4)
                        wk, wkb = load_w(st, s_wk, D, 896, "wk")
                        mm = Rot(P, [psum(st, nm("mm"), [128, 512]) for _ in range(4)], True)
                        abp = Rot(P, [psum(st, nm("abp"), [128, 512]) for _ in range(2)], True)
                        uTa = [sbuf(st, nm("uT"), [128, 4, 512], BF16) for _ in range(2)]
                        uTbs = [[P.buf() for _ in range(4)] for _ in range(2)]
                        uti = [0]
                        uTt2 = sbuf(st, nm("uTt"), [128, 128], BF16)
                        uTt = uTt2.rearrange("p (g n) -> p g n", g=4)
                        uTtb = [P.buf() for _ in range(4)]
                        P.op("dve", lambda e, t=uTt2: e.memset(t, 0.0), writes=[uTtb])
                        sqr = Rot(P, [sbuf(st, nm("sq"), [128, 2, 512], BF16) for _ in range(2)])
                        crr = Rot(P, [sbuf(st, nm("cr"), [128, 2, 512], F32) for _ in range(1)])
                        Rr = Rot(P, [sbuf(st, nm("R"), [128, 512], F32) for _ in range(1)])
                        rcr = Rot(P, [sbuf(st, nm("rc"), [64, 2, 512], F32) for _ in range(2)])
                        t12 = Rot(P, [sbuf(st, nm("t12"), [64, 2, 512], F32) for _ in range(1)])
                        def ktiles(s_):
                            r0_ = 512 * s_
                            if s_ == n_full:
                                return [(xk[job][r0_:r0_ + 16, :], 16)]
                            return [(xk[job][r0_ + 128 * t:r0_ + 128 * (t + 1), :], 128) for t in range(4)]

                        slist = [s_ for s_ in range(nst) if KLIM is None or s_ in KLIM]
                        pre = nt.prep(ktiles(slist[0]))
                        for si, s in enumerate(slist):
                            tail = s == n_full
                            r0 = 512 * s
                            hT, hTb, n, _ = nt.finish(pre)
                            if si + 1 < len(slist):
                                pre = nt.prep(ktiles(slist[si + 1]))
                            if tail:
                                u, ub = uTt, uTtb
                            else:
                                u, ub = uTa[uti[0] % 2], uTbs[uti[0] % 2]
                                uti[0] += 1
                            for g in range(4):
                                ps, psb = mm.next()
                                for c in range(8):
                                    MM(ps[:, 0:n], wk[:, c, g * 128:(g + 1) * 128], hT[:, c, 0:n], c == 0, c == 7,
                                       [hTb, wkb[c]], [psb])
                                EVAC(u[:, g, 0:n], ps[:, 0:n], [psb], [ub[g]])
                            sq, sqb = sqr.next()
                            cr, crb = crr.next()
                            for c2 in range(2):
                                ps, psb = mm.next()
                                for c in range(8):
                                    MM(ps[:, 0:n], wk[:, c, 512 + c2 * 128:512 + (c2 + 1) * 128], hT[:, c, 0:n],
                                       c == 0, c == 7, [hTb, wkb[c]], [psb])
                                ACT(sq[:, c2, 0:n], ps[:, 0:n], AF.Square, [psb], [sqb])
                                COPY("dve", cr[:, c2, 0:n], ps[:, 0:n], [psb], [crb])
                            rc, rcb = rcr.next()
                            P.dma("sp", rc[:, :, 0:n], ropek[job][:, :, r0:r0 + n].rearrange("a p n -> p a n"),
                                  writes=[rcb])
                            tt, ttb = t12.next()
                            for j in range(2):
                                ps, psb = mm.next()
                                for c in range(8):
                                    MM(ps[0:64, 0:n], wk[:, c, 768 + 64 * j:832 + 64 * j], hT[:, c, 0:n],
                                       c == 0, c == 7, [hTb, wkb[c]], [psb])
                                TT("dve", tt[:, j, 0:n], ps[0:64, 0:n], rc[:, j, 0:n], ALU.mult, [psb, rcb], [ttb])
                            TT(POOL_EW, kropeT[0:64, r0:r0 + n], tt[:, 0, 0:n], tt[:, 1, 0:n], ALU.add, [ttb, krzb], [krb[s]])
                            if tail:
                                blks = [(2 * n_full, 0, 9, 9)]
                            else:
                                blks = [(2 * s, 0, 256, 128), (2 * s + 1, 128, 384, 128)]
                            for (blk, f0, m0, m) in blks:
                                for hf, (rhs_f, rhs_m) in enumerate(((c128[:, 0:128], c128[:, 0:128]),
                                                                     (c128[:, 128:256], c128[:, 256:384]))):
                                    ab, abb = abp.next()
                                    for g in range(4):
                                        o_ap = ab[0:m, g * 128:(g + 1) * 128]
                                        MM(o_ap, u[:, g, f0:f0 + m], rhs_f, True, False, [ub[g], cb], [abb])
                                        MM(o_ap, u[:, g, m0:m0 + m], rhs_m, False, True, [ub[g], cb], [abb])
                                    EVAC(AB[0:m, blk, hf * 512:(hf + 1) * 512], ab[0:m, :], [abb], [ABb[blk][hf]])
                            ps, psb = mm.next()
                            for c2 in range(2):
                                MM(ps[:, 0:n], ones_bf, sq[:, c2, 0:n], c2 == 0, c2 == 1, [sqb, cb], [psb])
                            R, Rb = Rr.next()
                            ACT(R[:, 0:n], ps[:, 0:n], AF.Sqrt, [psb], [Rb], scale=1.0 / 256, bias=EPS)
                            RECIP(R[:, 0:n], R[:, 0:n], [Rb], [Rb])
                            for c2 in range(2):
                                TT("dve", ckvT[:, c2, r0:r0 + n], cr[:, c2, 0:n], R[:, 0:n], ALU.mult,
                                   [crb, Rb], [ckvb[s]])
                    P.barrier()
                    if STOP == "K" + job:
                        dA = nc.dram_tensor("dbg_AB", [128, nb, 1024], BF16, kind="ExternalOutput").ap()
                        dC = nc.dram_tensor("dbg_ckv", [128, 2, L], BF16, kind="ExternalOutput").ap()
                        dK = nc.dram_tensor("dbg_kr", [128, L], BF16, kind="ExternalOutput").ap()
                        P.dma("sp", dA, AB, final=True)
                        P.dma("sp", dC, ckvT, final=True)
                        P.dma("sp", dK, kropeT, final=True)
                        P.emit()
                        return nc
                    nfc = 4 if job == "A" else 2
                    Ns = 8 if job == "A" else 2
                    with contextlib.ExitStack() as st:
                        if job == "A":
                            FT0 = sbuf(st, nm("FT0"), [128, 4, 2048], BF16)
                            FT1 = sbuf(st, nm("FT1"), [128, 4, 2048], BF16)
                            FT0b, FT1b = P.buf(), P.buf()
                            fdst = lambda g, j: (FT0[:, g, j * 512:(j + 1) * 512], FT0b)
                            mdst = lambda g, j: (FT1[:, g, j * 512:(j + 1) * 512], FT1b)
                            sdst = lambda g: (FT1[:, g, 2048 - Ns:2048], FT1b)
                        else:
                            FT0 = sbuf(st, nm("FT0"), [128, 4, 2048], BF16)
                            FT0b = P.buf()
                            FT1b = P.buf()
                            fdst = lambda g, j: (FT0[:, g, j * 512:(j + 1) * 512], FT0b)
                            mdst = lambda g, j: (FT0[:, g, 1024 + j * 512:1024 + (j + 1) * 512], FT1b)
                            sdst = lambda g: (FT0[:, g, 2048 - Ns:2048], FT1b)
                        tabs = Rot(P, [sbuf(st, nm("tab"), [128, 4, 1024], BF16) for _ in range(3)])
                        tsm = sbuf(st, nm("tsm"), [128, nb, 2 * Ns], BF16)
                        tsmb = P.buf()
                        P.dma("sp", tsm, tabs_small[job].rearrange("b p n -> p b n"), writes=[tsmb])
                        p1R = Rot(P, [sbuf(st, nm("p1sb"), [128, 512], F32) for _ in range(2)])
                        fps = psum(st, nm("fps"), [128, 8, 512])
                        fpb = [P.buf("", True) for _ in range(8)]
                        for j in range(nfc):
                            for s4 in range(0, 2 * n_full + 1, 4):
                                tail = s4 == 2 * n_full
                                T, Tb = tabs.next()
                                if tail:
                                    P.dma("sp", T[:, 0, :], tab[job][j, 2 * n_full, :, :], writes=[Tb])
                                    blks = [(2 * n_full, 0, 9)]
                                else:
                                    P.dma("sp", T, tab[job][j, s4:s4 + 4, :, :].rearrange("b p n -> p b n"),
                                          writes=[Tb])
                                    blks = [(s4 + bi, bi, 128) for bi in range(4)]
                                for (blk, bi, m) in blks:
                                    for g in range(4):
                                        MM(fps[:, g, :], AB[0:m, blk, g * 128:(g + 1) * 128], T[0:m, bi, 0:512],
                                           blk == 0, blk == nb - 1, [ABb[blk][0], Tb], [fpb[g]])
                                        MM(fps[:, 4 + g, :], AB[0:m, blk, 512 + g * 128:512 + (g + 1) * 128],
                                           T[0:m, bi, 512:1024], blk == 0, blk == nb - 1, [ABb[blk][1], Tb],
                                           [fpb[4 + g]])
                            for g in range(4):
                                p1, p1b = p1R.next()
                                COPY("act", p1, fps[:, g, :], [fpb[g]], [p1b])
                                d_ap, d_b = fdst(g, j)
                                TT("dve", d_ap, fps[:, 4 + g, :], p1, ALU.add, [fpb[4 + g], p1b], [d_b])
                                d_ap, d_b = mdst(g, j)
                                TT("dve", d_ap, p1, fps[:, 4 + g, :], ALU.subtract, [fpb[4 + g], p1b], [d_b])
                        for g in range(4):
                            reg = fps[:, 0, g * Ns:(g + 1) * Ns]
                            for blk in range(nb):
                                m = 9 if blk == nb - 1 else 128
                                MM(reg, AB[0:m, blk, g * 128:(g + 1) * 128], tsm[0:m, blk, 0:Ns],
                                   blk == 0, False, [ABb[blk][0], tsmb], [fpb[0]])
                                MM(reg, AB[0:m, blk, 512 + g * 128:512 + (g + 1) * 128], tsm[0:m, blk, Ns:2 * Ns],
                                   False, blk == nb - 1, [ABb[blk][1], tsmb], [fpb[0]])
                        for g in range(4):
                            d_ap, d_b = sdst(g)
                            COPY("act", d_ap, fps[:, 0, g * Ns:(g + 1) * Ns], [fpb[0]], [d_b])
                        if job == "A":
                            P.dma("sp", FTs[0], FT0, reads=[FT0b])
                            P.dma("sp", FTs[1], FT1, reads=[FT1b])
                        else:
                            P.dma("sp", FTs[2], FT0, reads=[FT0b, FT1b])
                    P.barrier()
                    if STOP == "D" + job:
                        P.emit()
                        return nc
                for gi in range(NGROUPS[job]):
                    gg = GBASE[job] + gi
                    QQ = QN[job]
                    nqc = QQ // 512
                    with contextlib.ExitStack() as gst2:
                        cqT = sbuf(gst2, nm("cqT"), [128, 4, QQ], BF16)
                        cqb = [P.buf() for _ in range(nqc)]
                        with contextlib.ExitStack() as st:
                            nt = NT(st, 5)
                            wq, wqb = load_w(st, s_wq, D, 512, "wq")
                            mm = Rot(P, [psum(st, nm("mm"), [128, 512]) for _ in range(4)], True)
                            sqr = Rot(P, [sbuf(st, nm("sq"), [128, 4, 512], BF16) for _ in range(2)])
                            crr = Rot(P, [sbuf(st, nm("cr"), [128, 4, 512], F32) for _ in range(2)])
                            Rr = Rot(P, [sbuf(st, nm("R"), [128, 512], F32) for _ in range(2)])
                            def qtiles(qc_):
                                q0_ = gi * 2048 + qc_ * 512
                                return [(xq[job][q0_ + 128 * t:q0_ + 128 * (t + 1), :], 128) for t in range(4)]

                            pre = nt.prep(qtiles(0))
                            for qc in range(nqc):
                                hT, hTb, n, _ = nt.finish(pre)
                                if qc + 1 < nqc:
                                    pre = nt.prep(qtiles(qc + 1))
                                sq, sqb = sqr.next()
                                cr, crb = crr.next()
                                for c4 in range(4):
                                    ps, psb = mm.next()
                                    for c in range(8):
                                        MM(ps, wq[:, c, c4 * 128:(c4 + 1) * 128], hT[:, c, :], c == 0, c == 7,
                                           [hTb, wqb[c]], [psb])
                                    ACT(sq[:, c4, :], ps, AF.Square, [psb], [sqb])
                                    COPY("dve", cr[:, c4, :], ps, [psb], [crb])
                                ps, psb = mm.next()
                                for c4 in range(4):
                                    MM(ps, ones_bf, sq[:, c4, :], c4 == 0, c4 == 3, [sqb, cb], [psb])
                                R, Rb = Rr.next()
                                ACT(R, ps, AF.Sqrt, [psb], [Rb], scale=1.0 / 512, bias=EPS)
                                RECIP(R, R, [Rb], [Rb])
                                for c4 in range(4):
                                    TT("dve", cqT[:, c4, qc * 512:(qc + 1) * 512], cr[:, c4, :], R, ALU.mult,
                                       [crb, Rb], [cqb[qc]])
                        P.barrier()
                        if STOP == "Q" + job:
                            dQ = nc.dram_tensor("dbg_cq", [128, 4, QQ], BF16, kind="ExternalOutput").ap()
                            P.dma("sp", dQ, cqT, final=True)
                            P.emit()
                            return nc
                        with contextlib.ExitStack() as st:
                            wuq, wuqb = load_w(st, s_wuq, 512, 2048, "wuq")
                            wukv, wukvb = load_w(st, s_wukv, 256, 2048, "wukv")
                            aoR = Rot(P, [sbuf(st, nm("ao"), [128, 512], BF16) for _ in range(4)])
                            nkv = 2 if job == "A" else 1
                            nkt = 4 * n_full + 1
                            KhR = Rot(P, [sbuf(st, nm("Kh"), [128, L], BF16) for _ in range(nkv)])
                            VhR = Rot(P, [sbuf(st, nm("Vh"), [128, nkt, 128], BF16) for _ in range(nkv)])
                            rq = sbuf(st, nm("rq"), [64, 2, QQ], F32)
                            rqb = P.buf()
                            P.dma("sp", rq, ropeq[job][:, :, 0:QQ].rearrange("a p n -> p a n"),
                                  writes=[rqb])
                            SR = Rot(P, [psum(st, nm("S"), [128, 2, 512]) for _ in range(2)], True)
                            poR = Rot(P, [psum(st, nm("po"), [128, 512]) for _ in range(2)], True)
                            mm = Rot(P, [psum(st, nm("mm"), [128, 512]) for _ in range(2)], True)
                            PTR = Rot(P, [sbuf(st, nm("PT"), [128, 2, 512], BF16) for _ in range(4)])
                            qnR = Rot(P, [sbuf(st, nm("qn"), [128, 512], BF16) for _ in range(2)])
                            qrR = Rot(P, [sbuf(st, nm("qr"), [128, 512], BF16) for _ in range(2)])
                            for (qr_, qrb_) in qrR.items:
                                P.op("dve", lambda e, t=qr_: e.memset(t, 0.0), writes=[qrb_])
                            t12 = Rot(P, [sbuf(st, nm("t12"), [64, 2, 512], F32) for _ in range(2)])
                            accR = Rot(P, [sbuf(st, nm("acc"), [128, 2, 512], F32) for _ in range(2)])
                            recR = Rot(P, [sbuf(st, nm("rec"), [128, 512], F32) for _ in range(2)])

                            def kvproj(h):
                                Kh, Khb = KhR.next()
                                Vh, Vhb = VhR.next()
                                for s in range(nst):
                                    n = 16 if s == n_full else 512
                                    r0 = 512 * s
                                    ps, psb = mm.next()
                                    for c in range(2):
                                        MM(ps[:, 0:n], wukv[:, c, h * 256:h * 256 + 128], ckvT[:, c, r0:r0 + n],
                                           c == 0, c == 1, [ckvb[s], wukvb[c]], [psb])
                                    EVAC(Kh[:, r0:r0 + n], ps[:, 0:n], [psb], [Khb], "act")
                                    ps, psb = mm.next()
                                    if s == n_full:
                                        for c in range(2):
                                            MM(ps[0:16, 0:128], ckvT[:, c, r0:r0 + 16],
                                               wukv[:, c, h * 256 + 128:h * 256 + 256], c == 0, c == 1,
                                               [ckvb[s], wukvb[c]], [psb])
                                        EVAC(Vh[0:16, 4 * n_full, :], ps[0:16, 0:128], [psb], [Vhb], "act")
                                    else:
                                        for t in range(4):
                                            for c in range(2):
                                                MM(ps[:, t * 128:(t + 1) * 128],
                                                   ckvT[:, c, r0 + t * 128:r0 + (t + 1) * 128],
                                                   wukv[:, c, h * 256 + 128:h * 256 + 256], c == 0, c == 1,
                                                   [ckvb[s], wukvb[c]], [psb])
                                        EVAC(Vh[:, 4 * s:4 * s + 4, :], ps.rearrange("p (t d) -> p t d", t=4),
                                             [psb], [Vhb])
                                return Kh, Khb, Vh, Vhb

                            def qproj(h, qc):
                                q0 = qc * 512
                                ps, psb = mm.next()
                                for c in range(4):
                                    MM(ps, wuq[:, c, h * 256:h * 256 + 128], cqT[:, c, q0:q0 + 512],
                                       c == 0, c == 3, [cqb[qc], wuqb[c]], [psb])
                                qn, qnb = qnR.next()
                                COPY("act", qn, ps, [psb], [qnb])
                                tt, ttb = t12.next()
                                for j in range(2):
                                    ps, psb = mm.next()
                                    for c in range(4):
                                        MM(ps[0:64, :], wuq[:, c, h * 256 + 128 + 64 * j:h * 256 + 192 + 64 * j],
                                           cqT[:, c, q0:q0 + 512], c == 0, c == 3, [cqb[qc], wuqb[c]], [psb])
                                    TT("dve", tt[:, j, :], ps[0:64, :], rq[:, j, q0:q0 + 512], ALU.mult,
                                       [psb, rqb], [ttb])
                                qr, qrb = qrR.next()
                                TT(POOL_EW, qr[0:64, :], tt[:, 0, :], tt[:, 1, :], ALU.add, [ttb], [qrb])
                                return qn, qnb, qr, qrb

                            nfull_kb = 4 * n_full
                            npairs = nfull_kb // 2 + 1

                            def emit_S(i, Kh, Khb, qn, qnb, qr, qrb):
                                S, Sb = SR.next()
                                if i == npairs - 1:
                                    k0 = nfull_kb * 128
                                    MM(S[0:16, 0, :], Kh[:, k0:k0 + 16], qn, True, False, [Khb, qnb], [Sb])
                                    MM(S[0:16, 0, :], kropeT[:, k0:k0 + 16], qr, False, True, [krb[n_full], qrb], [Sb])
                                else:
                                    for j in range(2):
                                        k0 = (2 * i + j) * 128
                                        MM(S[:, j, :], Kh[:, k0:k0 + 128], qn, True, False, [Khb, qnb], [Sb])
                                        MM(S[:, j, :], kropeT[:, k0:k0 + 128], qr, False, True, [krb[k0 // 512], qrb], [Sb])
                                return S, Sb

                            def emit_exp(i, S, Sb, acc, accb):
                                PT, PTb = PTR.next()
                                if i == npairs - 1:
                                    ACT(PT[0:16, 0, :], S[0:16, 0, :], AF.Exp, [Sb], [PTb], scale=SCALE)
                                    TT("dve", acc[0:16, 0, :], acc[0:16, 0, :], PT[0:16, 0, :], ALU.add, [accb, PTb], [accb])
                                else:
                                    ACT(PT, S, AF.Exp, [Sb], [PTb], scale=SCALE)
                                    if i == 0:
                                        COPY("dve", acc, PT, [PTb], [accb])
                                    else:
                                        TT("dve", acc, acc, PT, ALU.add, [accb, PTb], [accb])
                                return PT, PTb

                            def emit_pv(i, PT, PTb, Vh, Vhb, po, pob):
                                kp = 2 * i
                                if i == npairs - 1:
                                    MM(po, Vh[0:16, kp, :], PT[0:16, 0, :], False, True, [Vhb, PTb], [pob])
                                else:
                                    for j in range(2):
                                        MM(po, Vh[:, kp + j, :], PT[:, j, :], (kp + j) == 0, False, [Vhb, PTb], [pob])

                            def finalize(h, qc, po, pob, acc, accb):
                                q0 = qc * 512
                                ps, psb = mm.next()
                                MM(ps, ones32, acc[:, 0, :], True, False, [accb, cb], [psb])
                                MM(ps, ones32, acc[:, 1, :], False, True, [accb, cb], [psb])
                                rec, recb = recR.next()
                                RECIP(rec, ps, [psb], [recb])
                                ao, aob = aoR.next()
                                TT("dve", ao, po, rec, ALU.mult, [pob, recb], [aob])
                                ggq = GBASE[job] + qc // 4
                                P.dma("sp", ATs[ggq, :, h, (qc % 4) * 512:(qc % 4 + 1) * 512], ao, reads=[aob])

                            if job == "A" and bg_tasks:
                                bstage = Rot(P, [sbuf(st, nm("bst"), [128, BGW], F32) for _ in range(4)])
                                bobuf = Rot(P, [sbuf(st, nm("bob"), [128, BGW], BF16) for _ in range(4)])

                            its = [(h, qc) for h in range(8) for qc in range(nqc)]
                            kv = {0: kvproj(0)}
                            q_next = qproj(0, 0)
                            pending = None
                            for idx, (h, qc) in enumerate(its):
                                qn, qnb, qr, qrb = q_next
                                Kh, Khb, Vh, Vhb = kv[h]
                                acc, accb = accR.next()
                                po, pob = poR.next()
                                nxt = its[idx + 1] if idx + 1 < len(its) else None
                                S_next = emit_S(0, Kh, Khb, qn, qnb, qr, qrb)
                                for i in range(npairs):
                                    S, Sb = S_next
                                    if i + 1 < npairs:
                                        S_next = emit_S(i + 1, Kh, Khb, qn, qnb, qr, qrb)
                                    if i == 1 and pending is not None:
                                        finalize(*pending)
                                        pending = None
                                    if i == 3 and nxt is not None:
                                        if nxt[0] != h and nkv == 2:
                                            kv[nxt[0]] = kvproj(nxt[0])
                                        q_next = qproj(nxt[0], nxt[1])
                                    PTcur = emit_exp(i, S, Sb, acc, accb)
                                    if i >= 1:
                                        emit_pv(i - 1, PTprev[0], PTprev[1], Vh, Vhb, po, pob)
                                    PTprev = PTcur
                                    if job == "A" and bg_tasks and i in (5, 9, 13):
                                        conv_chunk(bstage, bobuf, ("act",), "sp", *bg_tasks.pop(0))
                                emit_pv(npairs - 1, PTprev[0], PTprev[1], Vh, Vhb, po, pob)
                                pending = (h, qc, po, pob, acc, accb)
                                if nxt is not None and nxt[0] != h and nkv == 1:
                                    kv[nxt[0]] = kvproj(nxt[0])
                            finalize(*pending)
                        P.barrier()
                        if STOP == "At" + job:
                            P.emit()
                            return nc

        if bg_tasks:
            with contextlib.ExitStack() as st:
                stage = Rot(P, [sbuf(st, nm("wst"), [128, BGW], F32) for _ in range(3)])
                obuf = Rot(P, [sbuf(st, nm("wob"), [128, BGW], BF16) for _ in range(3)])
                while bg_tasks:
                    conv_chunk(stage, obuf, ("act", "dve"), "act", *bg_tasks.pop(0))
            P.barrier()
        with contextlib.ExitStack() as st:
            nt = NT(st, 8, 512, 4)
            wgt, wgtb = load_w(st, s_wgt, D, 2048, "wgt", 2)
            wfo, wfob = load_w(st, s_wfo, 512, D, "wfo")
            wao, waob = load_w(st, s_wao, D, D, "wao")
            wo, wob = load_w(st, s_wo, D, D, "wo")
            mm = Rot(P, [psum(st, nm("mm"), [128, 512]) for _ in range(4)], True)
            FTc = Rot(P, [sbuf(st, nm("FTc"), [128, 4, 512], BF16) for _ in range(2)])
            ATc = Rot(P, [sbuf(st, nm("ATc"), [128, 8, 512], BF16) for _ in range(2)])
            sgR = Rot(P, [sbuf(st, nm("sg"), [128, 2, 512], BF16) for _ in range(2)])
            tAB = Rot(P, [sbuf(st, nm("tAB"), [128, 2, 512], F32) for _ in range(2)])
            mTR = Rot(P, [sbuf(st, nm("mT"), [128, 8, 512], BF16) for _ in range(2)])
            x1R = Rot(P, [sbuf(st, nm("x1"), [128, D], F32) for _ in range(5)])
            pend = []
            def t1tiles(ci_):
                gg_, qc_ = ci_ // 4, ci_ % 4
                job_ = "A" if gg_ < 2 else "B"
                gi_ = gg_ if gg_ < 2 else 0
                q0_ = gi_ * 2048 + qc_ * 512
                return [(xq[job_][q0_ + 128 * t:q0_ + 128 * (t + 1), :], 128) for t in range(4)]

            pre = nt.prep(t1tiles(0))
            for gg in range(3):
                for qc in range(4):
                    hT, hTb, n, xl = nt.finish(pre)
                    if gg * 4 + qc + 1 < 12:
                        pre = nt.prep(t1tiles(gg * 4 + qc + 1))
                    for (d_, s_, b_) in pend:
                        P.dma("sp", d_, s_, reads=[b_])
                    pend = []
                    ft, ftb = FTc.next()
                    P.dma("sp", ft, FTs[gg, :, :, qc * 512:(qc + 1) * 512], writes=[ftb])
                    at, atb = ATc.next()
                    P.dma("sp", at, ATs[gg, :, :, qc * 512:(qc + 1) * 512], writes=[atb])
                    mT, mTb = mTR.next()
                    for j in range(8):
                        sg, sgb = sgR.next()
                        for k in range(2):
                            ps, psb = mm.next()
                            for c in range(8):
                                MM(ps, wgt[:, c, k * 1024 + j * 128:k * 1024 + (j + 1) * 128], hT[:, c, :],
                                   c == 0, c == 7, [hTb, wgtb[c]], [psb])
                            ACT(sg[:, k, :], ps, AF.Sigmoid, [psb], [sgb])
                        tab_, tabb = tAB.next()
                        ps, psb = mm.next()
                        for c in range(4):
                            MM(ps, wfo[:, c, j * 128:(j + 1) * 128], ft[:, c, :], c == 0, c == 3, [ftb, wfob[c]], [psb])
                        TT("dve", tab_[:, 0, :], ps, sg[:, 0, :], ALU.mult, [psb, sgb], [tabb])
                        ps, psb = mm.next()
                        for c in range(8):
                            MM(ps, wao[:, c, j * 128:(j + 1) * 128], at[:, c, :], c == 0, c == 7, [atb, waob[c]], [psb])
                        TT("dve", tab_[:, 1, :], ps, sg[:, 1, :], ALU.mult, [psb, sgb], [tabb])
                        TT(POOL_EW, mT[:, j, :], tab_[:, 0, :], tab_[:, 1, :], ALU.add, [tabb], [mTb])
                    for t in range(4):
                        x1, x1b = x1R.next()
                        x, xb, _r = xl[t]
                        for hh in range(2):
                            ps, psb = mm.next()
                            for c in range(8):
                                MM(ps, mT[:, c, t * 128:(t + 1) * 128], wo[:, c, hh * 512:(hh + 1) * 512],
                                   c == 0, c == 7, [mTb, wob[c]], [psb])
                            TT("dve", x1[:, hh * 512:(hh + 1) * 512], ps, x[:, hh * 512:(hh + 1) * 512], ALU.add,
                               [psb, xb], [x1b])
                        row = gg * 2048 + qc * 512 + t * 128
                        pend.append((x1s[row:row + 128, :], x1, x1b))
            for (d_, s_, b_) in pend:
                P.dma("sp", d_, s_, reads=[b_])
        P.barrier()
        if STOP == "T1":
            P.emit()
            return nc

        TC = 256
        with contextlib.ExitStack() as st:
            nt = NT(st, 4, TC)
            wg, wgb = load_w(st, s_wg, D, DFF, "wg", 2)
            wu, wub = load_w(st, s_wu, D, DFF, "wu", 2)
            wd, wdb = load_w(st, s_wd, DFF, D, "wd", 2)
            gfin = sbuf(st, "gfin_sb", [128, D], F32)
            gfb = P.buf()
            P.dma("sp", gfin, gfin_d, writes=[gfb])
            mm = Rot(P, [psum(st, nm("mm"), [128, 512]) for _ in range(6)], True)
            aTR = Rot(P, [sbuf(st, nm("aT"), [128, 22, TC], BF16)])
            slR = Rot(P, [sbuf(st, nm("sl"), [128, TC], F32) for _ in range(2)])
            x2R = Rot(P, [sbuf(st, nm("x2"), [128, D], F32) for _ in range(4)])
            pend = []
            stR = Rot(P, [sbuf(st, nm("st2"), [128, 2], F32) for _ in range(2)])
            junk2 = sbuf(st, nm("junk2"), [128, D], BF16)
            junk2b = P.buf()

            def STT(out, in0, sc, in1, reads, writes):
                P.op("dve", lambda e: e.scalar_tensor_tensor(out=out, in0=in0, scalar=sc, in1=in1,
                                                             op0=ALU.mult, op1=ALU.mult), reads, writes)

            nt2 = TC // 128
            def t2tiles(ci_):
                return [(x1s[ci_ * TC + 128 * t:ci_ * TC + 128 * (t + 1), :], 128) for t in range(nt2)]

            nch2 = 6144 // TC
            pre = nt.prep(t2tiles(0))
            for ci in range(nch2):
                row0 = ci * TC
                hT, hTb, n, xl = nt.finish(pre)
                if ci + 1 < nch2:
                    pre = nt.prep(t2tiles(ci + 1))
                for (d_, s_, b_) in pend:
                    P.dma("sp", d_, s_, reads=[b_], final=True)
                pend = []
                aT, aTb = aTR.next()
                for j in range(22):
                    pg, pgb = mm.next()
                    for c in range(8):
                        MM(pg[:, 0:TC], wg[:, c, j * 128:(j + 1) * 128], hT[:, c, :], c == 0, c == 7, [hTb, wgb[c]], [pgb])
                    pu, pub = mm.next()
                    for c in range(8):
                        MM(pu[:, 0:TC], wu[:, c, j * 128:(j + 1) * 128], hT[:, c, :], c == 0, c == 7, [hTb, wub[c]], [pub])
                    sl, slb = slR.next()
                    ACT(sl, pg[:, 0:TC], AF.Silu, [pgb], [slb])
                    TT("dve", aT[:, j, :], sl, pu[:, 0:TC], ALU.mult, [slb, pub], [aTb])
                for t in range(nt2):
                    x2, x2b = x2R.next()
                    x, xb, _r = xl[t]
                    for hh in range(2):
                        ps, psb = mm.next()
                        for j in range(22):
                            MM(ps, aT[:, j, t * 128:(t + 1) * 128], wd[:, j, hh * 512:(hh + 1) * 512],
                               j == 0, j == 21, [aTb, wdb[j]], [psb])
                        TT("dve", x2[:, hh * 512:(hh + 1) * 512], ps, x[:, hh * 512:(hh + 1) * 512], ALU.add,
                           [psb, xb], [x2b])
                    s2, s2b = stR.next()
                    ACT(junk2, x2, AF.Square, [x2b], [junk2b, s2b], accum_out=s2[:, 0:1])
                    ACT(s2[:, 1:2], s2[:, 0:1], AF.Sqrt, [s2b], [s2b], scale=1.0 / D, bias=EPS)
                    RECIP(s2[:, 1:2], s2[:, 1:2], [s2b], [s2b])
                    STT(x2, x2, s2[:, 1:2], gfin, [x2b, s2b, gfb], [x2b])
                    r_ = row0 + t * 128
                    if r_ < 4096:
                        dst = yA[r_:r_ + 128, :]
                    else:
                        dst = yB[r_ - 4096:r_ - 4096 + 128, :]
                    pend.append((dst, x2, x2b))
            for (d_, s_, b_) in pend:
                P.dma("sp", d_, s_, reads=[b_], final=True)
        P.emit()
    return nc


def _rope_tab(pos):
    inv = (1.0 / (10000.0 ** (np.arange(0, 64, 2, dtype=np.float32) / np.float32(64)))).astype(np.float32)
    ang = pos.astype(np.float32)[:, None] * inv[None, :]
    c = np.cos(ang).astype(np.float32).T
    s = np.sin(ang).astype(np.float32).T
    cc = np.concatenate([c, c], 0)
    ss = np.concatenate([-s, s], 0)
    return np.ascontiguousarray(np.stack([cc, ss], 0))


def _dft_tab(G, qpos, chunk=512):
    L = G["L"]
    nq = len(qpos)
    nqc = max(1, nq // chunk)
    w = nq // nqc
    out = np.zeros((nqc, G["nb"], 128, 2 * w), dtype=ml_dtypes.bfloat16)
    sc = 1.0 / np.sqrt(L)
    q = qpos.astype(np.int64)
    for b, sp in enumerate(G["sblocks"]):
        m = len(sp)
        prod = (sp.astype(np.int64)[:, None] * q[None, :]) % L
        ang = prod.astype(np.float64) * (2.0 * np.pi / L)
        cs = (np.cos(ang) * sc).reshape(m, nqc, w)
        sn = (-np.sin(ang) * sc).reshape(m, nqc, w)
        out[:, b, 0:m, 0:w] = cs.transpose(1, 0, 2).astype(ml_dtypes.bfloat16)
        out[:, b, 0:m, w:2 * w] = sn.transpose(1, 0, 2).astype(ml_dtypes.bfloat16)
    return out


def _qorders():
    fA = np.arange(16, LA // 2)
    sA = np.array([LA // 2] + list(range(LA - 15, LA)))
    qA = np.concatenate([fA, sA[0:8], LA - fA, sA[8:16]])
    assert len(qA) == 4096 and len(set(qA.tolist())) == 4096 and qA.min() == 16 and qA.max() == LA - 1
    fB = np.arange(16, LB // 2)
    sB = np.array([LB // 2] + list(range(LB - 15, LB)))
    qB = []
    for qt in range(4):
        f = fB[1022 * qt:1022 * (qt + 1)]
        sgl = sB[4 * qt:4 * qt + 4]
        qB.append(np.concatenate([f, sgl[0:2], LB - f, sgl[2:4]]))
    allB = np.concatenate(qB)
    assert len(allB) == 8192 and len(set(allB.tolist())) == 8192 and allB.min() == 16 and allB.max() == LB - 1
    return qA, qB


_CACHE = {}


def _consts():
    if "c" in _CACHE:
        return _CACHE["c"]
    k = np.arange(128)
    ang = 2.0 * np.pi * ((k[:, None] * k[None, :]) % 128) / 128.0
    sc = 1.0 / np.sqrt(128.0)
    c128 = np.concatenate([np.cos(ang) * sc, np.sin(ang) * sc, -np.sin(ang) * sc], 1).astype(ml_dtypes.bfloat16)
    GA, GB = GEOM["A"], GEOM["B"]
    qA, qB = _qorders()
    tabA = _dft_tab(GA, qA[0:2048])
    tabAs = np.ascontiguousarray(_dft_tab(GA, qA[4088:4096], chunk=8)[0])
    tabB = [_dft_tab(GB, qB[qt][0:1024]) for qt in range(4)]
    tabBs = [np.ascontiguousarray(_dft_tab(GB, qB[qt][2046:2048], chunk=2)[0]) for qt in range(4)]
    ropekA = _rope_tab(GA["korder"])
    ropekB = _rope_tab(GB["korder"])
    ropeqA = _rope_tab(qA)
    ropeqB = [_rope_tab(qB[qt]) for qt in range(4)]
    _CACHE["c"] = dict(c128=c128, tabA=tabA, tabB=tabB, tabAs=tabAs, tabBs=tabBs, ropekA=ropekA, ropekB=ropekB,
                       ropeqA=ropeqA, ropeqB=ropeqB, qA=qA, qB=qB)
    return _CACHE["c"]


def kernel(x_prompt, x_sample, meta_tokens, norm1_g, w_in, q_norm_g, kv_norm_g, w_uq, w_ukv,
           w_fourier_out, w_attn_out, w_o, norm2_g, w_ffn_gate, w_ffn_up, w_ffn_down, final_norm_g):
    f = lambda a: np.ascontiguousarray(np.asarray(a, dtype=np.float32))
    x_prompt, x_sample, meta = f(x_prompt), f(x_sample), f(meta_tokens)
    C = _consts()
    gl = lambda g: np.ascontiguousarray(f(g).reshape(-1, 128).T)
    common = {
        "c128": C["c128"], "g1": gl(norm1_g[0]), "gq": gl(q_norm_g[0]), "gkv": gl(kv_norm_g[0]), "g2": gl(norm2_g[0]),
        "gfin": np.ascontiguousarray(np.broadcast_to(f(final_norm_g)[None, :], (128, D))),
        "w_in": f(w_in[0]), "w_uq": f(w_uq[0]), "w_ukv": f(w_ukv[0]), "w_fo": f(w_fourier_out[0]),
        "w_ao": f(w_attn_out[0]), "w_o": f(w_o[0]), "w_g": f(w_ffn_gate[0]), "w_u": f(w_ffn_up[0]),
        "w_d": f(w_ffn_down[0]), "tabA": C["tabA"], "tabAs": C["tabAs"], "ropekA": C["ropekA"], "ropeqA": C["ropeqA"],
        "ropekB": C["ropekB"],
    }
    koA, koB = GEOM["A"]["korder"], GEOM["B"]["korder"]
    qA, qB = C["qA"], C["qB"]
    xkB = []
    for s in range(2):
        full = np.concatenate([meta, x_sample[s]], 0)
        xkB.append(np.ascontiguousarray(full[koB]))
    in_maps = []
    for c in range(8):
        fullA = np.concatenate([meta, x_prompt[c]], 0)
        s, qt = c // 4, c % 4
        m = dict(common)
        m["xkA"] = np.ascontiguousarray(fullA[koA])
        m["xqA"] = np.ascontiguousarray(x_prompt[c][qA - 16])
        m["xkB"] = xkB[s]
        m["xqB"] = np.ascontiguousarray(x_sample[s][qB[qt] - 16])
        m["tabB"] = C["tabB"][qt]
        m["tabBs"] = C["tabBs"][qt]
        if SMALL_TABS:
            m["tabA"] = C["tabA"][0:1, 0:1]
            m["tabB"] = C["tabB"][qt][0:1, 0:1]
        m["ropeqB"] = C["ropeqB"][qt]
        in_maps.append(m)
    if "nc" not in _CACHE:
        _CACHE["nc"] = build_program()
    res = run_bass_kernel_spmd(_CACHE["nc"], in_maps, core_ids=list(range(8)))
    _CACHE["last"] = res
    y_prompt = np.empty((8, 4096, D), np.float32)
    y_sample = np.empty((2, 8192, D), np.float32)
    for c in range(8):
        s, qt = c // 4, c % 4
        y_prompt[c][qA - 16] = np.asarray(res.results[c]["yA"], dtype=np.float32)
        y_sample[s][qB[qt] - 16] = np.asarray(res.results[c]["yB"], dtype=np.float32)
    return (y_prompt, y_sample)
```

```python
import contextlib
import numpy as np
import ml_dtypes
import concourse.bass as bass
import concourse.mybir as mybir
from concourse.bass_utils import run_bass_kernel_spmd

F32 = mybir.dt.float32
BF16 = mybir.dt.bfloat16
AF = mybir.ActivationFunctionType
ALU = mybir.AluOpType

D = 1024
NM = 16
DFF = 2816
EPS = 1e-6
SCALE = 192 ** -0.5
LA, LB = 4112, 8208
DEBUG = False
STOP = None
KLIM = None
SMALL_TABS = False
POOL_EW = "dve"
POOL_AT = "dve"


class Buf:
    __slots__ = ("name", "writer", "readers", "excl")

    def __init__(self, name="", excl=False):
        self.name = name
        self.writer = None
        self.readers = []
        self.excl = excl


class Op:
    __slots__ = ("eng", "fn", "deps", "need_inc", "val", "is_dma", "dsem", "dval")

    def __init__(self, eng, fn, is_dma=False):
        self.eng = eng
        self.fn = fn
        self.deps = []
        self.need_inc = False
        self.val = None
        self.is_dma = is_dma
        self.dsem = None
        self.dval = None


ENGS = ("pe", "act", "dve", "pool", "sp")


class Prog:
    def __init__(self, nc, n_dma_sems=48):
        self.nc = nc
        self.ops = {e: [] for e in ENGS}
        self.n_dma_sems = n_dma_sems
        self.dma_last = [None] * n_dma_sems
        self.dma_uses = [0] * n_dma_sems
        self.dma_n = {"sp": 0, "pool": 0, "act": 0}
        self.dma_rng = {"sp": (0, 24), "act": (24, 16), "pool": (40, n_dma_sems - 40)}
        self.all_bufs = []
        self.final_dmas = []

    def buf(self, name="", excl=False):
        b = Buf(name, excl)
        self.all_bufs.append(b)
        return b

    def _add_dep(self, op, prod):
        if prod is None or prod is op:
            return
        if (not prod.is_dma) and prod.eng == "pe" and op.eng == "pe" and not op.is_dma:
            return
        if not prod.is_dma:
            prod.need_inc = True
        op.deps.append(prod)

    @staticmethod
    def _flat(x):
        out = []
        for b in x:
            if isinstance(b, (list, tuple)):
                out.extend(Prog._flat(b))
            else:
                out.append(b)
        return out

    def op(self, eng, fn, reads=(), writes=(), dma=False):
        o = Op(eng, fn, is_dma=dma)
        reads = self._flat(reads)
        writes = self._flat(writes)
        xr = [b for b in reads if b.excl and b not in writes]
        if xr:
            reads = [b for b in reads if not b.excl]
            writes = list(writes) + xr
        for b in reads:
            self._add_dep(o, b.writer)
        for b in writes:
            self._add_dep(o, b.writer)
            for r in b.readers:
                self._add_dep(o, r)
        if dma:
            base, cnt = self.dma_rng[eng]
            k = base + self.dma_n[eng] % cnt
            self.dma_n[eng] += 1
            self._add_dep(o, self.dma_last[k])
            self.dma_last[k] = o
            self.dma_uses[k] += 1
            o.dsem = k
            o.dval = 16 * self.dma_uses[k]
        for b in reads:
            b.readers.append(o)
        for b in writes:
            b.writer = o
            b.readers = []
        self.ops[eng].append(o)
        return o

    def dma(self, eng, out_ap, in_ap, reads=(), writes=(), final=False):
        o = self.op(eng, lambda e: e.dma_start(out=out_ap, in_=in_ap), reads, writes, dma=True)
        if final:
            self.final_dmas.append(o)
        return o

    def barrier(self):
        lasts = []
        for e in ENGS:
            for o in reversed(self.ops[e]):
                if not o.is_dma and o.fn is not None:
                    lasts.append(o)
                    break
        for k in range(self.n_dma_sems):
            if self.dma_last[k] is not None:
                lasts.append(self.dma_last[k])
        for e in ENGS:
            o = Op(e, None)
            for p in lasts:
                if (not p.is_dma) and p.eng == e and e == "pe":
                    continue
                if not p.is_dma:
                    p.need_inc = True
                o.deps.append(p)
            self.ops[e].append(o)
        for b in self.all_bufs:
            b.writer = None
            b.readers = []

    def emit(self):
        nc = self.nc
        fin = Op("sp", None)
        for o in self.final_dmas:
            fin.deps.append(o)
        for k in range(self.n_dma_sems):
            if self.dma_last[k] is not None:
                fin.deps.append(self.dma_last[k])
        self.ops["sp"].append(fin)
        for e in ENGS:
            c = 0
            for o in self.ops[e]:
                if o.is_dma:
                    continue
                if o.need_inc:
                    c += 1
                o.val = c
        with contextlib.ExitStack() as st:
            esem = {e: st.enter_context(nc.semaphore("s_" + e)) for e in ENGS}
            dsem = [st.enter_context(nc.semaphore("d%d" % k)) for k in range(self.n_dma_sems)]
            block = st.enter_context(nc.Block())
            ops = self.ops

            def run(e, eng):
                waited = {}
                for o in ops[e]:
                    for p in o.deps:
                        if p.is_dma:
                            key, sem, val = ("d", p.dsem), dsem[p.dsem], p.dval
                        else:
                            key, sem, val = ("e", p.eng), esem[p.eng], p.val
                        if waited.get(key, 0) >= val:
                            continue
                        waited[key] = val
                        eng.wait_ge(sem, val)
                    if o.fn is None:
                        continue
                    ins = o.fn(eng)
                    if o.is_dma:
                        ins.then_inc(dsem[o.dsem], 16)
                    elif o.need_inc:
                        ins.then_inc(esem[e], 1)

            @block.tensor
            def _(eng):
                run("pe", eng)

            @block.scalar
            def _(eng):
                run("act", eng)

            @block.vector
            def _(eng):
                run("dve", eng)

            @block.gpsimd
            def _(eng):
                run("pool", eng)

            @block.sync
            def _(eng):
                run("sp", eng)


class Rot:
    def __init__(self, P, aps, excl=False):
        self.items = [(ap, P.buf("", excl)) for ap in aps]
        self.i = 0

    def next(self):
        it = self.items[self.i % len(self.items)]
        self.i += 1
        return it


def job_geom(L):
    half = L // 2
    Fp = np.arange(1, half)
    Mp = L - Fp
    n_full = len(Fp) // 256
    rem = len(Fp) - 256 * n_full
    assert rem == 7
    korder = []
    sblocks = []
    for s in range(n_full):
        korder += [Fp[256 * s:256 * s + 256], Mp[256 * s:256 * s + 256]]
        sblocks += [Fp[256 * s:256 * s + 128], Fp[256 * s + 128:256 * s + 256]]
    korder += [Fp[-rem:], np.array([0, half]), Mp[-rem:]]
    sblocks += [np.concatenate([Fp[-rem:], np.array([0, half])])]
    korder = np.concatenate(korder)
    assert len(korder) == L and len(set(korder.tolist())) == L
    return dict(L=L, n_full=n_full, korder=korder, sblocks=sblocks, nb=2 * n_full + 1)


GEOM = {"A": job_geom(LA), "B": job_geom(LB)}
NGROUPS = {"A": 1, "B": 1}
QN = {"A": 4096, "B": 2048}
GBASE = {"A": 0, "B": 2}


def build_program():
    nc = bass.Bass("TRN2", target_bir_lowering=False)
    P = Prog(nc)

    def din(name, shape, dt=F32):
        return nc.dram_tensor(name, shape, dt, kind="ExternalInput").ap()

    def dscr(name, shape, dt=BF16):
        kind = "ExternalOutput" if DEBUG else "Internal"
        return nc.dram_tensor(name, shape, dt, kind=kind).ap()

    xk = {"A": din("xkA", [LA, D]), "B": din("xkB", [LB, D])}
    xq = {"A": din("xqA", [4096, D]), "B": din("xqB", [2048, D])}
    ropek = {"A": din("ropekA", [2, 64, LA]), "B": din("ropekB", [2, 64, LB])}
    ropeq = {"A": din("ropeqA", [2, 64, 4096]), "B": din("ropeqB", [2, 64, 2048])}
    if SMALL_TABS:
        tab = {"A": din("tabA", [1, 1, 128, 1024], BF16), "B": din("tabB", [1, 1, 128, 1024], BF16)}
    else:
        tab = {"A": din("tabA", [4, GEOM["A"]["nb"], 128, 1024], BF16),
               "B": din("tabB", [2, GEOM["B"]["nb"], 128, 1024], BF16)}
    tabs_small = {"A": din("tabAs", [GEOM["A"]["nb"], 128, 16], BF16),
                  "B": din("tabBs", [GEOM["B"]["nb"], 128, 4], BF16)}
    c128_d = din("c128", [128, 384], BF16)
    g1_d = din("g1", [128, 8])
    gq_d = din("gq", [128, 4])
    gkv_d = din("gkv", [128, 2])
    g2_d = din("g2", [128, 8])
    gfin_d = din("gfin", [128, D])
    w_in_d = din("w_in", [D, 3392])
    w_uq_d = din("w_uq", [512, 1536])
    w_ukv_d = din("w_ukv", [256, 2048])
    w_fo_d = din("w_fo", [512, D])
    w_ao_d = din("w_ao", [D, D])
    w_o_d = din("w_o", [D, D])
    w_g_d = din("w_g", [D, DFF])
    w_u_d = din("w_u", [D, DFF])
    w_d_d = din("w_d", [DFF, D])
    yA = nc.dram_tensor("yA", [4096, D], F32, kind="ExternalOutput").ap()
    yB = nc.dram_tensor("yB", [2048, D], F32, kind="ExternalOutput").ap()

    s_wk = dscr("s_wk", [D, 896])
    s_wq = dscr("s_wq", [D, 512])
    s_wgt = dscr("s_wgt", [D, 2048])
    s_wuq = dscr("s_wuq", [512, 2048])
    s_wukv = dscr("s_wukv", [256, 2048])
    s_wfo = dscr("s_wfo", [512, D])
    s_wao = dscr("s_wao", [D, D])
    s_wo = dscr("s_wo", [D, D])
    s_wg = dscr("s_wg", [D, DFF])
    s_wu = dscr("s_wu", [D, DFF])
    s_wd = dscr("s_wd", [DFF, D])
    FTs = dscr("FTs", [3, 128, 4, 2048])
    ATs = dscr("ATs", [3, 128, 8, 2048])
    x1s = dscr("x1s", [6144, D], F32)
    scr_bufs = {}

    def sb(name):
        if name not in scr_bufs:
            scr_bufs[name] = P.buf(name)
        return scr_bufs[name]

    def MM(out, lhsT, rhs, start, stop, reads, writes):
        P.op("pe", lambda e: e.matmul(out, lhsT=lhsT, rhs=rhs, start=start, stop=stop), reads, writes)

    def TR(out, in_, ident, reads, writes):
        P.op("pe", lambda e: e.transpose(out=out, in_=in_, identity=ident), reads, writes)

    def ACT(out, in_, func, reads, writes, scale=1.0, bias=0.0, accum_out=None):
        if accum_out is None:
            P.op("act", lambda e: e.activation(out=out, in_=in_, func=func, scale=scale, bias=bias), reads, writes)
        else:
            P.op("act", lambda e: e.activation(out=out, in_=in_, func=func, scale=scale, bias=bias,
                                               accum_out=accum_out), reads, writes)

    def ACTS(out, in_, func, scale_ap, reads, writes):
        P.op("act", lambda e: e.activation(out=out, in_=in_, func=func, scale=scale_ap), reads, writes)

    def COPY(eng, out, in_, reads, writes):
        if eng == "act":
            P.op("act", lambda e: e.copy(out=out, in_=in_), reads, writes)
        else:
            P.op(eng, lambda e: e.tensor_copy(out=out, in_=in_), reads, writes)

    def TT(eng, out, in0, in1, op, reads, writes):
        P.op(eng, lambda e: e.tensor_tensor(out=out, in0=in0, in1=in1, op=op), reads, writes)

    def RECIP(out, in_, reads, writes):
        P.op("dve", lambda e: e.reciprocal(out=out, in_=in_), reads, writes)

    def TSMUL(out, in0, sc, reads, writes):
        P.op("dve", lambda e: e.tensor_scalar_mul(out=out, in0=in0, scalar1=sc), reads, writes)

    evac_ctr = [0]

    def EVAC(out, in_, reads, writes, pref=None):
        evac_ctr[0] += 1
        COPY(pref or ("act" if evac_ctr[0] % 2 else "dve"), out, in_, reads, writes)

    def sbuf(st, name, shape, dt):
        return st.enter_context(nc.sbuf_tensor(name, shape, dt)).ap()

    def psum(st, name, shape, dt=F32):
        return st.enter_context(nc.psum_tensor(name, shape, dt)).ap()

    uid = [0]

    def nm(s):
        uid[0] += 1
        return "%s_%d" % (s, uid[0])

    def load_w(st, scr, R, C, name, nsplit=1):
        t = sbuf(st, nm(name), [128, R // 128, C], BF16)
        src = scr.rearrange("(c p) n -> p c n", p=128)
        nch = R // 128
        bl = []
        for c0 in range(nch):
            b = P.buf(name)
            P.dma("sp", t[:, c0, :], src[:, c0, :], writes=[b])
            bl.append(b)
        return t, bl

    with contextlib.ExitStack() as gst:
        ident = sbuf(gst, "ident", [128, 128], BF16)
        identf = sbuf(gst, "identf", [128, 128], F32)
        ones_bf = sbuf(gst, "ones_bf", [128, 128], BF16)
        ones32 = sbuf(gst, "ones32", [128, 128], F32)
        c128 = sbuf(gst, "c128s", [128, 384], BF16)
        cb = P.buf("const")
        P.op("pool", lambda e: e.memset(identf, 0.0), writes=[cb])
        P.op("pool", lambda e: e.affine_select(out=identf, in_=identf, pattern=[[-1, 128]],
                                               compare_op=ALU.not_equal, fill=1.0, base=0,
                                               channel_multiplier=1), reads=[cb], writes=[cb])
        P.op("dve", lambda e: e.tensor_copy(out=ident, in_=identf), reads=[cb], writes=[cb])
        P.op("pool", lambda e: e.memset(ones32, 1.0), writes=[cb])
        P.op("dve", lambda e: e.tensor_copy(out=ones_bf, in_=ones32), reads=[cb], writes=[cb])
        P.dma("sp", c128, c128_d, writes=[cb])

        class NT:
            def __init__(self, st, nx, ncol=512, npt=2):
                self.xs = Rot(P, [sbuf(st, nm("x"), [128, D], F32) for _ in range(nx)])
                self.stt = Rot(P, [sbuf(st, nm("st"), [128, 2], F32) for _ in range(12)])
                self.xn = Rot(P, [sbuf(st, nm("xn"), [128, D], BF16) for _ in range(4)])
                self.junk = sbuf(st, nm("junk"), [128, D], BF16)
                self.junkb = P.buf()
                self.pT = Rot(P, [psum(st, nm("pT"), [128, 8, 128], BF16) for _ in range(npt)], True)
                self.hTa = [sbuf(st, nm("hT"), [128, 8, ncol], BF16) for _ in range(2)]
                self.hTbs = [[P.buf() for _ in range(4)] for _ in range(2)]
                self.hi = 0

            def prep(self, tiles):
                stg = []
                for ti, (src, r) in enumerate(tiles):
                    x, xb = self.xs.next()
                    P.dma("sp", x[0:r, :], src, writes=[xb])
                    s, sbf = self.stt.next()
                    ACT(self.junk[0:r, :], x[0:r, :], AF.Square, [xb], [self.junkb, sbf], accum_out=s[0:r, 0:1])
                    ACT(s[0:r, 1:2], s[0:r, 0:1], AF.Sqrt, [sbf], [sbf], scale=1.0 / D, bias=EPS)
                    stg.append([x, xb, r, s, sbf, None, None])
                for e_ in stg:
                    x, xb, r, s, sbf = e_[0:5]
                    xn, xnb = self.xn.next()
                    RECIP(s[0:r, 1:2], s[0:r, 1:2], [sbf], [sbf])
                    TSMUL(xn[0:r, :], x[0:r, :], s[0:r, 1:2], [xb, sbf], [xnb])
                    e_[5], e_[6] = xn, xnb
                return stg

            def finish(self, stg):
                hT, hTb = self.hTa[self.hi % 2], self.hTbs[self.hi % 2]
                self.hi += 1
                col = 0
                xl = []
                for ti, (x, xb, r, s, sbf, xn, xnb) in enumerate(stg):
                    pT, pTb = self.pT.next()
                    for c in range(8):
                        TR(pT[:, c, 0:r], xn[0:r, c * 128:(c + 1) * 128], ident[0:r, 0:r], [xnb, cb], [pTb])
                    EVAC(hT[:, :, col:col + r], pT[:, :, 0:r], [pTb], [hTb[ti]])
                    xl.append((x, xb, r))
                    col += r
                return hT, hTb, col, xl

            def run(self, tiles):
                return self.finish(self.prep(tiles))

        gains = {}
        for name, gd, n in (("g1", g1_d, 8), ("gq", gq_d, 4), ("gkv", gkv_d, 2), ("g2", g2_d, 8)):
            t = sbuf(gst, name + "_sb", [128, n], F32)
            b = P.buf(name)
            P.dma("sp", t, gd, writes=[b])
            gains[name] = (t, b)
        wctr = [0]

        def conv_chunk(stage, obuf, engs, stq, dst, src, rc, pieces, gain, d_lo, d_hi):
            lo = min(p[1] for p in pieces)
            hi = max(p[1] + p[2] for p in pieces)
            stg, stb = stage.next()
            P.dma("sp", stg[:, 0:hi - lo], src[rc * 128:(rc + 1) * 128, lo:hi], writes=[stb])
            ob, obb = obuf.next()
            for (d0, s0, w) in pieces:
                wctr[0] += 1
                eng = engs[wctr[0] % len(engs)]
                o_ap = ob[:, d0 - d_lo:d0 - d_lo + w]
                i_ap = stg[:, s0 - lo:s0 - lo + w]
                if gain is None:
                    COPY(eng, o_ap, i_ap, [stb], [obb])
                else:
                    gt, gb = gains[gain]
                    if eng == "act":
                        ACTS(o_ap, i_ap, AF.Copy, gt[:, rc:rc + 1], [stb, gb], [obb])
                    else:
                        TSMUL(o_ap, i_ap, gt[:, rc:rc + 1], [stb, gb], [obb])
            P.dma(stq, dst[rc * 128:(rc + 1) * 128, d_lo:d_hi], ob[:, 0:d_hi - d_lo], reads=[obb])

        def conv_tasks(dst, Cd, src, R, pieces, gain, maxw=None):
            tasks = []
            if maxw is None:
                for rc in range(R // 128):
                    tasks.append((dst, src, rc, pieces, gain, 0, Cd))
            else:
                assert len(pieces) == 1
                d0, s0, w = pieces[0]
                nsp = (w + maxw - 1) // maxw
                step = (w + nsp - 1) // nsp
                for rc in range(R // 128):
                    for o in range(0, w, step):
                        ww = min(step, w - o)
                        tasks.append((dst, src, rc, [(d0 + o, s0 + o, ww)], gain, d0 + o, d0 + o + ww))
            return tasks

        pcs = []
        for h in range(8):
            pcs += [(256 * h, 192 * h, 192), (256 * h + 192, 192 * h + 160, 32), (256 * h + 224, 192 * h + 128, 32)]
        fg_tasks = (conv_tasks(s_wk, 896, w_in_d, D,
                               [(0, 0, 512), (512, 1024, 256), (768, 1280, 64), (832, 1312, 32), (864, 1280, 32)], "g1")
                    + conv_tasks(s_wq, 512, w_in_d, D, [(0, 512, 512)], "g1")
                    + conv_tasks(s_wuq, 2048, w_uq_d, 512, pcs, "gq")
                    + conv_tasks(s_wukv, 2048, w_ukv_d, 256, [(0, 0, 2048)], "gkv"))
        BGW = 704
        bg_tasks = (conv_tasks(s_wgt, 2048, w_in_d, D, [(0, 1344, 2048)], "g1", BGW)
                    + conv_tasks(s_wfo, D, w_fo_d, 512, [(0, 0, D)], None, BGW)
                    + conv_tasks(s_wao, D, w_ao_d, D, [(0, 0, D)], None, BGW)
                    + conv_tasks(s_wo, D, w_o_d, D, [(0, 0, D)], None, BGW)
                    + conv_tasks(s_wg, DFF, w_g_d, D, [(0, 0, DFF)], "g2", BGW)
                    + conv_tasks(s_wu, DFF, w_u_d, D, [(0, 0, DFF)], "g2", BGW)
                    + conv_tasks(s_wd, D, w_d_d, DFF, [(0, 0, D)], None, BGW))
        with contextlib.ExitStack() as st:
            stage = Rot(P, [sbuf(st, nm("wst"), [128, 2048], F32) for _ in range(3)])
            obuf = Rot(P, [sbuf(st, nm("wob"), [128, 2048], BF16) for _ in range(3)])
            for tsk in fg_tasks:
                conv_chunk(stage, obuf, ("act", "dve"), "act", *tsk)
        P.barrier()
        if STOP == "W":
            P.emit()
            return nc

        for job in ("A", "B"):
            G = GEOM[job]
            L, n_full, nb = G["L"], G["n_full"], G["nb"]
            nst = n_full + 1
            with contextlib.ExitStack() as jst:
                ckvT = sbuf(jst, nm("ckvT"), [128, 2, L], BF16)
                kropeT = sbuf(jst, nm("kropeT"), [128, L], BF16)
                krzb = P.buf()
                P.op("dve", lambda e, t=kropeT: e.memset(t, 0.0), writes=[krzb])
                ckvb = [P.buf() for _ in range(nst)]
                krb = [P.buf() for _ in range(nst)]
                with contextlib.ExitStack() as abst:
                    AB = sbuf(abst, nm("AB"), [128, nb, 1024], BF16)
                    ABb = [[P.buf(), P.buf()] for _ in range(nb)]
                    if DEBUG:
                        P.op("pool", lambda e, AB=AB, nb=nb: e.memset(AB[:, nb - 1, :], 0.0), writes=[ABb[nb - 1]])
                    with contextlib.ExitStack() as st:
                        nt = NT(st, 4)
                        wk, wkb = load_w(st, s_wk, D, 896, "wk")
                        mm = Rot(P, [psum(st, nm("mm"), [128, 512]) for _ in range(4)], True)
                        abp = Rot(P, [psum(st, nm("abp"), [128, 512]) for _ in range(2)], True)
                        uTa = [sbuf(st, nm("uT"), [128, 4, 512], BF16) for _ in range(2)]
                        uTbs = [[P.buf() for _ in range(4)] for _ in range(2)]
                        uti = [0]
                        uTt2 = sbuf(st, nm("uTt"), [128, 128], BF16)
                        uTt = uTt2.rearrange("p (g n) -> p g n", g=4)
                        uTtb = [P.buf() for _ in range(4)]
                        P.op("dve", lambda e, t=uTt2: e.memset(t, 0.0), writes=[uTtb])
                        sqr = Rot(P, [sbuf(st, nm("sq"), [128, 2, 512], BF16) for _ in range(2)])
                        crr = Rot(P, [sbuf(st, nm("cr"), [128, 2, 512], F32) for _ in range(1)])
                        Rr = Rot(P, [sbuf(st, nm("R"), [128, 512], F32) for _ in range(1)])
                        rcr = Rot(P, [sbuf(st, nm("rc"), [64, 2, 512], F32) for _ in range(2)])
                        t12 = Rot(P, [sbuf(st, nm("t12"), [64, 2, 512], F32) for _ in range(1)])
                        def ktiles(s_):
                            r0_ = 512 * s_
                            if s_ == n_full:
                                return [(xk[job][r0_:r0_ + 16, :], 16)]
                            return [(xk[job][r0_ + 128 * t:r0_ + 128 * (t + 1), :], 128) for t in range(4)]

                        slist = [s_ for s_ in range(nst) if KLIM is None or s_ in KLIM]
                        pre = nt.prep(ktiles(slist[0]))
                        for si, s in enumerate(slist):
                            tail = s == n_full
                            r0 = 512 * s
                            hT, hTb, n, _ = nt.finish(pre)
                            if si + 1 < len(slist):
                                pre = nt.prep(ktiles(slist[si + 1]))
                            if tail:
                                u, ub = uTt, uTtb
                            else:
                                u, ub = uTa[uti[0] % 2], uTbs[uti[0] % 2]
                                uti[0] += 1
                            for g in range(4):
                                ps, psb = mm.next()
                                for c in range(8):
                                    MM(ps[:, 0:n], wk[:, c, g * 128:(g + 1) * 128], hT[:, c, 0:n], c == 0, c == 7,
                                       [hTb, wkb[c]], [psb])
                                EVAC(u[:, g, 0:n], ps[:, 0:n], [psb], [ub[g]])
                            sq, sqb = sqr.next()
                            cr, crb = crr.next()
                            for c2 in range(2):
                                ps, psb = mm.next()
                                for c in range(8):
                                    MM(ps[:, 0:n], wk[:, c, 512 + c2 * 128:512 + (c2 + 1) * 128], hT[:, c, 0:n],
                                       c == 0, c == 7, [hTb, wkb[c]], [psb])
                                ACT(sq[:, c2, 0:n], ps[:, 0:n], AF.Square, [psb], [sqb])
                                COPY("dve", cr[:, c2, 0:n], ps[:, 0:n], [psb], [crb])
                            rc, rcb = rcr.next()
                            P.dma("sp", rc[:, :, 0:n], ropek[job][:, :, r0:r0 + n].rearrange("a p n -> p a n"),
                                  writes=[rcb])
                            tt, ttb = t12.next()
                            for j in range(2):
                                ps, psb = mm.next()
                                for c in range(8):
                                    MM(ps[0:64, 0:n], wk[:, c, 768 + 64 * j:832 + 64 * j], hT[:, c, 0:n],
                                       c == 0, c == 7, [hTb, wkb[c]], [psb])
                                TT("dve", tt[:, j, 0:n], ps[0:64, 0:n], rc[:, j, 0:n], ALU.mult, [psb, rcb], [ttb])
                            TT(POOL_EW, kropeT[0:64, r0:r0 + n], tt[:, 0, 0:n], tt[:, 1, 0:n], ALU.add, [ttb, krzb], [krb[s]])
                            if tail:
                                blks = [(2 * n_full, 0, 9, 9)]
                            else:
                                blks = [(2 * s, 0, 256, 128), (2 * s + 1, 128, 384, 128)]
                            for (blk, f0, m0, m) in blks:
                                for hf, (rhs_f, rhs_m) in enumerate(((c128[:, 0:128], c128[:, 0:128]),
                                                                     (c128[:, 128:256], c128[:, 256:384]))):
                                    ab, abb = abp.next()
                                    for g in range(4):
                                        o_ap = ab[0:m, g * 128:(g + 1) * 128]
                                        MM(o_ap, u[:, g, f0:f0 + m], rhs_f, True, False, [ub[g], cb], [abb])
                                        MM(o_ap, u[:, g, m0:m0 + m], rhs_m, False, True, [ub[g], cb], [abb])
                                    EVAC(AB[0:m, blk, hf * 512:(hf + 1) * 512], ab[0:m, :], [abb], [ABb[blk][hf]])
                            ps, psb = mm.next()
                            for c2 in range(2):
                                MM(ps[:, 0:n], ones_bf, sq[:, c2, 0:n], c2 == 0, c2 == 1, [sqb, cb], [psb])
                            R, Rb = Rr.next()
                            ACT(R[:, 0:n], ps[:, 0:n], AF.Sqrt, [psb], [Rb], scale=1.0 / 256, bias=EPS)
                            RECIP(R[:, 0:n], R[:, 0:n], [Rb], [Rb])
                            for c2 in range(2):
                                TT("dve", ckvT[:, c2, r0:r0 + n], cr[:, c2, 0:n], R[:, 0:n], ALU.mult,
                                   [crb, Rb], [ckvb[s]])
                    P.barrier()
                    if STOP == "K" + job:
                        dA = nc.dram_tensor("dbg_AB", [128, nb, 1024], BF16, kind="ExternalOutput").ap()
                        dC = nc.dram_tensor("dbg_ckv", [128, 2, L], BF16, kind="ExternalOutput").ap()
                        dK = nc.dram_tensor("dbg_kr", [128, L], BF16, kind="ExternalOutput").ap()
                        P.dma("sp", dA, AB, final=True)
                        P.dma("sp", dC, ckvT, final=True)
                        P.dma("sp", dK, kropeT, final=True)
                        P.emit()
                        return nc
                    nfc = 4 if job == "A" else 2
                    Ns = 8 if job == "A" else 2
                    with contextlib.ExitStack() as st:
                        if job == "A":
                            FT0 = sbuf(st, nm("FT0"), [128, 4, 2048], BF16)
                            FT1 = sbuf(st, nm("FT1"), [128, 4, 2048], BF16)
                            FT0b, FT1b = P.buf(), P.buf()
                            fdst = lambda g, j: (FT0[:, g, j * 512:(j + 1) * 512], FT0b)
                            mdst = lambda g, j: (FT1[:, g, j * 512:(j + 1) * 512], FT1b)
                            sdst = lambda g: (FT1[:, g, 2048 - Ns:2048], FT1b)
                        else:
                            FT0 = sbuf(st, nm("FT0"), [128, 4, 2048], BF16)
                            FT0b = P.buf()
                            FT1b = P.buf()
                            fdst = lambda g, j: (FT0[:, g, j * 512:(j + 1) * 512], FT0b)
                            mdst = lambda g, j: (FT0[:, g, 1024 + j * 512:1024 + (j + 1) * 512], FT1b)
                            sdst = lambda g: (FT0[:, g, 2048 - Ns:2048], FT1b)
                        tabs = Rot(P, [sbuf(st, nm("tab"), [128, 4, 1024], BF16) for _ in range(3)])
                        tsm = sbuf(st, nm("tsm"), [128, nb, 2 * Ns], BF16)
                        tsmb = P.buf()
                        P.dma("sp", tsm, tabs_small[job].rearrange("b p n -> p b n"), writes=[tsmb])
                        p1R = Rot(P, [sbuf(st, nm("p1sb"), [128, 512], F32) for _ in range(2)])
                        fps = psum(st, nm("fps"), [128, 8, 512])
                        fpb = [P.buf("", True) for _ in range(8)]
                        for j in range(nfc):
                            for s4 in range(0, 2 * n_full + 1, 4):
                                tail = s4 == 2 * n_full
                                T, Tb = tabs.next()
                                if tail:
                                    P.dma("sp", T[:, 0, :], tab[job][j, 2 * n_full, :, :], writes=[Tb])
                                    blks = [(2 * n_full, 0, 9)]
                                else:
                                    P.dma("sp", T, tab[job][j, s4:s4 + 4, :, :].rearrange("b p n -> p b n"),
                                          writes=[Tb])
                                    blks = [(s4 + bi, bi, 128) for bi in range(4)]
                                for (blk, bi, m) in blks:
                                    for g in range(4):
                                        MM(fps[:, g, :], AB[0:m, blk, g * 128:(g + 1) * 128], T[0:m, bi, 0:512],
                                           blk == 0, blk == nb - 1, [ABb[blk][0], Tb], [fpb[g]])
                                        MM(fps[:, 4 + g, :], AB[0:m, blk, 512 + g * 128:512 + (g + 1) * 128],
                                           T[0:m, bi, 512:1024], blk == 0, blk == nb - 1, [ABb[blk][1], Tb],
                                           [fpb[4 + g]])
                            for g in range(4):
                                p1, p1b = p1R.next()
                                COPY("act", p1, fps[:, g, :], [fpb[g]], [p1b])
                                d_ap, d_b = fdst(g, j)
                                TT("dve", d_ap, fps[:, 4 + g, :], p1, ALU.add, [fpb[4 + g], p1b], [d_b])
                                d_ap, d_b = mdst(g, j)
                                TT("dve", d_ap, p1, fps[:, 4 + g, :], ALU.subtract, [fpb[4 + g], p1b], [d_b])
                        for g in range(4):
                            reg = fps[:, 0, g * Ns:(g + 1) * Ns]
                            for blk in range(nb):
                                m = 9 if blk == nb - 1 else 128
                                MM(reg, AB[0:m, blk, g * 128:(g + 1) * 128], tsm[0:m, blk, 0:Ns],
                                   blk == 0, False, [ABb[blk][0], tsmb], [fpb[0]])
                                MM(reg, AB[0:m, blk, 512 + g * 128:512 + (g + 1) * 128], tsm[0:m, blk, Ns:2 * Ns],
                                   False, blk == nb - 1, [ABb[blk][1], tsmb], [fpb[0]])
                        for g in range(4):
                            d_ap, d_b = sdst(g)
                            COPY("act", d_ap, fps[:, 0, g * Ns:(g + 1) * Ns], [fpb[0]], [d_b])
                        if job == "A":
                            P.dma("sp", FTs[0], FT0, reads=[FT0b])
                            P.dma("sp", FTs[1], FT1, reads=[FT1b])
                        else:
                            P.dma("sp", FTs[2], FT0, reads=[FT0b, FT1b])
                    P.barrier()
                    if STOP == "D" + job:
                        P.emit()
                        return nc
                for gi in range(NGROUPS[job]):
                    gg = GBASE[job] + gi
                    QQ = QN[job]
                    nqc = QQ // 512
                    with contextlib.ExitStack() as gst2:
                        cqT = sbuf(gst2, nm("cqT"), [128, 4, QQ], BF16)
                        cqb = [P.buf() for _ in range(nqc)]
                        with contextlib.ExitStack() as st:
                            nt = NT(st, 5, 512, 4)
                            wq, wqb = load_w(st, s_wq, D, 512, "wq")
                            mm = Rot(P, [psum(st, nm("mm"), [128, 512]) for _ in range(4)], True)
                            sqr = Rot(P, [sbuf(st, nm("sq"), [128, 4, 512], BF16) for _ in range(2)])
                            crr = Rot(P, [sbuf(st, nm("cr"), [128, 4, 512], F32) for _ in range(2)])
                            Rr = Rot(P, [sbuf(st, nm("R"), [128, 512], F32) for _ in range(2)])
                            def qtiles(qc_):
                                q0_ = gi * 2048 + qc_ * 512
                                return [(xq[job][q0_ + 128 * t:q0_ + 128 * (t + 1), :], 128) for t in range(4)]

                            pre = nt.prep(qtiles(0))
                            for qc in range(nqc):
                                hT, hTb, n, _ = nt.finish(pre)
                                if qc + 1 < nqc:
                                    pre = nt.prep(qtiles(qc + 1))
                                sq, sqb = sqr.next()
                                cr, crb = crr.next()
                                for c4 in range(4):
                                    ps, psb = mm.next()
                                    for c in range(8):
                                        MM(ps, wq[:, c, c4 * 128:(c4 + 1) * 128], hT[:, c, :], c == 0, c == 7,
                                           [hTb, wqb[c]], [psb])
                                    ACT(sq[:, c4, :], ps, AF.Square, [psb], [sqb])
                                    COPY("dve", cr[:, c4, :], ps, [psb], [crb])
                                ps, psb = mm.next()
                                for c4 in range(4):
                                    MM(ps, ones_bf, sq[:, c4, :], c4 == 0, c4 == 3, [sqb, cb], [psb])
                                R, Rb = Rr.next()
                                ACT(R, ps, AF.Sqrt, [psb], [Rb], scale=1.0 / 512, bias=EPS)
                                RECIP(R, R, [Rb], [Rb])
                                for c4 in range(4):
                                    TT("dve", cqT[:, c4, qc * 512:(qc + 1) * 512], cr[:, c4, :], R, ALU.mult,
                                       [crb, Rb], [cqb[qc]])
                        P.barrier()
                        if STOP == "Q" + job:
                            dQ = nc.dram_tensor("dbg_cq", [128, 4, QQ], BF16, kind="ExternalOutput").ap()
                            P.dma("sp", dQ, cqT, final=True)
                            P.emit()
                            return nc
                        with contextlib.ExitStack() as st:
                            wuq, wuqb = load_w(st, s_wuq, 512, 2048, "wuq")
                            wukv, wukvb = load_w(st, s_wukv, 256, 2048, "wukv")
                            aoR = Rot(P, [sbuf(st, nm("ao"), [128, 512], BF16) for _ in range(4)])
                            nkv = 2 if job == "A" else 1
                            nkt = 4 * n_full + 1
                            KhR = Rot(P, [sbuf(st, nm("Kh"), [128, L], BF16) for _ in range(nkv)])
                            VhR = Rot(P, [sbuf(st, nm("Vh"), [128, nkt, 128], BF16) for _ in range(nkv)])
                            rq = sbuf(st, nm("rq"), [64, 2, QQ], F32)
                            rqb = P.buf()
                            P.dma("sp", rq, ropeq[job][:, :, 0:QQ].rearrange("a p n -> p a n"),
                                  writes=[rqb])
                            SR = Rot(P, [psum(st, nm("S"), [128, 2, 512]) for _ in range(2)], True)
                            poR = Rot(P, [psum(st, nm("po"), [128, 512]) for _ in range(2)], True)
                            mm = Rot(P, [psum(st, nm("mm"), [128, 512]) for _ in range(2)], True)
                            PTR = Rot(P, [sbuf(st, nm("PT"), [128, 2, 512], BF16) for _ in range(4)])
                            qnR = Rot(P, [sbuf(st, nm("qn"), [128, 512], BF16) for _ in range(2)])
                            qrR = Rot(P, [sbuf(st, nm("qr"), [128, 512], BF16) for _ in range(2)])
                            for (qr_, qrb_) in qrR.items:
                                P.op("dve", lambda e, t=qr_: e.memset(t, 0.0), writes=[qrb_])
                            t12 = Rot(P, [sbuf(st, nm("t12"), [64, 2, 512], F32) for _ in range(2)])
                            accR = Rot(P, [sbuf(st, nm("acc"), [128, 2, 512], F32) for _ in range(2)])
                            recR = Rot(P, [sbuf(st, nm("rec"), [128, 512], F32) for _ in range(2)])

                            def kvproj(h):
                                Kh, Khb = KhR.next()
                                Vh, Vhb = VhR.next()
                                for s in range(nst):
                                    n = 16 if s == n_full else 512
                                    r0 = 512 * s
                                    ps, psb = mm.next()
                                    for c in range(2):
                                        MM(ps[:, 0:n], wukv[:, c, h * 256:h * 256 + 128], ckvT[:, c, r0:r0 + n],
                                           c == 0, c == 1, [ckvb[s], wukvb[c]], [psb])
                                    EVAC(Kh[:, r0:r0 + n], ps[:, 0:n], [psb], [Khb], "act")
                                    ps, psb = mm.next()
                                    if s == n_full:
                                        for c in range(2):
                                            MM(ps[0:16, 0:128], ckvT[:, c, r0:r0 + 16],
                                               wukv[:, c, h * 256 + 128:h * 256 + 256], c == 0, c == 1,
                                               [ckvb[s], wukvb[c]], [psb])
                                        EVAC(Vh[0:16, 4 * n_full, :], ps[0:16, 0:128], [psb], [Vhb], "dve" if nkv == 1 else "act")
                                    else:
                                        for t in range(4):
                                            for c in range(2):
                                                MM(ps[:, t * 128:(t + 1) * 128],
                                                   ckvT[:, c, r0 + t * 128:r0 + (t + 1) * 128],
                                                   wukv[:, c, h * 256 + 128:h * 256 + 256], c == 0, c == 1,
                                                   [ckvb[s], wukvb[c]], [psb])
                                        EVAC(Vh[:, 4 * s:4 * s + 4, :], ps.rearrange("p (t d) -> p t d", t=4),
                                             [psb], [Vhb], "dve" if nkv == 1 else "act")
                                return Kh, Khb, Vh, Vhb

                            def qproj(h, qc):
                                q0 = qc * 512
                                ps, psb = mm.next()
                                for c in range(4):
                                    MM(ps, wuq[:, c, h * 256:h * 256 + 128], cqT[:, c, q0:q0 + 512],
                                       c == 0, c == 3, [cqb[qc], wuqb[c]], [psb])
                                qn, qnb = qnR.next()
                                COPY("act", qn, ps, [psb], [qnb])
                                tt, ttb = t12.next()
                                for j in range(2):
                                    ps, psb = mm.next()
                                    for c in range(4):
                                        MM(ps[0:64, :], wuq[:, c, h * 256 + 128 + 64 * j:h * 256 + 192 + 64 * j],
                                           cqT[:, c, q0:q0 + 512], c == 0, c == 3, [cqb[qc], wuqb[c]], [psb])
                                    TT("dve", tt[:, j, :], ps[0:64, :], rq[:, j, q0:q0 + 512], ALU.mult,
                                       [psb, rqb], [ttb])
                                qr, qrb = qrR.next()
                                TT(POOL_EW, qr[0:64, :], tt[:, 0, :], tt[:, 1, :], ALU.add, [ttb], [qrb])
                                return qn, qnb, qr, qrb

                            nfull_kb = 4 * n_full
                            npairs = nfull_kb // 2 + 1

                            def emit_S(i, Kh, Khb, qn, qnb, qr, qrb):
                                S, Sb = SR.next()
                                if i == npairs - 1:
                                    k0 = nfull_kb * 128
                                    MM(S[0:16, 0, :], Kh[:, k0:k0 + 16], qn, True, False, [Khb, qnb], [Sb])
                                    MM(S[0:16, 0, :], kropeT[:, k0:k0 + 16], qr, False, True, [krb[n_full], qrb], [Sb])
                                else:
                                    for j in range(2):
                                        k0 = (2 * i + j) * 128
                                        MM(S[:, j, :], Kh[:, k0:k0 + 128], qn, True, False, [Khb, qnb], [Sb])
                                        MM(S[:, j, :], kropeT[:, k0:k0 + 128], qr, False, True, [krb[k0 // 512], qrb], [Sb])
                                return S, Sb

                            def emit_exp(i, S, Sb, acc, accb):
                                PT, PTb = PTR.next()
                                if i == npairs - 1:
                                    ACT(PT[0:16, 0, :], S[0:16, 0, :], AF.Exp, [Sb], [PTb], scale=SCALE)
                                    TT("dve", acc[0:16, 0, :], acc[0:16, 0, :], PT[0:16, 0, :], ALU.add, [accb, PTb], [accb])
                                else:
                                    ACT(PT, S, AF.Exp, [Sb], [PTb], scale=SCALE)
                                    if i == 0:
                                        COPY("dve", acc, PT, [PTb], [accb])
                                    else:
                                        TT("dve", acc, acc, PT, ALU.add, [accb, PTb], [accb])
                                return PT, PTb

                            def emit_pv(i, PT, PTb, Vh, Vhb, po, pob):
                                kp = 2 * i
                                if i == npairs - 1:
                                    MM(po, Vh[0:16, kp, :], PT[0:16, 0, :], False, True, [Vhb, PTb], [pob])
                                else:
                                    for j in range(2):
                                        MM(po, Vh[:, kp + j, :], PT[:, j, :], (kp + j) == 0, False, [Vhb, PTb], [pob])

                            def finalize(h, qc, po, pob, acc, accb):
                                q0 = qc * 512
                                ps, psb = mm.next()
                                MM(ps, ones32, acc[:, 0, :], True, False, [accb, cb], [psb])
                                MM(ps, ones32, acc[:, 1, :], False, True, [accb, cb], [psb])
                                rec, recb = recR.next()
                                RECIP(rec, ps, [psb], [recb])
                                ao, aob = aoR.next()
                                TT("dve", ao, po, rec, ALU.mult, [pob, recb], [aob])
                                ggq = GBASE[job] + qc // 4
                                P.dma("sp", ATs[ggq, :, h, (qc % 4) * 512:(qc % 4 + 1) * 512], ao, reads=[aob])

                            if job == "A" and bg_tasks:
                                bstage = Rot(P, [sbuf(st, nm("bst"), [128, BGW], F32) for _ in range(4)])
                                bobuf = Rot(P, [sbuf(st, nm("bob"), [128, BGW], BF16) for _ in range(4)])

                            its = [(h, qc) for h in range(8) for qc in range(nqc)]
                            kv = {0: kvproj(0)}
                            q_next = qproj(0, 0)
                            pending = None
                            for idx, (h, qc) in enumerate(its):
                                qn, qnb, qr, qrb = q_next
                                Kh, Khb, Vh, Vhb = kv[h]
                                acc, accb = accR.next()
                                po, pob = poR.next()
                                nxt = its[idx + 1] if idx + 1 < len(its) else None
                                S_next = emit_S(0, Kh, Khb, qn, qnb, qr, qrb)
                                for i in range(npairs):
                                    S, Sb = S_next
                                    if i + 1 < npairs:
                                        S_next = emit_S(i + 1, Kh, Khb, qn, qnb, qr, qrb)
                                    if i == 1 and pending is not None:
                                        finalize(*pending)
                                        pending = None
                                    if i == 3 and nxt is not None:
                                        if nxt[0] != h and nkv == 2:
                                            kv[nxt[0]] = kvproj(nxt[0])
                                        q_next = qproj(nxt[0], nxt[1])
                                    PTcur = emit_exp(i, S, Sb, acc, accb)
                                    if i >= 1:
                                        emit_pv(i - 1, PTprev[0], PTprev[1], Vh, Vhb, po, pob)
                                    PTprev = PTcur
                                    if job == "A" and bg_tasks and i in (5, 9, 13):
                                        conv_chunk(bstage, bobuf, ("act",), "sp", *bg_tasks.pop(0))
                                emit_pv(npairs - 1, PTprev[0], PTprev[1], Vh, Vhb, po, pob)
                                pending = (h, qc, po, pob, acc, accb)
                                if nxt is not None and nxt[0] != h and nkv == 1:
                                    kv[nxt[0]] = kvproj(nxt[0])
                            finalize(*pending)
                        P.barrier()
                        if STOP == "At" + job:
                            P.emit()
                            return nc

        if bg_tasks:
            with contextlib.ExitStack() as st:
                stage = Rot(P, [sbuf(st, nm("wst"), [128, BGW], F32) for _ in range(3)])
                obuf = Rot(P, [sbuf(st, nm("wob"), [128, BGW], BF16) for _ in range(3)])
                while bg_tasks:
                    conv_chunk(stage, obuf, ("act", "dve"), "act", *bg_tasks.pop(0))
            P.barrier()
        with contextlib.ExitStack() as st:
            nt = NT(st, 8, 512, 4)
            wgt, wgtb = load_w(st, s_wgt, D, 2048, "wgt", 2)
            wfo, wfob = load_w(st, s_wfo, 512, D, "wfo")
            wao, waob = load_w(st, s_wao, D, D, "wao")
            wo, wob = load_w(st, s_wo, D, D, "wo")
            mm = Rot(P, [psum(st, nm("mm"), [128, 512]) for _ in range(4)], True)
            FTc = Rot(P, [sbuf(st, nm("FTc"), [128, 4, 512], BF16) for _ in range(2)])
            ATc = Rot(P, [sbuf(st, nm("ATc"), [128, 8, 512], BF16) for _ in range(2)])
            sgR = Rot(P, [sbuf(st, nm("sg"), [128, 2, 512], BF16) for _ in range(2)])
            tAB = Rot(P, [sbuf(st, nm("tAB"), [128, 2, 512], F32) for _ in range(2)])
            mTR = Rot(P, [sbuf(st, nm("mT"), [128, 8, 512], BF16) for _ in range(2)])
            x1R = Rot(P, [sbuf(st, nm("x1"), [128, D], F32) for _ in range(5)])
            pend = []
            def t1tiles(ci_):
                gg_, qc_ = ci_ // 4, ci_ % 4
                job_ = "A" if gg_ < 2 else "B"
                gi_ = gg_ if gg_ < 2 else 0
                q0_ = gi_ * 2048 + qc_ * 512
                return [(xq[job_][q0_ + 128 * t:q0_ + 128 * (t + 1), :], 128) for t in range(4)]

            pre = nt.prep(t1tiles(0))
            for gg in range(3):
                for qc in range(4):
                    hT, hTb, n, xl = nt.finish(pre)
                    if gg * 4 + qc + 1 < 12:
                        pre = nt.prep(t1tiles(gg * 4 + qc + 1))
                    for (d_, s_, b_) in pend:
                        P.dma("sp", d_, s_, reads=[b_])
                    pend = []
                    ft, ftb = FTc.next()
                    P.dma("sp", ft, FTs[gg, :, :, qc * 512:(qc + 1) * 512], writes=[ftb])
                    at, atb = ATc.next()
                    P.dma("sp", at, ATs[gg, :, :, qc * 512:(qc + 1) * 512], writes=[atb])
                    mT, mTb = mTR.next()
                    for j in range(8):
                        sg, sgb = sgR.next()
                        for k in range(2):
                            ps, psb = mm.next()
                            for c in range(8):
                                MM(ps, wgt[:, c, k * 1024 + j * 128:k * 1024 + (j + 1) * 128], hT[:, c, :],
                                   c == 0, c == 7, [hTb, wgtb[c]], [psb])
                            ACT(sg[:, k, :], ps, AF.Sigmoid, [psb], [sgb])
                        tab_, tabb = tAB.next()
                        ps, psb = mm.next()
                        for c in range(4):
                            MM(ps, wfo[:, c, j * 128:(j + 1) * 128], ft[:, c, :], c == 0, c == 3, [ftb, wfob[c]], [psb])
                        TT("dve", tab_[:, 0, :], ps, sg[:, 0, :], ALU.mult, [psb, sgb], [tabb])
                        ps, psb = mm.next()
                        for c in range(8):
                            MM(ps, wao[:, c, j * 128:(j + 1) * 128], at[:, c, :], c == 0, c == 7, [atb, waob[c]], [psb])
                        TT("dve", tab_[:, 1, :], ps, sg[:, 1, :], ALU.mult, [psb, sgb], [tabb])
                        TT(POOL_EW, mT[:, j, :], tab_[:, 0, :], tab_[:, 1, :], ALU.add, [tabb], [mTb])
                    for t in range(4):
                        x1, x1b = x1R.next()
                        x, xb, _r = xl[t]
                        for hh in range(2):
                            ps, psb = mm.next()
                            for c in range(8):
                                MM(ps, mT[:, c, t * 128:(t + 1) * 128], wo[:, c, hh * 512:(hh + 1) * 512],
                                   c == 0, c == 7, [mTb, wob[c]], [psb])
                            TT("dve", x1[:, hh * 512:(hh + 1) * 512], ps, x[:, hh * 512:(hh + 1) * 512], ALU.add,
                               [psb, xb], [x1b])
                        row = gg * 2048 + qc * 512 + t * 128
                        pend.append((x1s[row:row + 128, :], x1, x1b))
            for (d_, s_, b_) in pend:
                P.dma("sp", d_, s_, reads=[b_])
        P.barrier()
        if STOP == "T1":
            P.emit()
            return nc

        TC = 256
        with contextlib.ExitStack() as st:
            nt = NT(st, 4, TC)
            wg, wgb = load_w(st, s_wg, D, DFF, "wg", 2)
            wu, wub = load_w(st, s_wu, D, DFF, "wu", 2)
            wd, wdb = load_w(st, s_wd, DFF, D, "wd", 2)
            gfin = sbuf(st, "gfin_sb", [128, D], F32)
            gfb = P.buf()
            P.dma("sp", gfin, gfin_d, writes=[gfb])
            mm = Rot(P, [psum(st, nm("mm"), [128, 512]) for _ in range(6)], True)
            aTR = Rot(P, [sbuf(st, nm("aT"), [128, 22, TC], BF16)])
            slR = Rot(P, [sbuf(st, nm("sl"), [128, TC], F32) for _ in range(2)])
            x2R = Rot(P, [sbuf(st, nm("x2"), [128, D], F32) for _ in range(4)])
            pend = []
            stR = Rot(P, [sbuf(st, nm("st2"), [128, 2], F32) for _ in range(2)])
            junk2 = sbuf(st, nm("junk2"), [128, D], BF16)
            junk2b = P.buf()

            def STT(out, in0, sc, in1, reads, writes):
                P.op("dve", lambda e: e.scalar_tensor_tensor(out=out, in0=in0, scalar=sc, in1=in1,
                                                             op0=ALU.mult, op1=ALU.mult), reads, writes)

            nt2 = TC // 128
            def t2tiles(ci_):
                return [(x1s[ci_ * TC + 128 * t:ci_ * TC + 128 * (t + 1), :], 128) for t in range(nt2)]

            nch2 = 6144 // TC
            pre = nt.prep(t2tiles(0))
            for ci in range(nch2):
                row0 = ci * TC
                hT, hTb, n, xl = nt.finish(pre)
                if ci + 1 < nch2:
                    pre = nt.prep(t2tiles(ci + 1))
                for (d_, s_, b_) in pend:
                    P.dma("sp", d_, s_, reads=[b_], final=True)
                pend = []
                aT, aTb = aTR.next()
                for j in range(22):
                    pg, pgb = mm.next()
                    for c in range(8):
                        MM(pg[:, 0:TC], wg[:, c, j * 128:(j + 1) * 128], hT[:, c, :], c == 0, c == 7, [hTb, wgb[c]], [pgb])
                    pu, pub = mm.next()
                    for c in range(8):
                        MM(pu[:, 0:TC], wu[:, c, j * 128:(j + 1) * 128], hT[:, c, :], c == 0, c == 7, [hTb, wub[c]], [pub])
                    sl, slb = slR.next()
                    ACT(sl, pg[:, 0:TC], AF.Silu, [pgb], [slb])
                    TT("dve", aT[:, j, :], sl, pu[:, 0:TC], ALU.mult, [slb, pub], [aTb])
                for t in range(nt2):
                    x2, x2b = x2R.next()
                    x, xb, _r = xl[t]
                    for hh in range(2):
                        ps, psb = mm.next()
                        for j in range(22):
                            MM(ps, aT[:, j, t * 128:(t + 1) * 128], wd[:, j, hh * 512:(hh + 1) * 512],
                               j == 0, j == 21, [aTb, wdb[j]], [psb])
                        TT("dve", x2[:, hh * 512:(hh + 1) * 512], ps, x[:, hh * 512:(hh + 1) * 512], ALU.add,
                           [psb, xb], [x2b])
                    s2, s2b = stR.next()
                    ACT(junk2, x2, AF.Square, [x2b], [junk2b, s2b], accum_out=s2[:, 0:1])
                    ACT(s2[:, 1:2], s2[:, 0:1], AF.Sqrt, [s2b], [s2b], scale=1.0 / D, bias=EPS)
                    RECIP(s2[:, 1:2], s2[:, 1:2], [s2b], [s2b])
                    STT(x2, x2, s2[:, 1:2], gfin, [x2b, s2b, gfb], [x2b])
                    r_ = row0 + t * 128
                    if r_ < 4096:
                        dst = yA[r_:r_ + 128, :]
                    else:
                        dst = yB[r_ - 4096:r_ - 4096 + 128, :]
                    pend.append((dst, x2, x2b))
            for (d_, s_, b_) in pend:
                P.dma("sp", d_, s_, reads=[b_], final=True)
        P.emit()
    return nc


def _rope_tab(pos):
    inv = (1.0 / (10000.0 ** (np.arange(0, 64, 2, dtype=np.float32) / np.float32(64)))).astype(np.float32)
    ang = pos.astype(np.float32)[:, None] * inv[None, :]
    c = np.cos(ang).astype(np.float32).T
    s = np.sin(ang).astype(np.float32).T
    cc = np.concatenate([c, c], 0)
    ss = np.concatenate([-s, s], 0)
    return np.ascontiguousarray(np.stack([cc, ss], 0))


def _dft_tab(G, qpos, chunk=512):
    L = G["L"]
    nq = len(qpos)
    nqc = max(1, nq // chunk)
    w = nq // nqc
    out = np.zeros((nqc, G["nb"], 128, 2 * w), dtype=ml_dtypes.bfloat16)
    sc = 1.0 / np.sqrt(L)
    q = qpos.astype(np.int64)
    for b, sp in enumerate(G["sblocks"]):
        m = len(sp)
        prod = (sp.astype(np.int64)[:, None] * q[None, :]) % L
        ang = prod.astype(np.float64) * (2.0 * np.pi / L)
        cs = (np.cos(ang) * sc).reshape(m, nqc, w)
        sn = (-np.sin(ang) * sc).reshape(m, nqc, w)
        out[:, b, 0:m, 0:w] = cs.transpose(1, 0, 2).astype(ml_dtypes.bfloat16)
        out[:, b, 0:m, w:2 * w] = sn.transpose(1, 0, 2).astype(ml_dtypes.bfloat16)
    return out


def _qorders():
    fA = np.arange(16, LA // 2)
    sA = np.array([LA // 2] + list(range(LA - 15, LA)))
    qA = np.concatenate([fA, sA[0:8], LA - fA, sA[8:16]])
    assert len(qA) == 4096 and len(set(qA.tolist())) == 4096 and qA.min() == 16 and qA.max() == LA - 1
    fB = np.arange(16, LB // 2)
    sB = np.array([LB // 2] + list(range(LB - 15, LB)))
    qB = []
    for qt in range(4):
        f = fB[1022 * qt:1022 * (qt + 1)]
        sgl = sB[4 * qt:4 * qt + 4]
        qB.append(np.concatenate([f, sgl[0:2], LB - f, sgl[2:4]]))
    allB = np.concatenate(qB)
    assert len(allB) == 8192 and len(set(allB.tolist())) == 8192 and allB.min() == 16 and allB.max() == LB - 1
    return qA, qB


_CACHE = {}


def _consts():
    if "c" in _CACHE:
        return _CACHE["c"]
    k = np.arange(128)
    ang = 2.0 * np.pi * ((k[:, None] * k[None, :]) % 128) / 128.0
    sc = 1.0 / np.sqrt(128.0)
    c128 = np.concatenate([np.cos(ang) * sc, np.sin(ang) * sc, -np.sin(ang) * sc], 1).astype(ml_dtypes.bfloat16)
    GA, GB = GEOM["A"], GEOM["B"]
    qA, qB = _qorders()
    tabA = _dft_tab(GA, qA[0:2048])
    tabAs = np.ascontiguousarray(_dft_tab(GA, qA[4088:4096], chunk=8)[0])
    tabB = [_dft_tab(GB, qB[qt][0:1024]) for qt in range(4)]
    tabBs = [np.ascontiguousarray(_dft_tab(GB, qB[qt][2046:2048], chunk=2)[0]) for qt in range(4)]
    ropekA = _rope_tab(GA["korder"])
    ropekB = _rope_tab(GB["korder"])
    ropeqA = _rope_tab(qA)
    ropeqB = [_rope_tab(qB[qt]) for qt in range(4)]
    _CACHE["c"] = dict(c128=c128, tabA=tabA, tabB=tabB, tabAs=tabAs, tabBs=tabBs, ropekA=ropekA, ropekB=ropekB,
                       ropeqA=ropeqA, ropeqB=ropeqB, qA=qA, qB=qB)
    return _CACHE["c"]


def kernel(x_prompt, x_sample, meta_tokens, norm1_g, w_in, q_norm_g, kv_norm_g, w_uq, w_ukv,
           w_fourier_out, w_attn_out, w_o, norm2_g, w_ffn_gate, w_ffn_up, w_ffn_down, final_norm_g):
    f = lambda a: np.ascontiguousarray(np.asarray(a, dtype=np.float32))
    x_prompt, x_sample, meta = f(x_prompt), f(x_sample), f(meta_tokens)
    C = _consts()
    gl = lambda g: np.ascontiguousarray(f(g).reshape(-1, 128).T)
    common = {
        "c128": C["c128"], "g1": gl(norm1_g[0]), "gq": gl(q_norm_g[0]), "gkv": gl(kv_norm_g[0]), "g2": gl(norm2_g[0]),
        "gfin": np.ascontiguousarray(np.broadcast_to(f(final_norm_g)[None, :], (128, D))),
        "w_in": f(w_in[0]), "w_uq": f(w_uq[0]), "w_ukv": f(w_ukv[0]), "w_fo": f(w_fourier_out[0]),
        "w_ao": f(w_attn_out[0]), "w_o": f(w_o[0]), "w_g": f(w_ffn_gate[0]), "w_u": f(w_ffn_up[0]),
        "w_d": f(w_ffn_down[0]), "tabA": C["tabA"], "tabAs": C["tabAs"], "ropekA": C["ropekA"], "ropeqA": C["ropeqA"],
        "ropekB": C["ropekB"],
    }
    koA, koB = GEOM["A"]["korder"], GEOM["B"]["korder"]
    qA, qB = C["qA"], C["qB"]
    xkB = []
    for s in range(2):
        full = np.concatenate([meta, x_sample[s]], 0)
        xkB.append(np.ascontiguousarray(full[koB]))
    in_maps = []
    for c in range(8):
        fullA = np.concatenate([meta, x_prompt[c]], 0)
        s, qt = c // 4, c % 4
        m = dict(common)
        m["xkA"] = np.ascontiguousarray(fullA[koA])
        m["xqA"] = np.ascontiguousarray(x_prompt[c][qA - 16])
        m["xkB"] = xkB[s]
        m["xqB"] = np.ascontiguousarray(x_sample[s][qB[qt] - 16])
        m["tabB"] = C["tabB"][qt]
        m["tabBs"] = C["tabBs"][qt]
        if SMALL_TABS:
            m["tabA"] = C["tabA"][0:1, 0:1]
            m["tabB"] = C["tabB"][qt][0:1, 0:1]
        m["ropeqB"] = C["ropeqB"][qt]
        in_maps.append(m)
    if "nc" not in _CACHE:
        _CACHE["nc"] = build_program()
    res = run_bass_kernel_spmd(_CACHE["nc"], in_maps, core_ids=list(range(8)))
    _CACHE["last"] = res
    y_prompt = np.empty((8, 4096, D), np.float32)
    y_sample = np.empty((2, 8192, D), np.float32)
    for c in range(8):
        s, qt = c // 4, c % 4
        y_prompt[c][qA - 16] = np.asarray(res.results[c]["yA"], dtype=np.float32)
        y_sample[s][qB[qt] - 16] = np.asarray(res.results[c]["yB"], dtype=np.float32)
    return (y_prompt, y_sample)
```
